# Optimizing a Trainium2 kernel written in Bass

```python
import jax, jax.numpy as jnp
from jax import lax
import numpy as np


D_MODEL = 1024
BATCH = 8
SEQ = 2048
DEPTH = 2

D_MIX = D_MODEL
ATTN_WIDTH = D_MIX // 2
ATTN_HEAD_DIM = 64
ATTN_HEADS = ATTN_WIDTH // ATTN_HEAD_DIM
DILATED_PATTERNS = ((128, 1), (512, 4), (2048, 16))
ROPE_THETA = 10000.0
MLSTM_WIDTH = D_MIX - ATTN_WIDTH
MLSTM_HEADS = 4
MLSTM_HEAD_DIM = MLSTM_WIDTH // MLSTM_HEADS
MLSTM_CHUNK = 128
MLSTM_CONV = 5
PROJ_COLS = 3 * ATTN_WIDTH + 4 * MLSTM_WIDTH + 4 * MLSTM_HEADS
D_FF = ((8 * D_MODEL // 3 + 255) // 256) * 256
FFN_CONV = 3
NORM_EPS = 1e-6
NEG_INF = -1e30

kernel_name = 'hybrid_dilated_attn_mlstm_convffn'


def rms_norm(x, g):
    xf = x.astype(jnp.float32)
    y = xf * lax.rsqrt(jnp.mean(xf * xf, axis=-1, keepdims=True) + NORM_EPS)
    return (y * g.astype(jnp.float32)).astype(x.dtype)


def depthwise_conv_centred(x, w, b):
    k = w.shape[0]
    pad = k // 2
    s = x.shape[1]
    xp = jnp.pad(x, ((0, 0), (pad, pad), (0, 0)))
    out = xp[:, 0:s] * w[0]
    for j in range(1, k):
        out = out + xp[:, j:j + s] * w[j]
    return out + b


def apply_rotary(t):
    s, dh = t.shape[2], t.shape[3]
    inv_freq = ROPE_THETA ** (-jnp.arange(0, dh, 2, dtype=jnp.float32) / dh)
    ang = jnp.arange(s, dtype=jnp.float32)[:, None] * inv_freq[None, :]
    cos, sin = jnp.cos(ang), jnp.sin(ang)
    t1, t2 = t[..., :dh // 2], t[..., dh // 2:]
    return jnp.concatenate([t1 * cos - t2 * sin, t2 * cos + t1 * sin], axis=-1)


def dilated_window_branch(q, k, v, window, dilation):
    b, h, s, dh = q.shape
    half = window // (2 * dilation)
    blk = half
    n_sub = s // dilation
    nb = -(-n_sub // blk)
    lp = nb * blk

    def to_sub(t):
        return jnp.swapaxes(t.reshape(b, h, n_sub, dilation, dh), 2, 3)

    qs = jnp.pad(to_sub(q), ((0, 0), (0, 0), (0, 0), (0, lp - n_sub), (0, 0)))
    kpad = ((0, 0), (0, 0), (0, 0), (blk, lp - n_sub + blk), (0, 0))
    ks = jnp.pad(to_sub(k), kpad)
    vs = jnp.pad(to_sub(v), kpad)

    def band(t):
        return jnp.concatenate(
            [t[:, :, :, i * blk:i * blk + lp].reshape(b, h, dilation, nb, blk, dh) for i in range(3)], axis=4)

    qb = qs.reshape(b, h, dilation, nb, blk, dh)
    kb, vb = band(ks), band(vs)
    scores = jnp.einsum('bhrnqe,bhrnke->bhrnqk', qb, kb) * (dh ** -0.5)
    qi = jnp.arange(nb)[:, None] * blk + jnp.arange(blk)[None, :]
    ki = jnp.arange(nb)[:, None] * blk - blk + jnp.arange(3 * blk)[None, :]
    off = ki[:, None, :] - qi[:, :, None]
    valid = (jnp.abs(off) <= half) & (ki[:, None, :] >= 0) & (ki[:, None, :] < n_sub)
    scores = jnp.where(valid, scores, NEG_INF)
    m = jnp.max(scores, axis=-1, keepdims=True)
    p = jnp.exp(scores - m)
    den = jnp.sum(p, axis=-1)
    out = jnp.einsum('bhrnqk,bhrnke->bhrnqe', p, vb) / den[..., None]
    lse = m[..., 0] + jnp.log(den)
    out = jnp.swapaxes(out.reshape(b, h, dilation, lp, dh)[:, :, :, :n_sub], 2, 3).reshape(b, h, s, dh)
    lse = jnp.swapaxes(lse.reshape(b, h, dilation, lp)[:, :, :, :n_sub], 2, 3).reshape(b, h, s)
    return out, lse


def dilated_attention(q, k, v):
    outs, lses = [], []
    for window, dilation in DILATED_PATTERNS:
        o, l = dilated_window_branch(q, k, v, window, dilation)
        outs.append(o)
        lses.append(l)
    wts = jax.nn.softmax(jnp.stack(lses, axis=0), axis=0)
    out = wts[0][..., None] * outs[0]
    for i in range(1, len(outs)):
        out = out + wts[i][..., None] * outs[i]
    return out


def mlstm_direction(q, k, v, log_i, log_f):
    b, h, s, dh = q.shape
    ch = MLSTM_CHUNK
    nc = s // ch
    k = k * (dh ** -0.5)

    def chunks(t):
        return jnp.moveaxis(t.reshape(b, h, nc, ch, *t.shape[3:]), 2, 0)

    causal = jnp.tril(jnp.ones((ch, ch), dtype=bool))

    def step(carry, xs):
        c_st, n_st, m_st = carry
        qc, kc, vc, lic, lfc = xs
        bcum = jnp.cumsum(lfc, axis=-1)
        dlog = bcum[..., :, None] - bcum[..., None, :] + lic[..., None, :]
        dlog = jnp.where(causal, dlog, NEG_INF)
        inter = bcum + m_st[..., None]
        m_t = jnp.maximum(inter, jnp.max(dlog, axis=-1))
        sc = jnp.einsum('bhte,bhse->bhts', qc, kc) * jnp.exp(dlog - m_t[..., None])
        g = jnp.exp(inter - m_t)
        num = jnp.einsum('bhts,bhsf->bhtf', sc, vc) + g[..., None] * jnp.einsum('bhte,bhef->bhtf', qc, c_st)
        den = jnp.sum(sc, axis=-1) + g * jnp.einsum('bhte,bhe->bht', qc, n_st)
        h_t = num / jnp.maximum(jnp.abs(den), jnp.exp(-m_t))[..., None]
        b_last = bcum[..., -1]
        wlog = b_last[..., None] - bcum + lic
        m_new = jnp.maximum(b_last + m_st, jnp.max(wlog, axis=-1))
        wexp = jnp.exp(wlog - m_new[..., None])
        decay = jnp.exp(b_last + m_st - m_new)
        c_new = decay[..., None, None] * c_st + jnp.einsum('bhs,bhse,bhsf->bhef', wexp, kc, vc)
        n_new = decay[..., None] * n_st + jnp.einsum('bhs,bhse->bhe', wexp, kc)
        return (c_new, n_new, m_new), h_t

    init = (jnp.zeros((b, h, dh, dh), jnp.float32), jnp.zeros((b, h, dh), jnp.float32),
            jnp.zeros((b, h), jnp.float32))
    _, hs = lax.scan(step, init, (chunks(q), chunks(k), chunks(v), chunks(log_i), chunks(log_f)))
    return jnp.moveaxis(hs, 0, 2).reshape(b, h, s, dh)


def hybrid_mixer(hn, w_in, mlstm_conv_w, mlstm_conv_b, mlstm_gate_b, mlstm_head_g, w_out):
    b, s, _ = hn.shape
    f32 = jnp.float32
    sizes = [ATTN_WIDTH] * 3 + [MLSTM_WIDTH] * 4 + [4 * MLSTM_HEADS]
    split_at = np.cumsum(sizes)[:-1].tolist()
    aq, ak, av, mq, mk, mv, mo, gates = jnp.split(hn @ w_in, split_at, axis=-1)

    def heads(t, n):
        return jnp.transpose(t.astype(f32).reshape(b, s, n, -1), (0, 2, 1, 3))

    q = apply_rotary(heads(aq, ATTN_HEADS))
    k = apply_rotary(heads(ak, ATTN_HEADS))
    v = heads(av, ATTN_HEADS)
    attn = dilated_attention(q, k, v)
    attn = jnp.transpose(attn, (0, 2, 1, 3)).reshape(b, s, ATTN_WIDTH).astype(hn.dtype)

    qk = jax.nn.silu(depthwise_conv_centred(jnp.concatenate([mq, mk], axis=-1), mlstm_conv_w, mlstm_conv_b))
    mq, mk = jnp.split(qk, 2, axis=-1)
    q = heads(mq, MLSTM_HEADS)
    k = heads(mk, MLSTM_HEADS)
    v = heads(mv, MLSTM_HEADS)
    g = (gates + mlstm_gate_b).astype(f32).reshape(b, s, 4, MLSTM_HEADS)
    g = jnp.transpose(g, (2, 0, 3, 1))
    h_fwd = mlstm_direction(q, k, v, g[0], jax.nn.log_sigmoid(g[1]))
    flip = lambda t: jnp.flip(t, axis=2)
    h_bwd = flip(mlstm_direction(flip(q), flip(k), flip(v), flip(g[2]), flip(jax.nn.log_sigmoid(g[3]))))
    hm = h_fwd + h_bwd
    hm = hm * lax.rsqrt(jnp.mean(hm * hm, axis=-1, keepdims=True) + NORM_EPS)
    hm = jnp.transpose(hm, (0, 2, 1, 3)).reshape(b, s, MLSTM_WIDTH)
    mlstm = (hm * mlstm_head_g.astype(f32) * jax.nn.sigmoid(mo.astype(f32))).astype(hn.dtype)

    return jnp.concatenate([attn, mlstm], axis=-1) @ w_out


def conv_ffn(hn, w_up, conv_w, conv_b, w_down):
    u = depthwise_conv_centred(hn @ w_up, conv_w, conv_b)
    gate, val = jnp.split(u, 2, axis=-1)
    return (jax.nn.gelu(gate, approximate=True) * val) @ w_down


def setup_inputs(seed: int = 0) -> dict:
    key = jax.random.key(seed)
    ks = jax.random.split(key, 16)

    def nrm(k, shape, scale):
        return jax.random.normal(k, shape, jnp.float32) * scale

    base = jnp.stack([jnp.zeros((MLSTM_HEADS,), jnp.float32),
                      jnp.linspace(3.0, 6.0, MLSTM_HEADS, dtype=jnp.float32)], axis=0)
    gate_b = (nrm(ks[5], (DEPTH, 2, 2, MLSTM_HEADS), 0.1) + base[None, None]).reshape(DEPTH, 4 * MLSTM_HEADS)
    return {
        'x': nrm(ks[0], (BATCH, SEQ, D_MODEL), 1.0),
        'mix_pre_g': 1.0 + nrm(ks[1], (DEPTH, D_MODEL), 0.02),
        'w_in': nrm(ks[2], (DEPTH, D_MODEL, PROJ_COLS), D_MODEL ** -0.5),
        'mlstm_conv_w': nrm(ks[3], (DEPTH, MLSTM_CONV, 2 * MLSTM_WIDTH), MLSTM_CONV ** -0.5),
        'mlstm_conv_b': nrm(ks[4], (DEPTH, 2 * MLSTM_WIDTH), 0.02),
        'mlstm_gate_b': gate_b,
        'mlstm_head_g': 1.0 + nrm(ks[6], (DEPTH, MLSTM_WIDTH), 0.02),
        'w_out': nrm(ks[7], (DEPTH, D_MIX, D_MODEL), D_MIX ** -0.5),
        'mix_post_g': 1.0 + nrm(ks[8], (DEPTH, D_MODEL), 0.02),
        'ffn_pre_g': 1.0 + nrm(ks[9], (DEPTH, D_MODEL), 0.02),
        'w_up': nrm(ks[10], (DEPTH, D_MODEL, 2 * D_FF), D_MODEL ** -0.5),
        'ffn_conv_w': nrm(ks[11], (DEPTH, FFN_CONV, 2 * D_FF), FFN_CONV ** -0.5),
        'ffn_conv_b': nrm(ks[12], (DEPTH, 2 * D_FF), 0.02),
        'w_down': nrm(ks[13], (DEPTH, D_FF, D_MODEL), D_FF ** -0.5),
        'ffn_post_g': 1.0 + nrm(ks[14], (DEPTH, D_MODEL), 0.02),
    }


def reference(x, mix_pre_g, w_in, mlstm_conv_w, mlstm_conv_b, mlstm_gate_b, mlstm_head_g, w_out,
              mix_post_g, ffn_pre_g, w_up, ffn_conv_w, ffn_conv_b, w_down, ffn_post_g):
    for l in range(DEPTH):
        mixed = hybrid_mixer(rms_norm(x, mix_pre_g[l]), w_in[l], mlstm_conv_w[l], mlstm_conv_b[l],
                             mlstm_gate_b[l], mlstm_head_g[l], w_out[l])
        x = x + rms_norm(mixed, mix_post_g[l])
        ff = conv_ffn(rms_norm(x, ffn_pre_g[l]), w_up[l], ffn_conv_w[l], ffn_conv_b[l], w_down[l])
        x = x + rms_norm(ff, ffn_post_g[l])
    return x
```

```python
import math
from contextlib import ExitStack
import numpy as np
import concourse.bass as bass
import concourse.mybir as mybir
from concourse.bass_utils import run_bass_kernel_spmd

F32 = mybir.dt.float32
BF16 = mybir.dt.bfloat16
AF = mybir.ActivationFunctionType
ALU = mybir.AluOpType

S = 2048
D = 1024
KT = 8
NCH = 4
CH = 512
DFF = 2816
NFT = 22
EPS = 1e-6
NDS = 24


def ssl(start, n, step):
    return slice(start, start + step * (n - 1) + 1, step)


class KB:
    def __init__(self, nc):
        self.nc = nc
        self.E = {'pe': nc.tensor, 'act': nc.scalar, 'dve': nc.vector, 'pool': nc.gpsimd, 'sp': nc.sync}
        self.sem = {e: nc.alloc_semaphore('c_' + e) for e in ('pe', 'act', 'dve', 'pool')}
        self.cnt = {e: 0 for e in self.sem}
        self.waited = {e: {} for e in self.E}
        self.dsems = [nc.alloc_semaphore('d%d' % i) for i in range(NDS)]
        self.dcnt = [0] * NDS
        self.dlast = [None] * NDS
        self.dnext = 0
        self.dnext_p = 0
        self.tk = {}
        self.dead = False
        self.prog = {e: [] for e in self.E}

    def _t(self, key):
        t = self.tk.get(key)
        if t is None:
            t = {'w': None, 'r': {}}
            self.tk[key] = t
        return t

    def _wait(self, e, ev):
        sem, val, sid = ev
        if self.waited[e].get(sid, 0) >= val:
            return
        self.E[e].wait_ge(sem, val)
        self.prog[e].append(('w', sid, val))
        self.waited[e][sid] = val

    def _deps(self, e, r, w, loose=False):
        evs = {}

        def add(ev):
            if ev is None:
                return
            if ev[2] not in evs or evs[ev[2]][1] < ev[1]:
                evs[ev[2]] = ev
        for k in r:
            add(self._t(k)['w'])
        for k in w:
            t = self._t(k)
            add(t['w'])
            for ev in t['r'].values():
                add(ev)
        for sid, ev in evs.items():
            if sid == e and (e == 'pe' or loose):
                continue
            self._wait(e, ev)

    def _record(self, ev, r, w):
        for k in r:
            self._t(k)['r'][ev[2]] = ev
        for k in w:
            t = self._t(k)
            t['w'] = ev
            t['r'] = {}

    def op(self, e, fn, r=(), w=(), loose=False, inc=True):
        if self.dead:
            return
        self._deps(e, r, w, loose)
        ins = fn(self.E[e])
        if inc:
            self.cnt[e] += 1
            ins.then_inc(self.sem[e], 1)
            self.prog[e].append(('i', e, 1))
            self._record((self.sem[e], self.cnt[e], e), r, w)
        else:
            self._record((self.sem[e], self.cnt[e] + 1, e), r, w)

    def dma(self, q, out, in_, r=(), w=()):
        if self.dead:
            return
        if q == 'pool':
            i = 16 + self.dnext_p
            self.dnext_p = (self.dnext_p + 1) % (NDS - 16)
        else:
            i = self.dnext
            self.dnext = (i + 1) % 16
        if self.dlast[i] is not None:
            self._wait(q, self.dlast[i])
        self._deps(q, r, w)
        ins = self.E[q].dma_start(out=out, in_=in_)
        self.dcnt[i] += 16
        ins.then_inc(self.dsems[i], 16)
        self.prog[q].append(('i', 'd%d' % i, 16))
        ev = (self.dsems[i], self.dcnt[i], 'd%d' % i)
        self.dlast[i] = ev
        self._record(ev, r, w)
        return ev

    def check_deadlock(self):
        pc = {e: 0 for e in self.prog}
        val = {}
        progress = True
        while progress:
            progress = False
            for e, p in self.prog.items():
                while pc[e] < len(p):
                    k, sid, v = p[pc[e]]
                    if k == 'w':
                        if val.get(sid, 0) < v:
                            break
                    else:
                        val[sid] = val.get(sid, 0) + v
                    pc[e] += 1
                    progress = True
        stuck = {e: (pc[e], len(p), p[pc[e]]) for e, p in self.prog.items() if pc[e] < len(p)}
        return stuck

    def barrier(self):
        if self.dead:
            return
        evs = [(self.sem[e], self.cnt[e], e) for e in self.sem if self.cnt[e] > 0]
        evs += [ev for ev in self.dlast if ev is not None]
        for e in self.E:
            for ev in evs:
                if ev[2] == e:
                    continue
                self._wait(e, ev)


class _Stop(Exception):
    pass


def build(nlayers=2, dbg=None, stop_at=None):
    dbg = dbg or {}
    nc = bass.Bass("TRN2", target_bir_lowering=False)
    kb = KB(nc)
    ein = lambda n, s, dt=F32: nc.dram_tensor(n, list(s), dt, kind="ExternalInput").ap()
    xT_in = ein("xT", [D, S])
    w_in = ein("w_in", [2, D, 3600])
    w_out = ein("w_out", [2, D, D])
    w_up = ein("w_up", [2, D, 2 * DFF])
    w_down = ein("w_down", [2, DFF, D])
    gvec_d = ein("gvec", [128, 2 * 4 * KT])
    mconv_d = ein("mconv", [128, 2 * 8 * 6])
    fconv_d = ein("fconv", [128, 2 * 44 * 4])
    gateb_d = ein("gateb", [128, 2 * 16])
    headg_d = ein("headg", [128, 2 * 512])
    cos_d = ein("cosT", [128, S])
    sin_d = ein("sinT", [128, S])
    mstrip_d = ein("mstrip", [128, 4 * 512])
    cst_d = ein("cst", [128, 9 * 128])
    yT = nc.dram_tensor("yT", [D, S], F32, kind="ExternalOutput").ap()
    xs = [nc.dram_tensor("xs%d" % i, [D, S], F32).ap() for i in range(2)]
    dbg_out = {}
    for name, shape in dbg.items():
        dbg_out[name] = nc.dram_tensor("dbg_" + name, list(shape), F32, kind="ExternalOutput").ap()

    pm = lambda ap: ap.rearrange("(k p) t -> p k t", p=128)

    with ExitStack() as top:
        uid = [0]

        def sb(st, name, shape, dt):
            uid[0] += 1
            return st.enter_context(nc.sbuf_tensor("s%d_%s" % (uid[0], name), list(shape), dt))
        gvec = sb(top, "gvec", [128, 2, 4, KT], F32)
        mconv = sb(top, "mconv", [128, 2, 8, 6], F32)
        fconv = sb(top, "fconv", [128, 2, 44, 4], F32)
        gateb = sb(top, "gateb", [128, 2, 16], F32)
        headg = sb(top, "headg", [128, 2, 512], BF16)
        cstb = sb(top, "cstb", [128, 9, 128], BF16)
        cstf = sb(top, "cstf", [128, 3, 128], F32)
        epsb = sb(top, "epsb", [128, 1], F32)
        mstrip = sb(top, "mstrip", [128, 4, 512], BF16)
        psall = top.enter_context(nc.psum_tensor("psall", [128, 7, 512], F32))
        PS = [psall[:, i, :] for i in range(7)]
        PST = top.enter_context(nc.psum_tensor("pst", [128, 1024], BF16))
        for i in range(7):
            kb._t(('ps', i))
        kb.dma('sp', gvec[:].rearrange("p a b c -> p (a b c)"), gvec_d, w=['gvec'])
        kb.dma('sp', mconv[:].rearrange("p a b c -> p (a b c)"), mconv_d, w=['mconv'])
        kb.dma('sp', fconv[:].rearrange("p a b c -> p (a b c)"), fconv_d, w=['fconv'])
        kb.dma('sp', gateb[:].rearrange("p a b -> p (a b)"), gateb_d, w=['gateb'])
        kb.dma('pool', headg[:].rearrange("p a b -> p (a b)"), headg_d, w=['headg'])
        kb.dma('pool', cstb[:].rearrange("p a b -> p (a b)"), cst_d, w=['cstb'])
        kb.dma('pool', mstrip[:].rearrange("p a b -> p (a b)"), mstrip_d, w=['mstrip'])
        cview = cst_d.rearrange("p (a b) -> p a b", b=128)
        kb.dma('sp', cstf[:, 0, :], cview[:, 0, :], w=['cstf'])
        kb.dma('sp', cstf[:, 1:3, :], cview[:, 6:8, :], w=['cstf'])
        kb.op('dve', lambda e: e.memset(epsb[:], EPS), w=['epsb'])
        ones_bf = cstb[:, 0, :]
        ident_bf = cstb[:, 1, :]
        MASK = {'A': cstb[:, 2, :], 'B': cstb[:, 3, :], 'F': cstb[:, 4, :], 'E': cstb[:, 5, :]}
        tri_bf = [cstb[:, 6, :], cstb[:, 7, :]]
        ones_f = cstf[:, 0, :]
        tri_f = [cstf[:, 1, :], cstf[:, 2, :]]

        def chk(name):
            if stop_at == name:
                kb.barrier()
                kb.dead = True

        def dump(name, sb_ap, rkeys):
            if name in dbg_out:
                kb.dma('sp', dbg_out[name], sb_ap, r=rkeys, w=['dbg_' + name])

        def rstd_from_ps(ps_i, rstd_ap, rkey, n):
            kb.op('act', lambda e: e.activation(out=rstd_ap, in_=PS[ps_i][:, 0:rstd_ap.shape[1]], func=AF.Sqrt,
                                                scale=1.0 / n, bias=epsb[:, 0:1]),
                  r=[('ps', ps_i), 'epsb'], w=[rkey])
            kb.op('dve', lambda e: e.reciprocal(out=rstd_ap, in_=rstd_ap), r=[rkey], w=[rkey])

        def norm_pre(st, xsrc, l, j, hnT):
            xcs = [sb(st, "np_xc%d" % i, [128, KT, CH], F32) for i in range(2)]
            sq = sb(st, "np_sq", [128, KT, CH], BF16)
            rstd = sb(st, "np_rstd", [128, CH], F32)
            for c in range(NCH):
                xc = xcs[c % 2]
                xk = ('np_xc', c % 2)
                kb.dma('sp', xc[:], pm(xsrc)[:, :, c * CH:(c + 1) * CH], w=[xk])
                kb.op('act', lambda e: e.activation(out=sq[:], in_=xc[:], func=AF.Square), r=[xk], w=['np_sq'])
                for k in range(KT):
                    kb.op('pe', lambda e: e.matmul(PS[6][:], lhsT=ones_bf, rhs=sq[:, k, :], start=(k == 0), stop=(k == KT - 1)),
                          r=['np_sq', 'cstb'], w=[('ps', 6)])
                rstd_from_ps(6, rstd[:], 'np_rstd', D)
                for k in range(KT):
                    kb.op('dve', lambda e: e.scalar_tensor_tensor(out=hnT[:, k, c * CH:(c + 1) * CH], in0=xc[:, k, :],
                                                                  scalar=gvec[:, l, j, k:k + 1], in1=rstd[:],
                                                                  op0=ALU.mult, op1=ALU.mult),
                          r=[xk, 'np_rstd', 'gvec'], w=[('hnT', c)])

        def load_w(dst, src_rows_cols, wkey):
            kb.dma('pool', dst, src_rows_cols.rearrange("(k p) c -> p k c", p=128), w=[wkey])

        def epilogue(st, Wd, nct, rhs_fn, rkeys_fn, xsrc, xdst, l, j):
            W = sb(st, "ep_w", [128, nct, D], BF16)
            xcs = [sb(st, "ep_xc%d" % i, [128, KT, CH], F32) for i in range(2)]
            ff = sb(st, "ep_ff", [128, KT, CH], F32)
            sq = sb(st, "ep_sq", [128, KT, CH], BF16)
            rstd = sb(st, "ep_rstd", [128, CH], F32)
            kstep = 8
            for m in range(KT):
                for k0 in range(0, nct, kstep):
                    k1 = min(nct, k0 + kstep)
                    load_w(W[:, k0:k1, m * 128:(m + 1) * 128], Wd[k0 * 128:k1 * 128, m * 128:(m + 1) * 128], ('ep_w', m))
            pi = 0
            for c in range(NCH):
                xc = xcs[c % 2]
                xk = ('ep_xc', c % 2)
                kb.dma('sp', xc[:], pm(xsrc)[:, :, c * CH:(c + 1) * CH], w=[xk])
                for m in range(KT):
                    b = pi % 4
                    pi += 1
                    for ci in range(nct):
                        kb.op('pe', lambda e: e.matmul(PS[b][:], lhsT=W[:, ci, m * 128:(m + 1) * 128], rhs=rhs_fn(ci, c),
                                                       start=(ci == 0), stop=(ci == nct - 1)),
                              r=[('ep_w', m)] + rkeys_fn(ci, c), w=[('ps', b)])
                    kb.op('act', lambda e: e.activation(out=ff[:, m, :], in_=PS[b][:], func=AF.Copy), r=[('ps', b)], w=[('ep_ff', m)])
                    kb.op('act', lambda e: e.activation(out=sq[:, m, :], in_=ff[:, m, :], func=AF.Square), r=[('ep_ff', m)], w=[('ep_sq', m)])
                for m in range(KT):
                    kb.op('pe', lambda e: e.matmul(PS[6][:], lhsT=ones_bf, rhs=sq[:, m, :], start=(m == 0), stop=(m == KT - 1)),
                          r=[('ep_sq', m), 'cstb'], w=[('ps', 6)])
                rstd_from_ps(6, rstd[:], 'ep_rstd', D)
                for m in range(KT):
                    kb.op('dve', lambda e: e.scalar_tensor_tensor(out=ff[:, m, :], in0=ff[:, m, :], scalar=gvec[:, l, j, m:m + 1],
                                                                  in1=rstd[:], op0=ALU.mult, op1=ALU.mult),
                          r=['ep_rstd', 'gvec'], w=[('ep_ff', m)])
                    kb.op('dve', lambda e: e.tensor_tensor(out=xc[:, m, :], in0=xc[:, m, :], in1=ff[:, m, :], op=ALU.add),
                          r=[('ep_ff', m)], w=[xk])
                kb.dma('sp', pm(xdst)[:, :, c * CH:(c + 1) * CH], xc[:], r=[xk], w=[('xdst', c)])

        def mlstm_phase(st0, l, hnT, catT):
            st = st0.enter_context(ExitStack())
            qkT = sb(st, "m_qkT", [128, 8, S], BF16)
            kTM = sb(st, "m_kTM", [128, 16, 512], BF16)
            vaug = sb(st, "m_vaug", [128, 16, 4, 132], BF16)
            gsig = sb(st, "m_gsig", [128, 16, 512], BF16)
            gates = sb(st, "m_gates", [128, 16, 16], F32)
            l1 = sb(st, "m_l1", [128, 16, 8], F32)
            gd = sb(st, "m_gd", [128, 5, 2, 16, 4], F32)
            with ExitStack() as sp_:
                Wa = sb(sp_, "m_wa", [128, KT, 512], BF16)
                Wb = sb(sp_, "m_wb", [128, KT, 512], BF16)
                Wg = sb(sp_, "m_wg", [128, KT, 16], BF16)
                ypads = [sb(sp_, "m_ypad%d" % i, [128, S + 4], BF16) for i in range(2)]
                uaccs = [sb(sp_, "m_uacc%d" % i, [128, S], F32) for i in range(2)]
                sgt = sb(sp_, "m_sgt", [128, 512], BF16)
                load_w(Wa[:], w_in[l, :, 1536:2048], 'm_wa')
                load_w(Wb[:], w_in[l, :, 2048:2560], 'm_wb')
                load_w(Wg[:], w_in[l, :, 3584:3600], 'm_wg')
                for i in range(2):
                    kb.op('dve', lambda e: e.memset(ypads[i][:, 0:2], 0.0), w=[('m_ypad', i)])
                    kb.op('dve', lambda e: e.memset(ypads[i][:, S + 2:S + 4], 0.0), w=[('m_ypad', i)])
                kb.op('dve', lambda e: e.memset(vaug[:, :, :, 128:129], 1.0), w=['m_vaug1'])
                pi = 0
                for i in range(8):
                    W = Wa if i < 4 else Wb
                    wk = 'm_wa' if i < 4 else 'm_wb'
                    cs = (i % 4) * 128
                    ypad, uacc = ypads[i % 2], uaccs[i % 2]
                    yk, uk = ('m_ypad', i % 2), ('m_uacc', i % 2)
                    for c in range(NCH):
                        b = pi % 4
                        pi += 1
                        for k in range(KT):
                            kb.op('pe', lambda e: e.matmul(PS[b][:], lhsT=W[:, k, cs:cs + 128], rhs=hnT[:, k, c * CH:(c + 1) * CH],
                                                           start=(k == 0), stop=(k == KT - 1)),
                                  r=[wk, ('hnT', c)], w=[('ps', b)])
                        kb.op('act', lambda e: e.activation(out=ypad[:, 2 + c * CH:2 + (c + 1) * CH], in_=PS[b][:], func=AF.Copy),
                              r=[('ps', b)], w=[yk])
                    kb.op('act', lambda e: e.activation(out=uacc[:], in_=ypad[:, 0:S], func=AF.Identity,
                                                        scale=mconv[:, l, i, 0:1], bias=mconv[:, l, i, 5:6]),
                          r=[yk, 'mconv'], w=[uk])
                    for jj in range(1, 5):
                        kb.op('dve', lambda e: e.scalar_tensor_tensor(out=uacc[:], in0=ypad[:, jj:jj + S], scalar=mconv[:, l, i, jj:jj + 1],
                                                                      in1=uacc[:], op0=ALU.mult, op1=ALU.add),
                              r=[yk, 'mconv'], w=[uk])
                    kb.op('act', lambda e: e.activation(out=qkT[:, i, :], in_=uacc[:], func=AF.Silu), r=[uk], w=[('m_qkT', i)])
                if 'qk' in dbg_out:
                    for i in range(8):
                        kb.op('act', lambda e: e.activation(out=uaccs[0][:], in_=qkT[:, i, :], func=AF.Copy), r=[('m_qkT', i)], w=[('m_uacc', 0)])
                        dump_ap = dbg_out['qk'][i * 128:(i + 1) * 128, :]
                        kb.dma('sp', dump_ap, uaccs[0][:], r=[('m_uacc', 0)], w=['dbg_qk'])
                for c in range(16):
                    for h in range(4):
                        kb.op('pe', lambda e: e.transpose(PST[:, h * 128:(h + 1) * 128], qkT[:, 4 + h, c * 128:(c + 1) * 128], ident_bf),
                              r=[('m_qkT', 4 + h), 'cstb'], w=['pst'])
                    kb.op('act', lambda e: e.activation(out=kTM[:, c, :], in_=PST[:, 0:512], func=AF.Copy), r=['pst'], w=[('m_kTM', c)])
                load_w(Wa[:], w_in[l, :, 2560:3072], 'm_wa')
                load_w(Wb[:], w_in[l, :, 3072:3584], 'm_wb')
                for c in range(16):
                    tk = ('hnT', c // 4)
                    for k in range(KT):
                        kb.op('pe', lambda e: e.matmul(PS[0][:], lhsT=hnT[:, k, c * 128:(c + 1) * 128], rhs=Wa[:, k, :],
                                                       start=(k == 0), stop=(k == KT - 1)), r=['m_wa', tk], w=[('ps', 0)])
                    kb.op('act', lambda e: e.activation(out=vaug[:, c, :, 0:128], in_=PS[0][:].rearrange("p (h f) -> p h f", f=128), func=AF.Copy),
                          r=[('ps', 0)], w=[('m_vaug', c)])
                    for k in range(KT):
                        kb.op('pe', lambda e: e.matmul(PS[1][:], lhsT=hnT[:, k, c * 128:(c + 1) * 128], rhs=Wb[:, k, :],
                                                       start=(k == 0), stop=(k == KT - 1)), r=['m_wb', tk], w=[('ps', 1)])
                    kb.op('act', lambda e: e.activation(out=sgt[:], in_=PS[1][:], func=AF.Sigmoid), r=[('ps', 1)], w=['m_sgt'])
                    kb.op('dve', lambda e: e.tensor_tensor(out=gsig[:, c, :], in0=sgt[:], in1=headg[:, l, :], op=ALU.mult),
                          r=['m_sgt', 'headg'], w=[('m_gsig', c)])
                    for k in range(KT):
                        kb.op('pe', lambda e: e.matmul(PS[2][:, 0:16], lhsT=hnT[:, k, c * 128:(c + 1) * 128], rhs=Wg[:, k, :],
                                                       start=(k == 0), stop=(k == KT - 1)), r=['m_wg', tk], w=[('ps', 2)])
                    kb.op('dve', lambda e: e.tensor_tensor(out=gates[:, c, :], in0=PS[2][:, 0:16], in1=gateb[:, l, :], op=ALU.add),
                          r=[('ps', 2), 'gateb'], w=['m_gates'])
                for d_ in range(2):
                    kb.op('act', lambda e: e.activation(out=l1[:, :, 4 * d_:4 * d_ + 4], in_=gates[:, :, 8 * d_ + 4:8 * d_ + 8], func=AF.Exp, scale=-1.0),
                          r=['m_gates'], w=['m_l1'])
                kb.op('act', lambda e: e.activation(out=l1[:], in_=l1[:], func=AF.Ln, bias=1.0), r=['m_l1'], w=['m_l1'])
                l1h = sb(sp_, "m_l1h", [128, 2, 16, 8], BF16)
                l1r = sb(sp_, "m_l1r", [128, 16, 8], F32)
                kb.op('dve', lambda e: e.tensor_copy(out=l1h[:, 0], in_=l1[:]), r=['m_l1'], w=['m_l1h'])
                kb.op('dve', lambda e: e.tensor_copy(out=l1r[:], in_=l1h[:, 0]), r=['m_l1h'], w=['m_l1r'])
                kb.op('dve', lambda e: e.tensor_tensor(out=l1r[:], in0=l1[:], in1=l1r[:], op=ALU.subtract), r=['m_l1', 'm_l1r'], w=['m_l1r'])
                kb.op('dve', lambda e: e.tensor_copy(out=l1h[:, 1], in_=l1r[:]), r=['m_l1r'], w=['m_l1h'])
                for d_ in range(2):
                    for (bank, lhs) in ((3, tri_bf[d_]), (4, ones_bf)):
                        for hl in range(2):
                            kb.op('pe', lambda e: e.matmul(PS[bank][:, 64 * d_:64 * d_ + 64].rearrange("p (c h) -> p c h", h=4), lhsT=lhs,
                                                           rhs=l1h[:, hl, :, 4 * d_:4 * d_ + 4], start=(hl == 0), stop=(hl == 1)),
                                  r=['m_l1h', 'cstb'], w=[('ps', bank)])
                    na = gd[:, 0, d_]
                    kb.op('act', lambda e: e.activation(out=na, in_=PS[3][:, 64 * d_:64 * d_ + 64].rearrange("p (c h) -> p c h", h=4), func=AF.Copy),
                          r=[('ps', 3)], w=['m_gd'])
                    kb.op('dve', lambda e: e.tensor_tensor(out=gd[:, 1, d_], in0=gates[:, :, 8 * d_:8 * d_ + 4], in1=na, op=ALU.add),
                          r=['m_gates', 'm_gd'], w=['m_gd'])
                    kb.op('act', lambda e: e.activation(out=gd[:, 1, d_], in_=gd[:, 1, d_], func=AF.Exp, bias=-0.5 * math.log(128.0)),
                          r=['m_gd'], w=['m_gd'])
                    kb.op('act', lambda e: e.activation(out=gd[:, 2, d_], in_=na, func=AF.Exp), r=['m_gd'], w=['m_gd'])
                    kb.op('act', lambda e: e.activation(out=gd[:, 3, d_], in_=PS[4][:, 64 * d_:64 * d_ + 64].rearrange("p (c h) -> p c h", h=4),
                                                        func=AF.Exp, scale=-1.0), r=[('ps', 4)], w=['m_gd'])
                    kb.op('dve', lambda e: e.tensor_tensor(out=gd[:, 4, d_], in0=gd[:, 1, d_], in1=gd[:, 3, d_], op=ALU.mult),
                          r=['m_gd'], w=['m_gd'])
            kb.barrier()
            chk('mproj')
            with ExitStack() as ss_:
                hm = sb(ss_, "m_hm", [128, 16, 512], F32)
                Cst = sb(ss_, "m_C", [128, 8, 132], F32)
                Cbf = sb(ss_, "m_Cbf", [128, 8, 132], BF16)
                PT = [sb(ss_, "m_PT%d" % i, [128, 128], BF16) for i in range(4)]
                Kt = [sb(ss_, "m_Kt%d" % i, [128, 128], BF16) for i in range(4)]
                sm = sb(ss_, "m_sm", [128, 8, 4], F32)
                ssh = sb(ss_, "m_ssh", [128, 16, 4], F32)
                junk = sb(ss_, "m_junk", [128, 128], BF16)
                mot = sb(ss_, "m_mot", [128, 512], BF16)
                kb.op('dve', lambda e: e.memset(Cst[:], 0.0), w=[('m_C', i) for i in range(8)])
                kb.op('dve', lambda e: e.memset(Cbf[:], 0.0), w=[('m_Cbf', i) for i in range(8)])
                written = set()

                def scanA(w_):
                    (it, step, h, d_, c) = w_
                    hd = h * 2 + d_
                    tok = slice(c * 128, (c + 1) * 128)
                    bs, bn, bu = it % 2, 2 + it % 2, 4 + it % 2
                    pt, ktl = PT[it % 4], Kt[it % 4]
                    ptk, ktk = ('m_PT', it % 4), ('m_Kt', it % 4)
                    kb.op('pe', lambda e: e.matmul(PS[bs][:, 0:128], lhsT=qkT[:, 4 + h, tok], rhs=qkT[:, h, tok], start=True, stop=True),
                          r=[('m_qkT', h), ('m_qkT', 4 + h)], w=[('ps', bs)])
                    kb.op('dve', lambda e: e.scalar_tensor_tensor(out=pt[:], in0=PS[bs][:, 0:128], scalar=gd[:, 1, d_, c, h:h + 1],
                                                                  in1=tri_bf[d_], op0=ALU.mult, op1=ALU.mult),
                          r=[('ps', bs), 'm_gd', 'cstb'], w=[ptk])
                    kb.op('act', lambda e: e.activation(out=ktl[:], in_=kTM[:, c, h * 128:(h + 1) * 128], func=AF.Copy, scale=gd[:, 4, d_, c, h:h + 1]),
                          r=[('m_kTM', c), 'm_gd'], w=[ktk])
                    kb.op('pe', lambda e: e.matmul(PS[bn][:, 0:129], lhsT=pt[:], rhs=vaug[:, c, h, 0:129], start=True, stop=False),
                          r=[ptk, ('m_vaug', c), 'm_vaug1'], w=[('ps', bn)])
                    kb.op('pe', lambda e: e.matmul(PS[bn][:, 0:129], lhsT=qkT[:, h, tok], rhs=Cbf[:, hd, 0:129], start=False, stop=True),
                          r=[('m_qkT', h), ('m_Cbf', hd)], w=[('ps', bn)])
                    kb.op('pe', lambda e: e.matmul(PS[bu][:, 0:129], lhsT=ktl[:], rhs=vaug[:, c, h, 0:129], start=True, stop=True),
                          r=[ktk, ('m_vaug', c), 'm_vaug1'], w=[('ps', bu)])
                    smk = ('m_sm', hd)
                    kb.op('act', lambda e: e.activation(out=sm[:, hd, 0:1], in_=PS[bn][:, 128:129], func=AF.Abs), r=[('ps', bn)], w=[smk])

                def scanB(w_):
                    (it, step, h, d_, c) = w_
                    hd = h * 2 + d_
                    bs, bn, bu = it % 2, 2 + it % 2, 4 + it % 2
                    smk = ('m_sm', hd)
                    kb.op('dve', lambda e: e.tensor_tensor(out=sm[:, hd, 1:2], in0=sm[:, hd, 0:1], in1=gd[:, 2, d_, c, h:h + 1], op=ALU.max),
                          r=[smk, 'm_gd'], w=[smk])
                    kb.op('dve', lambda e: e.reciprocal(out=sm[:, hd, 2:3], in_=sm[:, hd, 1:2]), r=[smk], w=[smk])
                    hk = ('m_hm', c, h)
                    hdst = hm[:, c, h * 128:(h + 1) * 128]
                    if (c, h) not in written:
                        written.add((c, h))
                        kb.op('act', lambda e: e.activation(out=hdst, in_=PS[bn][:, 0:128], func=AF.Copy, scale=sm[:, hd, 2:3]),
                              r=[('ps', bn), smk], w=[hk])
                    else:
                        kb.op('dve', lambda e: e.scalar_tensor_tensor(out=hdst, in0=PS[bn][:, 0:128], scalar=sm[:, hd, 2:3], in1=hdst,
                                                                      op0=ALU.mult, op1=ALU.add),
                              r=[('ps', bn), smk], w=[hk])
                    kb.op('dve', lambda e: e.scalar_tensor_tensor(out=Cst[:, hd, 0:129], in0=Cst[:, hd, 0:129], scalar=gd[:, 3, d_, c, h:h + 1],
                                                                  in1=PS[bu][:, 0:129], op0=ALU.mult, op1=ALU.add),
                          r=[('ps', bu), 'm_gd'], w=[('m_C', hd)])
                    kb.op('act', lambda e: e.activation(out=Cbf[:, hd, 0:129], in_=Cst[:, hd, 0:129], func=AF.Copy),
                          r=[('m_C', hd)], w=[('m_Cbf', hd)])

                work = []
                for step in range(16):
                    for h in range(4):
                        for d_ in range(2):
                            work.append((len(work), step, h, d_, step if d_ == 0 else 15 - step))
                prevw = None
                for w_ in work:
                    scanA(w_)
                    if prevw is not None:
                        scanB(prevw)
                    prevw = w_
                scanB(prevw)
                if 'hm' in dbg_out:
                    kb.dma('sp', dbg_out['hm'].rearrange("(c p) f -> p c f", p=128), hm[:], r=[('m_hm', c, h) for c in range(16) for h in range(4)], w=['dbg_hm'])
                for c in range(16):
                    for h in range(4):
                        kb.op('act', lambda e: e.activation(out=junk[:], in_=hm[:, c, h * 128:(h + 1) * 128], func=AF.Square,
                                                            accum_out=ssh[:, c, h:h + 1]), r=[('m_hm', c, h)], w=['m_junk', 'm_ssh'])
                kb.op('act', lambda e: e.activation(out=ssh[:], in_=ssh[:], func=AF.Sqrt, scale=1.0 / 128, bias=epsb[:, 0:1]), r=['m_ssh', 'epsb'], w=['m_ssh'])
                kb.op('dve', lambda e: e.reciprocal(out=ssh[:], in_=ssh[:]), r=['m_ssh'], w=['m_ssh'])
                for c in range(16):
                    for h in range(4):
                        kb.op('dve', lambda e: e.scalar_tensor_tensor(out=mot[:, h * 128:(h + 1) * 128], in0=hm[:, c, h * 128:(h + 1) * 128],
                                                                      scalar=ssh[:, c, h:h + 1], in1=gsig[:, c, h * 128:(h + 1) * 128],
                                                                      op0=ALU.mult, op1=ALU.mult),
                              r=[('m_hm', c, h), 'm_ssh', ('m_gsig', c)], w=['m_mot'])
                    for h in range(4):
                        kb.op('pe', lambda e: e.transpose(PST[:, h * 128:(h + 1) * 128], mot[:, h * 128:(h + 1) * 128], ident_bf),
                              r=['m_mot', 'cstb'], w=['pst'])
                    kb.op('act', lambda e: e.activation(out=catT[:, 4:8, c * 128:(c + 1) * 128], in_=PST[:, 0:512].rearrange("p (h t) -> p h t", t=128), func=AF.Copy),
                          r=['pst'], w=[('catT', 4 + h_) for h_ in range(4)])
            kb.barrier()
            chk('mlstm')
            st.close()

        def attn_phase(st0, l, hnT, catT):
            st = st0.enter_context(ExitStack())
            qR = sb(st, "a_qR", [128, 4, S], BF16)
            kR = sb(st, "a_kR", [128, 4, S], BF16)
            with ExitStack() as s1:
                cosT = sb(s1, "a_cos", [128, S], F32)
                sinT = sb(s1, "a_sin", [128, S], F32)
                Wns = [sb(s1, "a_wn%d" % i, [128, KT, 512], BF16) for i in range(2)]
                Ws = sb(s1, "a_ws", [128, KT, 512], BF16)
                t1s = [sb(s1, "a_t1%d" % i, [128, CH], F32) for i in range(2)]
                t2s = [sb(s1, "a_t2%d" % i, [128, CH], F32) for i in range(2)]
                kb.dma('sp', cosT[:], cos_d, w=['a_cos'])
                kb.dma('sp', sinT[:], sin_d, w=['a_sin'])
                pi = 0
                for qk in range(2):
                    dst = qR if qk == 0 else kR
                    dk = 'a_qR' if qk == 0 else 'a_kR'
                    Wn = Wns[qk]
                    if qk == 0:
                        load_w(Wns[0][:], w_in[l, :, 0:512], ('a_wn', 0))
                        load_w(Wns[1][:], w_in[l, :, 512:1024], ('a_wn', 1))
                    wn5 = Wn[:].rearrange("p k (h two j) -> p k h two j", two=2, j=32)
                    ws5 = Ws[:].rearrange("p k (h two j) -> p k h two j", two=2, j=32)
                    for half in range(2):
                        kb.op('dve', lambda e: e.tensor_copy(out=ws5[:, :, :, half, :], in_=wn5[:, :, :, 1 - half, :]), r=[('a_wn', qk)], w=['a_ws'])
                    for hp in range(4):
                        for c in range(NCH):
                            ba, bb = (pi % 2) * 2, (pi % 2) * 2 + 1
                            t1, t2 = t1s[pi % 2], t2s[pi % 2]
                            t1k, t2k = ('a_t1', pi % 2), ('a_t2', pi % 2)
                            pi += 1
                            for (bk, W, wk) in ((ba, Wn, ('a_wn', qk)), (bb, Ws, 'a_ws')):
                                for k in range(KT):
                                    kb.op('pe', lambda e: e.matmul(PS[bk][:], lhsT=W[:, k, hp * 128:(hp + 1) * 128], rhs=hnT[:, k, c * CH:(c + 1) * CH],
                                                                   start=(k == 0), stop=(k == KT - 1)), r=[wk, ('hnT', c)], w=[('ps', bk)])
                            kb.op('dve', lambda e: e.tensor_tensor(out=t1[:], in0=PS[ba][:], in1=cosT[:, c * CH:(c + 1) * CH], op=ALU.mult),
                                  r=[('ps', ba), 'a_cos'], w=[t1k])
                            kb.op('dve', lambda e: e.tensor_tensor(out=t2[:], in0=PS[bb][:], in1=sinT[:, c * CH:(c + 1) * CH], op=ALU.mult),
                                  r=[('ps', bb), 'a_sin'], w=[t2k])
                            kb.op('dve', lambda e: e.tensor_tensor(out=dst[:, hp, c * CH:(c + 1) * CH], in0=t1[:], in1=t2[:], op=ALU.add),
                                  r=[t1k, t2k], w=[(dk, hp)])
            kb.barrier()
            chk('aproj')
            with ExitStack() as s2:
                vb = [sb(s2, "a_vb%d" % i, [128, 16, 512], BF16) for i in range(3)]
                Wv = sb(s2, "a_wv", [128, KT, 512], BF16)
                numacc = sb(s2, "a_num", [128, S], F32)
                denacc = sb(s2, "a_den", [128, S], F32)
                load_w(Wv[:], w_in[l, :, 1024:1536], 'a_wv')
                DIL = (1, 4, 16)
                pi = 0
                for bi, dil in enumerate(DIL):
                    nb = (S // dil) // 128
                    for ti in range(16):
                        r_, j_ = divmod(ti, nb)
                        t0 = r_ + dil * 128 * j_
                        b = pi % 2
                        pi += 1
                        for k in range(KT):
                            kb.op('pe', lambda e: e.matmul(PS[b][:], lhsT=hnT[:, k, ssl(t0, 128, dil)], rhs=Wv[:, k, :],
                                                           start=(k == 0), stop=(k == KT - 1)),
                                  r=['a_wv'] + [('hnT', c) for c in range(NCH)], w=[('ps', b)])
                        kb.op('act', lambda e: e.activation(out=vb[bi][:, ti, :], in_=PS[b][:], func=AF.Copy), r=[('ps', b)], w=[('a_vb', bi)])
                qc = [sb(s2, "a_qc%d" % i, [128, S], BF16) for i in range(2)]
                kc = [sb(s2, "a_kc%d" % i, [128, S], BF16) for i in range(2)]
                pT3 = [sb(s2, "a_pTb%d" % i, [128, 512], BF16) for i in range(3)]
                state = {'it': 0, 'cc': 0}

                def emit_S(w_):
                    (hp, bi, dil, nb, r_, q0, qn, kts, qsrc, ksrc, qk_, kk_, sub0) = w_['a']
                    it = w_['it']
                    bset = it % 2
                    p_ = pT3[it % 3]
                    pk = ('a_pT', it % 3)
                    nkt = len(kts)
                    wdt = nkt * qn
                    for hh in range(2):
                        base = 64 * hh
                        bank = 2 * bset + hh
                        for i_, (kt, mk) in enumerate(kts):
                            slot = i_ * qn
                            if dil == 1:
                                lhs = ksrc[base:base + 64, hp, 128 * kt:128 * kt + 128]
                                rhs = qsrc[base:base + 64, hp, q0:q0 + qn]
                            else:
                                lhs = ksrc[base:base + 64, sub0 + 128 * kt:sub0 + 128 * kt + 128]
                                rhs = qsrc[base:base + 64, sub0 + q0:sub0 + q0 + qn]
                            kb.op('pe', lambda e: e.matmul(PS[bank][:, slot:slot + qn], lhsT=lhs, rhs=rhs, start=True, stop=True),
                                  r=[kk_, qk_], w=[('ps', bank)], inc=(hh == 1 and i_ == nkt - 1))
                    ncols = 2 * wdt
                    sidx = 3 if kts[0][1] == 'E' else (1 if kts[0][1] == 'F' else (0 if nkt == 2 else 2))
                    kb.op('act', lambda e: e.activation(out=p_[:, 0:ncols].rearrange("p (h w) -> p h w", h=2),
                                                        in_=psall[:, 2 * bset:2 * bset + 2, 0:wdt], func=AF.Exp, scale=0.125),
                          r=[('ps', 2 * bset), ('ps', 2 * bset + 1)], w=[pk])
                    kb.op('dve', lambda e: e.tensor_tensor(out=p_[:, 0:ncols], in0=p_[:, 0:ncols], in1=mstrip[:, sidx, 0:ncols], op=ALU.mult),
                          r=['mstrip'], w=[pk])

                def emit_PV(w_):
                    (hp, bi, dil, nb, r_, q0, qn, kts, qsrc, ksrc, qk_, kk_, sub0) = w_['a']
                    it = w_['it']
                    bnk = {0: 4 + (2 * it) % 3, 128: 4 + (2 * it + 1) % 3}
                    p_ = pT3[it % 3]
                    pk = ('a_pT', it % 3)
                    nkt = len(kts)
                    qsl = ssl(r_ + dil * q0, qn, dil)
                    for (c0, is_num) in ((0, True), (128, False)):
                        for hh in range(2):
                            base = 64 * hh
                            for i_, (kt, mk) in enumerate(kts):
                                slot = (hh * nkt + i_) * qn
                                hcol = (hp * 2 + hh) * 64
                                lhs = vb[bi][:, r_ * nb + kt, hcol:hcol + 64] if is_num else ones_bf[:, 0:64]
                                kb.op('pe', lambda e: e.matmul(PS[bnk[c0]][base:base + 64, 0:qn], lhsT=lhs, rhs=p_[:, slot:slot + qn],
                                                               start=(i_ == 0), stop=(i_ == nkt - 1), tile_position=(0, base)),
                                      r=[pk, ('a_vb', bi), 'cstb'], w=[('ps', bnk[c0])], inc=(hh == 1 and i_ == nkt - 1))
                    for (c0, acc, ak, eng) in ((0, numacc, 'a_num', 'act' if bi == 0 else 'dve'), (128, denacc, 'a_den', 'act' if bi == 0 else 'dve')):
                        if bi == 0:
                            if eng == 'act':
                                kb.op('act', lambda e: e.activation(out=acc[:, qsl], in_=PS[bnk[c0]][:, 0:qn], func=AF.Copy),
                                      r=[('ps', bnk[c0])], w=[(ak, 0)], loose=True)
                            else:
                                kb.op('dve', lambda e: e.tensor_copy(out=acc[:, qsl], in_=PS[bnk[c0]][:, 0:qn]),
                                      r=[('ps', bnk[c0])], w=[(ak, 0)], loose=True)
                        else:
                            kb.op('dve', lambda e: e.tensor_tensor(out=acc[:, qsl], in0=PS[bnk[c0]][:, 0:qn], in1=acc[:, qsl], op=ALU.add),
                                  r=[('ps', bnk[c0]), (ak, bi - 1)], w=[(ak, bi)], loose=True)

                for hp in range(4):
                    items = []
                    for bi, dil in enumerate(DIL):
                        nsub = S // dil
                        nb = nsub // 128
                        if dil == 1:
                            qsrc, ksrc, qk_, kk_ = qR, kR, ('a_qR', hp), ('a_kR', hp)
                        else:
                            cc = state['cc'] % 2
                            state['cc'] += 1
                            qsrc, ksrc, qk_, kk_ = qc[cc], kc[cc], ('a_qc', cc), ('a_kc', cc)
                            kb.op('pool', lambda e: e.tensor_copy(out=qsrc[:].rearrange("p (r i) -> p r i", r=dil),
                                                                  in_=qR[:, hp, :].rearrange("p (i r) -> p r i", r=dil)),
                                  r=[('a_qR', hp)], w=[qk_])
                            kb.op('pool', lambda e: e.tensor_copy(out=ksrc[:].rearrange("p (r i) -> p r i", r=dil),
                                                                  in_=kR[:, hp, :].rearrange("p (i r) -> p r i", r=dil)),
                                  r=[('a_kR', hp)], w=[kk_])
                        if nb == 1:
                            blocks = [(0, 128, [(0, 'E')])]
                        else:
                            blocks = [(0, 64, [(0, 'F')])]
                            blocks += [(64 + 128 * j, 128, [(j, 'A'), (j + 1, 'B')]) for j in range(nb - 1)]
                            blocks += [(nsub - 64, 64, [(nb - 1, 'A')])]
                        for r_ in range(dil):
                            for (q0, qn, kts) in blocks:
                                items.append({'a': (hp, bi, dil, nb, r_, q0, qn, kts, qsrc, ksrc, qk_, kk_, r_ * nsub), 'it': state['it']})
                                state['it'] += 1
                    prev = None
                    for w_ in items:
                        emit_S(w_)
                        if prev is not None:
                            emit_PV(prev)
                        prev = w_
                    emit_PV(prev)
                    nk = [('a_num', b_) for b_ in range(3)]
                    dk_ = [('a_den', b_) for b_ in range(3)]
                    kb.op('dve', lambda e: e.reciprocal(out=denacc[:], in_=denacc[:]), r=dk_, w=dk_)
                    kb.op('dve', lambda e: e.tensor_tensor(out=catT[:, hp, :], in0=numacc[:], in1=denacc[:], op=ALU.mult),
                          r=nk + dk_, w=[('catT', hp)] + nk)
            kb.barrier()
            chk('attn')
            st.close()

        def ffn_phase(st0, l, xsrc, xdst):
            st = st0.enter_context(ExitStack())
            aT = sb(st, "f_aT", [128, NFT, S], BF16)
            with ExitStack() as s1:
                hnT = sb(s1, "f_hnT", [128, KT, S], BF16)
                with ExitStack() as s0:
                    norm_pre(s0, xsrc, l, 2, hnT)
                kb.barrier()
                chk('fnorm')
                Wgs = [sb(s1, "f_wg%d" % i, [128, KT, 512], BF16) for i in range(2)]
                Wvs = [sb(s1, "f_wv%d" % i, [128, KT, 512], BF16) for i in range(2)]
                yp = [sb(s1, "f_yp%d" % i, [128, S + 2], F32) for i in range(2)]
                u = [sb(s1, "f_u%d" % i, [128, S], F32) for i in range(2)]
                gl = sb(s1, "f_gl", [128, S], BF16)
                for i in range(2):
                    kb.op('dve', lambda e: e.memset(yp[i][:, 0:1], 0.0), w=[('f_yp', i)])
                    kb.op('dve', lambda e: e.memset(yp[i][:, S + 1:S + 2], 0.0), w=[('f_yp', i)])
                pi = 0
                for c in range(NFT):
                    g4 = c % 4

                    def ldgrp(c_):
                        gi = (c_ // 4) % 2
                        n = min(4, NFT - c_) * 128
                        load_w(Wgs[gi][:, :, 0:n], w_up[l, :, c_ * 128:c_ * 128 + n], ('f_wg', gi))
                        load_w(Wvs[gi][:, :, 0:n], w_up[l, :, DFF + c_ * 128:DFF + c_ * 128 + n], ('f_wv', gi))
                    if c == 0:
                        ldgrp(0)
                    if g4 == 0 and c + 4 < NFT:
                        ldgrp(c + 4)
                    gi_ = (c // 4) % 2
                    Wg, Wv = Wgs[gi_], Wvs[gi_]
                    for part, (W, wk) in enumerate(((Wg, ('f_wg', gi_)), (Wv, ('f_wv', gi_)))):
                        ti = part * NFT + c
                        for ch in range(NCH):
                            b = pi % 4
                            pi += 1
                            for k in range(KT):
                                kb.op('pe', lambda e: e.matmul(PS[b][:], lhsT=W[:, k, g4 * 128:(g4 + 1) * 128], rhs=hnT[:, k, ch * CH:(ch + 1) * CH],
                                                               start=(k == 0), stop=(k == KT - 1)), r=[wk, ('hnT', ch)], w=[('ps', b)])
                            kb.op('act', lambda e: e.activation(out=yp[part][:, 1 + ch * CH:1 + (ch + 1) * CH], in_=PS[b][:], func=AF.Copy),
                                  r=[('ps', b)], w=[('f_yp', part)])
                        kb.op('act', lambda e: e.activation(out=u[part][:], in_=yp[part][:, 0:S], func=AF.Identity,
                                                            scale=fconv[:, l, ti, 0:1], bias=fconv[:, l, ti, 3:4]),
                              r=[('f_yp', part), 'fconv'], w=[('f_u', part)])
                        for jj in (1, 2):
                            kb.op('dve', lambda e: e.scalar_tensor_tensor(out=u[part][:], in0=yp[part][:, jj:jj + S], scalar=fconv[:, l, ti, jj:jj + 1],
                                                                          in1=u[part][:], op0=ALU.mult, op1=ALU.add),
                                  r=[('f_yp', part), 'fconv'], w=[('f_u', part)])
                    kb.op('act', lambda e: e.activation(out=gl[:], in_=u[0][:], func=AF.Gelu_apprx_tanh), r=[('f_u', 0)], w=['f_gl'])
                    kb.op('dve', lambda e: e.tensor_tensor(out=aT[:, c, :], in0=gl[:], in1=u[1][:], op=ALU.mult),
                          r=['f_gl', ('f_u', 1)], w=[('f_aT', c)])
            kb.barrier()
            chk('fup')
            with ExitStack() as s2:
                epilogue(s2, w_down[l], NFT, lambda ci, c: aT[:, ci, c * CH:(c + 1) * CH], lambda ci, c: [('f_aT', ci)], xsrc, xdst, l, 3)
            kb.barrier()
            st.close()

        xcur = xT_in
        try:
          for l in range(nlayers if stop_at != 'init' else 0):
              xmid = xs[0]
              xnext = yT if l == nlayers - 1 else xs[1]
              with ExitStack() as sm_:
                  catT = sb(sm_, "catT", [128, KT, S], BF16)
                  with ExitStack() as sh_:
                      hnT = sb(sh_, "hnT", [128, KT, S], BF16)
                      with ExitStack() as s0:
                          norm_pre(s0, xcur, l, 0, hnT)
                      kb.barrier()
                      chk('norm')
                      if 'hn' in dbg_out and l == 0:
                          pass
                      mlstm_phase(sh_, l, hnT, catT)
                      attn_phase(sh_, l, hnT, catT)
                  if 'cat' in dbg_out and l == dbg.get('_layer', 0):
                      with ExitStack() as sd:
                          tmpf = sb(sd, "dbg_tmp", [128, S], F32)
                          for i in range(KT):
                              kb.op('act', lambda e: e.activation(out=tmpf[:], in_=catT[:, i, :], func=AF.Copy), r=[('catT', i)], w=['dbg_tmp'])
                              kb.dma('sp', dbg_out['cat'][i * 128:(i + 1) * 128, :], tmpf[:], r=['dbg_tmp'], w=['dbg_cat'])
                      kb.barrier()
                  with ExitStack() as se:
                      epilogue(se, w_out[l], KT, lambda ci, c: catT[:, ci, c * CH:(c + 1) * CH], lambda ci, c: [('catT', ci)], xcur, xmid, l, 1)
                  kb.barrier()
                  chk('ep1')
              with ExitStack() as sf:
                  ffn_phase(sf, l, xmid, xnext)
              xcur = xnext
        except _Stop:
            pass
        kb.barrier()
    stuck = kb.check_deadlock()
    if stuck:
        raise RuntimeError('static deadlock: %r' % (stuck,))
    return nc, list(dbg_out.keys())


def _host_prep(inputs):
    f = np.float32
    g = np.stack([inputs['mix_pre_g'], inputs['mix_post_g'], inputs['ffn_pre_g'], inputs['ffn_post_g']], axis=1)
    gvec = np.ascontiguousarray(g.reshape(2, 4, KT, 128).transpose(3, 0, 1, 2)).reshape(128, -1).astype(f)
    mc = np.concatenate([inputs['mlstm_conv_w'], inputs['mlstm_conv_b'][:, None, :]], axis=1)
    mconv = np.ascontiguousarray(mc.reshape(2, 6, 8, 128).transpose(3, 0, 2, 1)).reshape(128, -1).astype(f)
    fc = np.concatenate([inputs['ffn_conv_w'], inputs['ffn_conv_b'][:, None, :]], axis=1)
    fconv = np.ascontiguousarray(fc.reshape(2, 4, 44, 128).transpose(3, 0, 2, 1)).reshape(128, -1).astype(f)
    gateb = np.ascontiguousarray(np.broadcast_to(inputs['mlstm_gate_b'].reshape(1, -1), (128, 32))).astype(f)
    headg = np.ascontiguousarray(np.broadcast_to(inputs['mlstm_head_g'].reshape(1, -1), (128, 1024))).astype(f)
    p = np.arange(128)
    inv_freq = (10000.0 ** (-np.arange(0, 64, 2, dtype=np.float32) / 64)).astype(np.float32)
    ang = np.arange(S, dtype=np.float32)[None, :] * inv_freq[p % 32][:, None]
    cosT = np.cos(ang).astype(f)
    sgn = np.where((p % 64) < 32, -1.0, 1.0).astype(f)[:, None]
    sinT = (np.sin(ang) * sgn).astype(f)
    a = p[:, None]
    b = p[None, :]
    NEG = -30000.0
    cst = np.zeros((128, 9, 128), f)
    cst[:, 0] = 1.0
    cst[:, 1] = (a == b)
    cst[:, 2] = np.where(a >= b, 0.0, NEG)
    cst[:, 3] = np.where(a <= b, 0.0, NEG)
    cst[:, 4] = np.where(a <= b + 64, 0.0, NEG)
    cst[:, 5] = np.where(np.abs(a - b) <= 64, 0.0, NEG)
    cst[:, 6] = (a <= b)
    cst[:, 7] = (a >= b)
    A01 = (a >= b).astype(f); B01 = (a <= b).astype(f); F01 = (a <= b + 64).astype(f); E01 = (np.abs(a - b) <= 64).astype(f)
    ms = np.zeros((128, 4, 512), f)
    ms[:, 0] = np.concatenate([A01, B01, A01, B01], axis=1)
    ms[:, 1, 0:128] = np.concatenate([F01[:, :64], F01[:, :64]], axis=1)
    ms[:, 2, 0:128] = np.concatenate([A01[:, :64], A01[:, :64]], axis=1)
    ms[:, 3, 0:256] = np.concatenate([E01, E01], axis=1)
    shared = dict(mstrip=ms.reshape(128, -1), w_in=np.ascontiguousarray(inputs['w_in'], dtype=f), w_out=np.ascontiguousarray(inputs['w_out'], dtype=f),
                  w_up=np.ascontiguousarray(inputs['w_up'], dtype=f), w_down=np.ascontiguousarray(inputs['w_down'], dtype=f),
                  gvec=gvec, mconv=mconv, fconv=fconv, gateb=gateb, headg=headg, cosT=cosT, sinT=sinT,
                  cst=cst.reshape(128, -1))
    return shared


_NC_CACHE = {}


def kernel(**inputs):
    x = np.asarray(inputs['x'], dtype=np.float32)
    B = x.shape[0]
    shared = _host_prep(inputs)
    if 'nc' not in _NC_CACHE:
        _NC_CACHE['nc'] = build(2)[0]
    nc = _NC_CACHE['nc']
    in_maps = []
    for b in range(B):
        m = dict(shared)
        m['xT'] = np.ascontiguousarray(x[b].T)
        in_maps.append(m)
    res = run_bass_kernel_spmd(nc, in_maps, core_ids=list(range(B)))
    out = np.stack([np.ascontiguousarray(res.results[b]['yT'].T) for b in range(B)], axis=0)
    return out.astype(np.float32)
```

```python
import math
from contextlib import ExitStack
import numpy as np
import concourse.bass as bass
import concourse.mybir as mybir
from concourse.bass_utils import run_bass_kernel_spmd

F32 = mybir.dt.float32
BF16 = mybir.dt.bfloat16
AF = mybir.ActivationFunctionType
ALU = mybir.AluOpType

S = 2048
D = 1024
KT = 8
NCH = 4
CH = 512
DFF = 2816
NFT = 22
EPS = 1e-6
NDS = 24


def ssl(start, n, step):
    return slice(start, start + step * (n - 1) + 1, step)


class KB:
    def __init__(self, nc):
        self.nc = nc
        self.E = {'pe': nc.tensor, 'act': nc.scalar, 'dve': nc.vector, 'pool': nc.gpsimd, 'sp': nc.sync}
        self.sem = {e: nc.alloc_semaphore('c_' + e) for e in ('pe', 'act', 'dve', 'pool')}
        self.cnt = {e: 0 for e in self.sem}
        self.waited = {e: {} for e in self.E}
        self.dsems = [nc.alloc_semaphore('d%d' % i) for i in range(NDS)]
        self.dcnt = [0] * NDS
        self.dlast = [None] * NDS
        self.dnext = 0
        self.dnext_p = 0
        self.tk = {}
        self.dead = False
        self.prog = {e: [] for e in self.E}

    def _t(self, key):
        t = self.tk.get(key)
        if t is None:
            t = {'w': None, 'r': {}}
            self.tk[key] = t
        return t

    def _wait(self, e, ev):
        sem, val, sid = ev
        if self.waited[e].get(sid, 0) >= val:
            return
        self.E[e].wait_ge(sem, val)
        self.prog[e].append(('w', sid, val))
        self.waited[e][sid] = val

    def _deps(self, e, r, w, loose=False):
        evs = {}

        def add(ev):
            if ev is None:
                return
            if ev[2] not in evs or evs[ev[2]][1] < ev[1]:
                evs[ev[2]] = ev
        for k in r:
            add(self._t(k)['w'])
        for k in w:
            t = self._t(k)
            add(t['w'])
            for ev in t['r'].values():
                add(ev)
        for sid, ev in evs.items():
            if sid == e and (e == 'pe' or loose):
                continue
            self._wait(e, ev)

    def _record(self, ev, r, w):
        for k in r:
            self._t(k)['r'][ev[2]] = ev
        for k in w:
            t = self._t(k)
            t['w'] = ev
            t['r'] = {}

    def op(self, e, fn, r=(), w=(), loose=False, inc=True):
        if self.dead:
            return
        self._deps(e, r, w, loose)
        ins = fn(self.E[e])
        if inc:
            self.cnt[e] += 1
            ins.then_inc(self.sem[e], 1)
            self.prog[e].append(('i', e, 1))
            self._record((self.sem[e], self.cnt[e], e), r, w)
        else:
            self._record((self.sem[e], self.cnt[e] + 1, e), r, w)

    def dma(self, q, out, in_, r=(), w=()):
        if self.dead:
            return
        if q == 'pool':
            i = 16 + self.dnext_p
            self.dnext_p = (self.dnext_p + 1) % (NDS - 16)
        else:
            i = self.dnext
            self.dnext = (i + 1) % 16
        if self.dlast[i] is not None:
            self._wait(q, self.dlast[i])
        self._deps(q, r, w)
        ins = self.E[q].dma_start(out=out, in_=in_)
        self.dcnt[i] += 16
        ins.then_inc(self.dsems[i], 16)
        self.prog[q].append(('i', 'd%d' % i, 16))
        ev = (self.dsems[i], self.dcnt[i], 'd%d' % i)
        self.dlast[i] = ev
        self._record(ev, r, w)
        return ev

    def check_deadlock(self):
        pc = {e: 0 for e in self.prog}
        val = {}
        progress = True
        while progress:
            progress = False
            for e, p in self.prog.items():
                while pc[e] < len(p):
                    k, sid, v = p[pc[e]]
                    if k == 'w':
                        if val.get(sid, 0) < v:
                            break
                    else:
                        val[sid] = val.get(sid, 0) + v
                    pc[e] += 1
                    progress = True
        stuck = {e: (pc[e], len(p), p[pc[e]]) for e, p in self.prog.items() if pc[e] < len(p)}
        return stuck

    def barrier(self):
        if self.dead:
            return
        evs = [(self.sem[e], self.cnt[e], e) for e in self.sem if self.cnt[e] > 0]
        evs += [ev for ev in self.dlast if ev is not None]
        for e in self.E:
            for ev in evs:
                if ev[2] == e:
                    continue
                self._wait(e, ev)


class _Stop(Exception):
    pass


def build(nlayers=2, dbg=None, stop_at=None):
    dbg = dbg or {}
    nc = bass.Bass("TRN2", target_bir_lowering=False)
    kb = KB(nc)
    ein = lambda n, s, dt=F32: nc.dram_tensor(n, list(s), dt, kind="ExternalInput").ap()
    xT_in = ein("xT", [D, S])
    w_in = ein("w_in", [2, D, 3600])
    w_out = ein("w_out", [2, D, D])
    w_up = ein("w_up", [2, D, 2 * DFF])
    w_down = ein("w_down", [2, DFF, D])
    gvec_d = ein("gvec", [128, 2 * 4 * KT])
    mconv_d = ein("mconv", [128, 2 * 8 * 6])
    fconv_d = ein("fconv", [128, 2 * 44 * 4])
    gateb_d = ein("gateb", [128, 2 * 16])
    headg_d = ein("headg", [128, 2 * 512])
    cos_d = ein("cosT", [128, S])
    sin_d = ein("sinT", [128, S])
    mstrip_d = ein("mstrip", [128, 4 * 512])
    cst_d = ein("cst", [128, 9 * 128])
    yT = nc.dram_tensor("yT", [D, S], F32, kind="ExternalOutput").ap()
    xs = [nc.dram_tensor("xs%d" % i, [D, S], F32).ap() for i in range(2)]
    dbg_out = {}
    for name, shape in dbg.items():
        dbg_out[name] = nc.dram_tensor("dbg_" + name, list(shape), F32, kind="ExternalOutput").ap()

    pm = lambda ap: ap.rearrange("(k p) t -> p k t", p=128)

    with ExitStack() as top:
        uid = [0]

        def sb(st, name, shape, dt):
            uid[0] += 1
            return st.enter_context(nc.sbuf_tensor("s%d_%s" % (uid[0], name), list(shape), dt))
        gvec = sb(top, "gvec", [128, 2, 4, KT], F32)
        mconv = sb(top, "mconv", [128, 2, 8, 6], F32)
        fconv = sb(top, "fconv", [128, 2, 44, 4], F32)
        gateb = sb(top, "gateb", [128, 2, 16], F32)
        headg = sb(top, "headg", [128, 2, 512], BF16)
        cstb = sb(top, "cstb", [128, 9, 128], BF16)
        cstf = sb(top, "cstf", [128, 3, 128], F32)
        epsb = sb(top, "epsb", [128, 1], F32)
        mstrip = sb(top, "mstrip", [128, 4, 512], BF16)
        psall = top.enter_context(nc.psum_tensor("psall", [128, 7, 512], F32))
        PS = [psall[:, i, :] for i in range(7)]
        PST = top.enter_context(nc.psum_tensor("pst", [128, 1024], BF16))
        for i in range(7):
            kb._t(('ps', i))
        kb.dma('sp', gvec[:].rearrange("p a b c -> p (a b c)"), gvec_d, w=['gvec'])
        kb.dma('sp', mconv[:].rearrange("p a b c -> p (a b c)"), mconv_d, w=['mconv'])
        kb.dma('sp', fconv[:].rearrange("p a b c -> p (a b c)"), fconv_d, w=['fconv'])
        kb.dma('sp', gateb[:].rearrange("p a b -> p (a b)"), gateb_d, w=['gateb'])
        kb.dma('pool', headg[:].rearrange("p a b -> p (a b)"), headg_d, w=['headg'])
        kb.dma('pool', cstb[:].rearrange("p a b -> p (a b)"), cst_d, w=['cstb'])
        kb.dma('pool', mstrip[:].rearrange("p a b -> p (a b)"), mstrip_d, w=['mstrip'])
        cview = cst_d.rearrange("p (a b) -> p a b", b=128)
        kb.dma('sp', cstf[:, 0, :], cview[:, 0, :], w=['cstf'])
        kb.dma('sp', cstf[:, 1:3, :], cview[:, 6:8, :], w=['cstf'])
        kb.op('dve', lambda e: e.memset(epsb[:], EPS), w=['epsb'])
        ones_bf = cstb[:, 0, :]
        ident_bf = cstb[:, 1, :]
        MASK = {'A': cstb[:, 2, :], 'B': cstb[:, 3, :], 'F': cstb[:, 4, :], 'E': cstb[:, 5, :]}
        tri_bf = [cstb[:, 6, :], cstb[:, 7, :]]
        ones_f = cstf[:, 0, :]
        tri_f = [cstf[:, 1, :], cstf[:, 2, :]]

        def chk(name):
            if stop_at == name:
                kb.barrier()
                kb.dead = True

        def dump(name, sb_ap, rkeys):
            if name in dbg_out:
                kb.dma('sp', dbg_out[name], sb_ap, r=rkeys, w=['dbg_' + name])

        def rstd_from_ps(ps_i, rstd_ap, rkey, n):
            kb.op('act', lambda e: e.activation(out=rstd_ap, in_=PS[ps_i][:, 0:rstd_ap.shape[1]], func=AF.Sqrt,
                                                scale=1.0 / n, bias=epsb[:, 0:1]),
                  r=[('ps', ps_i), 'epsb'], w=[rkey])
            kb.op('dve', lambda e: e.reciprocal(out=rstd_ap, in_=rstd_ap), r=[rkey], w=[rkey])

        def norm_pre(st, xsrc, l, j, hnT):
            xcs = [sb(st, "np_xc%d" % i, [128, KT, CH], F32) for i in range(2)]
            sq = sb(st, "np_sq", [128, KT, CH], BF16)
            rstd = sb(st, "np_rstd", [128, CH], F32)
            for c in range(NCH):
                xc = xcs[c % 2]
                xk = ('np_xc', c % 2)
                kb.dma('sp', xc[:], pm(xsrc)[:, :, c * CH:(c + 1) * CH], w=[xk])
                kb.op('act', lambda e: e.activation(out=sq[:], in_=xc[:], func=AF.Square), r=[xk], w=['np_sq'])
                for k in range(KT):
                    kb.op('pe', lambda e: e.matmul(PS[6][:], lhsT=ones_bf, rhs=sq[:, k, :], start=(k == 0), stop=(k == KT - 1)),
                          r=['np_sq', 'cstb'], w=[('ps', 6)])
                rstd_from_ps(6, rstd[:], 'np_rstd', D)
                for k in range(KT):
                    kb.op('dve', lambda e: e.scalar_tensor_tensor(out=hnT[:, k, c * CH:(c + 1) * CH], in0=xc[:, k, :],
                                                                  scalar=gvec[:, l, j, k:k + 1], in1=rstd[:],
                                                                  op0=ALU.mult, op1=ALU.mult),
                          r=[xk, 'np_rstd', 'gvec'], w=[('hnT', c)])

        def load_w(dst, src_rows_cols, wkey):
            kb.dma('pool', dst, src_rows_cols.rearrange("(k p) c -> p k c", p=128), w=[wkey])

        def epilogue(st, Wd, nct, rhs_fn, rkeys_fn, xsrc, xdst, l, j, che=CH, hn_out=None, nxt=None):
            W = sb(st, "ep_w", [128, nct, D], BF16)
            xcs = [sb(st, "ep_xc%d" % i, [128, KT, che], F32) for i in range(2)]
            ff = sb(st, "ep_ff", [128, KT, che], F32)
            sq = sb(st, "ep_sq", [128, KT, che], BF16)
            rstd = sb(st, "ep_rstd", [128, che], F32)
            for hf in range(2):
                for k0 in range(0, nct, 8):
                    k1 = min(nct, k0 + 8)
                    load_w(W[:, k0:k1, hf * 512:(hf + 1) * 512], Wd[k0 * 128:k1 * 128, hf * 512:(hf + 1) * 512], ('ep_w', hf))
            pi = 0
            for c in range(S // che):
                tsl = slice(c * che, (c + 1) * che)
                xc = xcs[c % 2]
                xk = ('ep_xc', c % 2)
                kb.dma('sp', xc[:], pm(xsrc)[:, :, tsl], w=[xk])
                for m in range(KT):
                    b = pi % 4
                    pi += 1
                    for ci in range(nct):
                        kb.op('pe', lambda e: e.matmul(PS[b][:, 0:che], lhsT=W[:, ci, m * 128:(m + 1) * 128], rhs=rhs_fn(ci, tsl),
                                                       start=(ci == 0), stop=(ci == nct - 1)),
                              r=[('ep_w', m // 4)] + rkeys_fn(ci, c), w=[('ps', b)], inc=(ci == nct - 1))
                    kb.op('act', lambda e: e.activation(out=ff[:, m, :], in_=PS[b][:, 0:che], func=AF.Copy), r=[('ps', b)], w=[('ep_ff', m)])
                    kb.op('act', lambda e: e.activation(out=sq[:, m, :], in_=ff[:, m, :], func=AF.Square), r=[('ep_ff', m)], w=[('ep_sq', m)])
                for m in range(KT):
                    kb.op('pe', lambda e: e.matmul(PS[6][:, 0:che], lhsT=ones_bf, rhs=sq[:, m, :], start=(m == 0), stop=(m == KT - 1)),
                          r=[('ep_sq', m), 'cstb'], w=[('ps', 6)], inc=(m == KT - 1))
                rstd_from_ps(6, rstd[:], 'ep_rstd', D)
                for m in range(KT):
                    kb.op('dve', lambda e: e.scalar_tensor_tensor(out=ff[:, m, :], in0=ff[:, m, :], scalar=gvec[:, l, j, m:m + 1],
                                                                  in1=rstd[:], op0=ALU.mult, op1=ALU.mult),
                          r=['ep_rstd', 'gvec'], w=[('ep_ff', m)])
                    kb.op('dve', lambda e: e.tensor_tensor(out=xc[:, m, :], in0=xc[:, m, :], in1=ff[:, m, :], op=ALU.add),
                          r=[('ep_ff', m)], w=[xk])
                kb.dma('sp', pm(xdst)[:, :, tsl], xc[:], r=[xk], w=[('xdst', c)])
                if hn_out is not None:
                    l2, j2 = nxt
                    allsq = [('ep_sq', m) for m in range(KT)]
                    kb.op('act', lambda e: e.activation(out=sq[:], in_=xc[:], func=AF.Square), r=[xk], w=allsq)
                    for m in range(KT):
                        kb.op('pe', lambda e: e.matmul(PS[6][:, 0:che], lhsT=ones_bf, rhs=sq[:, m, :], start=(m == 0), stop=(m == KT - 1)),
                              r=[('ep_sq', m), 'cstb'], w=[('ps', 6)], inc=(m == KT - 1))
                    rstd_from_ps(6, rstd[:], 'ep_rstd', D)
                    for m in range(KT):
                        kb.op('dve', lambda e: e.scalar_tensor_tensor(out=hn_out[:, m, tsl], in0=xc[:, m, :], scalar=gvec[:, l2, j2, m:m + 1],
                                                                      in1=rstd[:], op0=ALU.mult, op1=ALU.mult),
                              r=[xk, 'ep_rstd', 'gvec'], w=[('hnT', (c * che) // CH)])

        def mlstm_phase(st0, l, hnT, catT):
            st = st0.enter_context(ExitStack())
            qkT = sb(st, "m_qkT", [128, 8, S], BF16)
            kTM = sb(st, "m_kTM", [128, 16, 512], BF16)
            vaug = sb(st, "m_vaug", [128, 16, 4, 132], BF16)
            gsig = sb(st, "m_gsig", [128, 16, 512], BF16)
            gates = sb(st, "m_gates", [128, 16, 16], F32)
            l1 = sb(st, "m_l1", [128, 16, 8], F32)
            gd = sb(st, "m_gd", [128, 5, 2, 16, 4], F32)
            with ExitStack() as sp_:
                Wa = sb(sp_, "m_wa", [128, KT, 512], BF16)
                Wb = sb(sp_, "m_wb", [128, KT, 512], BF16)
                Wg = sb(sp_, "m_wg", [128, KT, 16], BF16)
                ypads = [sb(sp_, "m_ypad%d" % i, [128, S + 4], BF16) for i in range(2)]
                uaccs = [sb(sp_, "m_uacc%d" % i, [128, S], F32) for i in range(2)]
                sgt = sb(sp_, "m_sgt", [128, 512], BF16)
                load_w(Wa[:], w_in[l, :, 1536:2048], 'm_wa')
                load_w(Wb[:], w_in[l, :, 2048:2560], 'm_wb')
                load_w(Wg[:], w_in[l, :, 3584:3600], 'm_wg')
                for i in range(2):
                    kb.op('dve', lambda e: e.memset(ypads[i][:, 0:2], 0.0), w=[('m_ypad', i)])
                    kb.op('dve', lambda e: e.memset(ypads[i][:, S + 2:S + 4], 0.0), w=[('m_ypad', i)])
                kb.op('dve', lambda e: e.memset(vaug[:, :, :, 128:129], 1.0), w=['m_vaug1'])
                pi = 0
                for i in range(8):
                    W = Wa if i < 4 else Wb
                    wk = 'm_wa' if i < 4 else 'm_wb'
                    cs = (i % 4) * 128
                    ypad, uacc = ypads[i % 2], uaccs[i % 2]
                    yk, uk = ('m_ypad', i % 2), ('m_uacc', i % 2)
                    for c in range(NCH):
                        b = pi % 4
                        pi += 1
                        for k in range(KT):
                            kb.op('pe', lambda e: e.matmul(PS[b][:], lhsT=W[:, k, cs:cs + 128], rhs=hnT[:, k, c * CH:(c + 1) * CH],
                                                           start=(k == 0), stop=(k == KT - 1)),
                                  r=[wk, ('hnT', c)], w=[('ps', b)])
                        kb.op('act', lambda e: e.activation(out=ypad[:, 2 + c * CH:2 + (c + 1) * CH], in_=PS[b][:], func=AF.Copy),
                              r=[('ps', b)], w=[yk])
                    kb.op('act', lambda e: e.activation(out=uacc[:], in_=ypad[:, 0:S], func=AF.Identity,
                                                        scale=mconv[:, l, i, 0:1], bias=mconv[:, l, i, 5:6]),
                          r=[yk, 'mconv'], w=[uk])
                    for jj in range(1, 5):
                        kb.op('dve', lambda e: e.scalar_tensor_tensor(out=uacc[:], in0=ypad[:, jj:jj + S], scalar=mconv[:, l, i, jj:jj + 1],
                                                                      in1=uacc[:], op0=ALU.mult, op1=ALU.add),
                              r=[yk, 'mconv'], w=[uk])
                    kb.op('act', lambda e: e.activation(out=qkT[:, i, :], in_=uacc[:], func=AF.Silu), r=[uk], w=[('m_qkT', i)])
                if 'qk' in dbg_out:
                    for i in range(8):
                        kb.op('act', lambda e: e.activation(out=uaccs[0][:], in_=qkT[:, i, :], func=AF.Copy), r=[('m_qkT', i)], w=[('m_uacc', 0)])
                        dump_ap = dbg_out['qk'][i * 128:(i + 1) * 128, :]
                        kb.dma('sp', dump_ap, uaccs[0][:], r=[('m_uacc', 0)], w=['dbg_qk'])
                for c in range(16):
                    for h in range(4):
                        kb.op('pe', lambda e: e.transpose(PST[:, h * 128:(h + 1) * 128], qkT[:, 4 + h, c * 128:(c + 1) * 128], ident_bf),
                              r=[('m_qkT', 4 + h), 'cstb'], w=['pst'])
                    kb.op('act', lambda e: e.activation(out=kTM[:, c, :], in_=PST[:, 0:512], func=AF.Copy), r=['pst'], w=[('m_kTM', c)])
                load_w(Wa[:], w_in[l, :, 2560:3072], 'm_wa')
                load_w(Wb[:], w_in[l, :, 3072:3584], 'm_wb')
                for c in range(16):
                    tk = ('hnT', c // 4)
                    for k in range(KT):
                        kb.op('pe', lambda e: e.matmul(PS[0][:], lhsT=hnT[:, k, c * 128:(c + 1) * 128], rhs=Wa[:, k, :],
                                                       start=(k == 0), stop=(k == KT - 1)), r=['m_wa', tk], w=[('ps', 0)])
                    kb.op('act', lambda e: e.activation(out=vaug[:, c, :, 0:128], in_=PS[0][:].rearrange("p (h f) -> p h f", f=128), func=AF.Copy),
                          r=[('ps', 0)], w=[('m_vaug', c)])
                    for k in range(KT):
                        kb.op('pe', lambda e: e.matmul(PS[1][:], lhsT=hnT[:, k, c * 128:(c + 1) * 128], rhs=Wb[:, k, :],
                                                       start=(k == 0), stop=(k == KT - 1)), r=['m_wb', tk], w=[('ps', 1)])
                    kb.op('act', lambda e: e.activation(out=sgt[:], in_=PS[1][:], func=AF.Sigmoid), r=[('ps', 1)], w=['m_sgt'])
                    kb.op('dve', lambda e: e.tensor_tensor(out=gsig[:, c, :], in0=sgt[:], in1=headg[:, l, :], op=ALU.mult),
                          r=['m_sgt', 'headg'], w=[('m_gsig', c)])
                    for k in range(KT):
                        kb.op('pe', lambda e: e.matmul(PS[2][:, 0:16], lhsT=hnT[:, k, c * 128:(c + 1) * 128], rhs=Wg[:, k, :],
                                                       start=(k == 0), stop=(k == KT - 1)), r=['m_wg', tk], w=[('ps', 2)])
                    kb.op('dve', lambda e: e.tensor_tensor(out=gates[:, c, :], in0=PS[2][:, 0:16], in1=gateb[:, l, :], op=ALU.add),
                          r=[('ps', 2), 'gateb'], w=['m_gates'])
                for d_ in range(2):
                    kb.op('act', lambda e: e.activation(out=l1[:, :, 4 * d_:4 * d_ + 4], in_=gates[:, :, 8 * d_ + 4:8 * d_ + 8], func=AF.Exp, scale=-1.0),
                          r=['m_gates'], w=['m_l1'])
                kb.op('act', lambda e: e.activation(out=l1[:], in_=l1[:], func=AF.Ln, bias=1.0), r=['m_l1'], w=['m_l1'])
                l1h = sb(sp_, "m_l1h", [128, 2, 16, 8], BF16)
                l1r = sb(sp_, "m_l1r", [128, 16, 8], F32)
                kb.op('dve', lambda e: e.tensor_copy(out=l1h[:, 0], in_=l1[:]), r=['m_l1'], w=['m_l1h'])
                kb.op('dve', lambda e: e.tensor_copy(out=l1r[:], in_=l1h[:, 0]), r=['m_l1h'], w=['m_l1r'])
                kb.op('dve', lambda e: e.tensor_tensor(out=l1r[:], in0=l1[:], in1=l1r[:], op=ALU.subtract), r=['m_l1', 'm_l1r'], w=['m_l1r'])
                kb.op('dve', lambda e: e.tensor_copy(out=l1h[:, 1], in_=l1r[:]), r=['m_l1r'], w=['m_l1h'])
                for d_ in range(2):
                    for (bank, lhs) in ((3, tri_bf[d_]), (4, ones_bf)):
                        for hl in range(2):
                            kb.op('pe', lambda e: e.matmul(PS[bank][:, 64 * d_:64 * d_ + 64].rearrange("p (c h) -> p c h", h=4), lhsT=lhs,
                                                           rhs=l1h[:, hl, :, 4 * d_:4 * d_ + 4], start=(hl == 0), stop=(hl == 1)),
                                  r=['m_l1h', 'cstb'], w=[('ps', bank)])
                    na = gd[:, 0, d_]
                    kb.op('act', lambda e: e.activation(out=na, in_=PS[3][:, 64 * d_:64 * d_ + 64].rearrange("p (c h) -> p c h", h=4), func=AF.Copy),
                          r=[('ps', 3)], w=['m_gd'])
                    kb.op('dve', lambda e: e.tensor_tensor(out=gd[:, 1, d_], in0=gates[:, :, 8 * d_:8 * d_ + 4], in1=na, op=ALU.add),
                          r=['m_gates', 'm_gd'], w=['m_gd'])
                    kb.op('act', lambda e: e.activation(out=gd[:, 1, d_], in_=gd[:, 1, d_], func=AF.Exp, bias=-0.5 * math.log(128.0)),
                          r=['m_gd'], w=['m_gd'])
                    kb.op('act', lambda e: e.activation(out=gd[:, 2, d_], in_=na, func=AF.Exp), r=['m_gd'], w=['m_gd'])
                    kb.op('act', lambda e: e.activation(out=gd[:, 3, d_], in_=PS[4][:, 64 * d_:64 * d_ + 64].rearrange("p (c h) -> p c h", h=4),
                                                        func=AF.Exp, scale=-1.0), r=[('ps', 4)], w=['m_gd'])
                    kb.op('dve', lambda e: e.tensor_tensor(out=gd[:, 4, d_], in0=gd[:, 1, d_], in1=gd[:, 3, d_], op=ALU.mult),
                          r=['m_gd'], w=['m_gd'])
            kb.barrier()
            chk('mproj')
            with ExitStack() as ss_:
                hm = sb(ss_, "m_hm", [128, 16, 512], F32)
                Cst = sb(ss_, "m_C", [128, 8, 132], F32)
                Cbf = sb(ss_, "m_Cbf", [128, 8, 132], BF16)
                PT = [sb(ss_, "m_PT%d" % i, [128, 128], BF16) for i in range(4)]
                Kt = [sb(ss_, "m_Kt%d" % i, [128, 128], BF16) for i in range(4)]
                sm = sb(ss_, "m_sm", [128, 8, 4], F32)
                ssh = sb(ss_, "m_ssh", [128, 16, 4], F32)
                junk = sb(ss_, "m_junk", [128, 128], BF16)
                mot = sb(ss_, "m_mot", [128, 512], BF16)
                kb.op('dve', lambda e: e.memset(Cst[:], 0.0), w=[('m_C', i) for i in range(8)])
                kb.op('dve', lambda e: e.memset(Cbf[:], 0.0), w=[('m_Cbf', i) for i in range(8)])
                written = set()

                def scanA(w_):
                    (it, step, h, d_, c) = w_
                    hd = h * 2 + d_
                    tok = slice(c * 128, (c + 1) * 128)
                    bs, bn, bu = it % 2, 2 + it % 2, 4 + it % 2
                    pt, ktl = PT[it % 4], Kt[it % 4]
                    ptk, ktk = ('m_PT', it % 4), ('m_Kt', it % 4)
                    kb.op('pe', lambda e: e.matmul(PS[bs][:, 0:128], lhsT=qkT[:, 4 + h, tok], rhs=qkT[:, h, tok], start=True, stop=True),
                          r=[('m_qkT', h), ('m_qkT', 4 + h)], w=[('ps', bs)])
                    kb.op('dve', lambda e: e.scalar_tensor_tensor(out=pt[:], in0=PS[bs][:, 0:128], scalar=gd[:, 1, d_, c, h:h + 1],
                                                                  in1=tri_bf[d_], op0=ALU.mult, op1=ALU.mult),
                          r=[('ps', bs), 'm_gd', 'cstb'], w=[ptk])
                    kb.op('act', lambda e: e.activation(out=ktl[:], in_=kTM[:, c, h * 128:(h + 1) * 128], func=AF.Copy, scale=gd[:, 4, d_, c, h:h + 1]),
                          r=[('m_kTM', c), 'm_gd'], w=[ktk])
                    kb.op('pe', lambda e: e.matmul(PS[bn][:, 0:129], lhsT=pt[:], rhs=vaug[:, c, h, 0:129], start=True, stop=False),
                          r=[ptk, ('m_vaug', c), 'm_vaug1'], w=[('ps', bn)])
                    kb.op('pe', lambda e: e.matmul(PS[bn][:, 0:129], lhsT=qkT[:, h, tok], rhs=Cbf[:, hd, 0:129], start=False, stop=True),
                          r=[('m_qkT', h), ('m_Cbf', hd)], w=[('ps', bn)])
                    kb.op('pe', lambda e: e.matmul(PS[bu][:, 0:129], lhsT=ktl[:], rhs=vaug[:, c, h, 0:129], start=True, stop=True),
                          r=[ktk, ('m_vaug', c), 'm_vaug1'], w=[('ps', bu)])
                    smk = ('m_sm', hd)
                    kb.op('act', lambda e: e.activation(out=sm[:, hd, 0:1], in_=PS[bn][:, 128:129], func=AF.Abs), r=[('ps', bn)], w=[smk])

                def scanB(w_):
                    (it, step, h, d_, c) = w_
                    hd = h * 2 + d_
                    bs, bn, bu = it % 2, 2 + it % 2, 4 + it % 2
                    smk = ('m_sm', hd)
                    kb.op('dve', lambda e: e.tensor_tensor(out=sm[:, hd, 1:2], in0=sm[:, hd, 0:1], in1=gd[:, 2, d_, c, h:h + 1], op=ALU.max),
                          r=[smk, 'm_gd'], w=[smk])
                    kb.op('dve', lambda e: e.reciprocal(out=sm[:, hd, 2:3], in_=sm[:, hd, 1:2]), r=[smk], w=[smk])
                    hk = ('m_hm', c, h)
                    hdst = hm[:, c, h * 128:(h + 1) * 128]
                    if (c, h) not in written:
                        written.add((c, h))
                        kb.op('act', lambda e: e.activation(out=hdst, in_=PS[bn][:, 0:128], func=AF.Copy, scale=sm[:, hd, 2:3]),
                              r=[('ps', bn), smk], w=[hk])
                    else:
                        kb.op('dve', lambda e: e.scalar_tensor_tensor(out=hdst, in0=PS[bn][:, 0:128], scalar=sm[:, hd, 2:3], in1=hdst,
                                                                      op0=ALU.mult, op1=ALU.add),
                              r=[('ps', bn), smk], w=[hk])
                    kb.op('dve', lambda e: e.scalar_tensor_tensor(out=Cst[:, hd, 0:129], in0=Cst[:, hd, 0:129], scalar=gd[:, 3, d_, c, h:h + 1],
                                                                  in1=PS[bu][:, 0:129], op0=ALU.mult, op1=ALU.add),
                          r=[('ps', bu), 'm_gd'], w=[('m_C', hd)])
                    kb.op('act', lambda e: e.activation(out=Cbf[:, hd, 0:129], in_=Cst[:, hd, 0:129], func=AF.Copy),
                          r=[('m_C', hd)], w=[('m_Cbf', hd)])

                work = []
                for step in range(16):
                    for h in range(4):
                        for d_ in range(2):
                            work.append((len(work), step, h, d_, step if d_ == 0 else 15 - step))
                prevw = None
                for w_ in work:
                    scanA(w_)
                    if prevw is not None:
                        scanB(prevw)
                    prevw = w_
                scanB(prevw)
                if 'hm' in dbg_out:
                    kb.dma('sp', dbg_out['hm'].rearrange("(c p) f -> p c f", p=128), hm[:], r=[('m_hm', c, h) for c in range(16) for h in range(4)], w=['dbg_hm'])
                for c in range(16):
                    for h in range(4):
                        kb.op('act', lambda e: e.activation(out=junk[:], in_=hm[:, c, h * 128:(h + 1) * 128], func=AF.Square,
                                                            accum_out=ssh[:, c, h:h + 1]), r=[('m_hm', c, h)], w=['m_junk', 'm_ssh'])
                kb.op('act', lambda e: e.activation(out=ssh[:], in_=ssh[:], func=AF.Sqrt, scale=1.0 / 128, bias=epsb[:, 0:1]), r=['m_ssh', 'epsb'], w=['m_ssh'])
                kb.op('dve', lambda e: e.reciprocal(out=ssh[:], in_=ssh[:]), r=['m_ssh'], w=['m_ssh'])
                for c in range(16):
                    for h in range(4):
                        kb.op('dve', lambda e: e.scalar_tensor_tensor(out=mot[:, h * 128:(h + 1) * 128], in0=hm[:, c, h * 128:(h + 1) * 128],
                                                                      scalar=ssh[:, c, h:h + 1], in1=gsig[:, c, h * 128:(h + 1) * 128],
                                                                      op0=ALU.mult, op1=ALU.mult),
                              r=[('m_hm', c, h), 'm_ssh', ('m_gsig', c)], w=['m_mot'])
                    for h in range(4):
                        kb.op('pe', lambda e: e.transpose(PST[:, h * 128:(h + 1) * 128], mot[:, h * 128:(h + 1) * 128], ident_bf),
                              r=['m_mot', 'cstb'], w=['pst'])
                    kb.op('act', lambda e: e.activation(out=catT[:, 4:8, c * 128:(c + 1) * 128], in_=PST[:, 0:512].rearrange("p (h t) -> p h t", t=128), func=AF.Copy),
                          r=['pst'], w=[('catT', 4 + h_) for h_ in range(4)])
            kb.barrier()
            chk('mlstm')
            st.close()

        def attn_phase(st0, l, hnT, catT):
            st = st0.enter_context(ExitStack())
            qR = sb(st, "a_qR", [128, 4, S], BF16)
            kR = sb(st, "a_kR", [128, 4, S], BF16)
            with ExitStack() as s1:
                cosT = sb(s1, "a_cos", [128, S], F32)
                sinT = sb(s1, "a_sin", [128, S], F32)
                Wns = [sb(s1, "a_wn%d" % i, [128, KT, 512], BF16) for i in range(2)]
                Ws = sb(s1, "a_ws", [128, KT, 512], BF16)
                t1s = [sb(s1, "a_t1%d" % i, [128, CH], F32) for i in range(2)]
                t2s = [sb(s1, "a_t2%d" % i, [128, CH], F32) for i in range(2)]
                kb.dma('sp', cosT[:], cos_d, w=['a_cos'])
                kb.dma('sp', sinT[:], sin_d, w=['a_sin'])
                pi = 0
                for qk in range(2):
                    dst = qR if qk == 0 else kR
                    dk = 'a_qR' if qk == 0 else 'a_kR'
                    Wn = Wns[qk]
                    if qk == 0:
                        load_w(Wns[0][:], w_in[l, :, 0:512], ('a_wn', 0))
                        load_w(Wns[1][:], w_in[l, :, 512:1024], ('a_wn', 1))
                    wn5 = Wn[:].rearrange("p k (h two j) -> p k h two j", two=2, j=32)
                    ws5 = Ws[:].rearrange("p k (h two j) -> p k h two j", two=2, j=32)
                    for half in range(2):
                        kb.op('dve', lambda e: e.tensor_copy(out=ws5[:, :, :, half, :], in_=wn5[:, :, :, 1 - half, :]), r=[('a_wn', qk)], w=['a_ws'])
                    for hp in range(4):
                        for c in range(NCH):
                            ba, bb = (pi % 2) * 2, (pi % 2) * 2 + 1
                            t1, t2 = t1s[pi % 2], t2s[pi % 2]
                            t1k, t2k = ('a_t1', pi % 2), ('a_t2', pi % 2)
                            pi += 1
                            for (bk, W, wk) in ((ba, Wn, ('a_wn', qk)), (bb, Ws, 'a_ws')):
                                for k in range(KT):
                                    kb.op('pe', lambda e: e.matmul(PS[bk][:], lhsT=W[:, k, hp * 128:(hp + 1) * 128], rhs=hnT[:, k, c * CH:(c + 1) * CH],
                                                                   start=(k == 0), stop=(k == KT - 1)), r=[wk, ('hnT', c)], w=[('ps', bk)])
                            kb.op('dve', lambda e: e.tensor_tensor(out=t1[:], in0=PS[ba][:], in1=cosT[:, c * CH:(c + 1) * CH], op=ALU.mult),
                                  r=[('ps', ba), 'a_cos'], w=[t1k])
                            kb.op('dve', lambda e: e.tensor_tensor(out=t2[:], in0=PS[bb][:], in1=sinT[:, c * CH:(c + 1) * CH], op=ALU.mult),
                                  r=[('ps', bb), 'a_sin'], w=[t2k])
                            kb.op('dve', lambda e: e.tensor_tensor(out=dst[:, hp, c * CH:(c + 1) * CH], in0=t1[:], in1=t2[:], op=ALU.add),
                                  r=[t1k, t2k], w=[(dk, hp)])
            kb.barrier()
            chk('aproj')
            with ExitStack() as s2:
                vb = [sb(s2, "a_vb%d" % i, [128, 16, 512], BF16) for i in range(3)]
                Wv = sb(s2, "a_wv", [128, KT, 512], BF16)
                numacc = sb(s2, "a_num", [128, S], F32)
                denacc = sb(s2, "a_den", [128, S], F32)
                load_w(Wv[:], w_in[l, :, 1024:1536], 'a_wv')
                DIL = (1, 4, 16)
                pi = 0
                for bi, dil in enumerate(DIL):
                    nb = (S // dil) // 128
                    for ti in range(16):
                        r_, j_ = divmod(ti, nb)
                        t0 = r_ + dil * 128 * j_
                        b = pi % 2
                        pi += 1
                        for k in range(KT):
                            kb.op('pe', lambda e: e.matmul(PS[b][:], lhsT=hnT[:, k, ssl(t0, 128, dil)], rhs=Wv[:, k, :],
                                                           start=(k == 0), stop=(k == KT - 1)),
                                  r=['a_wv'] + [('hnT', c) for c in range(NCH)], w=[('ps', b)])
                        kb.op('act', lambda e: e.activation(out=vb[bi][:, ti, :], in_=PS[b][:], func=AF.Copy), r=[('ps', b)], w=[('a_vb', bi)])
                qc = [sb(s2, "a_qc%d" % i, [128, S], BF16) for i in range(2)]
                kc = [sb(s2, "a_kc%d" % i, [128, S], BF16) for i in range(2)]
                pT3 = [sb(s2, "a_pTb%d" % i, [128, 512], BF16) for i in range(3)]
                state = {'it': 0, 'cc': 0}

                def emit_S(w_):
                    (hp, bi, dil, nb, r_, q0, qn, kts, qsrc, ksrc, qk_, kk_, sub0) = w_['a']
                    it = w_['it']
                    bset = it % 2
                    p_ = pT3[it % 3]
                    pk = ('a_pT', it % 3)
                    nkt = len(kts)
                    wdt = nkt * qn
                    for hh in range(2):
                        base = 64 * hh
                        bank = 2 * bset + hh
                        for i_, (kt, mk) in enumerate(kts):
                            slot = i_ * qn
                            if dil == 1:
                                lhs = ksrc[base:base + 64, hp, 128 * kt:128 * kt + 128]
                                rhs = qsrc[base:base + 64, hp, q0:q0 + qn]
                            else:
                                lhs = ksrc[base:base + 64, sub0 + 128 * kt:sub0 + 128 * kt + 128]
                                rhs = qsrc[base:base + 64, sub0 + q0:sub0 + q0 + qn]
                            kb.op('pe', lambda e: e.matmul(PS[bank][:, slot:slot + qn], lhsT=lhs, rhs=rhs, start=True, stop=True),
                                  r=[kk_, qk_], w=[('ps', bank)], inc=(hh == 1 and i_ == nkt - 1))
                    ncols = 2 * wdt
                    sidx = 3 if kts[0][1] == 'E' else (1 if kts[0][1] == 'F' else (0 if nkt == 2 else 2))
                    kb.op('act', lambda e: e.activation(out=p_[:, 0:ncols].rearrange("p (h w) -> p h w", h=2),
                                                        in_=psall[:, 2 * bset:2 * bset + 2, 0:wdt], func=AF.Exp, scale=0.125),
                          r=[('ps', 2 * bset), ('ps', 2 * bset + 1)], w=[pk])
                    kb.op('dve', lambda e: e.tensor_tensor(out=p_[:, 0:ncols], in0=p_[:, 0:ncols], in1=mstrip[:, sidx, 0:ncols], op=ALU.mult),
                          r=['mstrip'], w=[pk])

                def emit_PV(w_):
                    (hp, bi, dil, nb, r_, q0, qn, kts, qsrc, ksrc, qk_, kk_, sub0) = w_['a']
                    it = w_['it']
                    bnk = {0: 4 + (2 * it) % 3, 128: 4 + (2 * it + 1) % 3}
                    p_ = pT3[it % 3]
                    pk = ('a_pT', it % 3)
                    nkt = len(kts)
                    qsl = ssl(r_ + dil * q0, qn, dil)
                    for (c0, is_num) in ((0, True), (128, False)):
                        for hh in range(2):
                            base = 64 * hh
                            for i_, (kt, mk) in enumerate(kts):
                                slot = (hh * nkt + i_) * qn
                                hcol = (hp * 2 + hh) * 64
                                lhs = vb[bi][:, r_ * nb + kt, hcol:hcol + 64] if is_num else ones_bf[:, 0:64]
                                kb.op('pe', lambda e: e.matmul(PS[bnk[c0]][base:base + 64, 0:qn], lhsT=lhs, rhs=p_[:, slot:slot + qn],
                                                               start=(i_ == 0), stop=(i_ == nkt - 1), tile_position=(0, base)),
                                      r=[pk, ('a_vb', bi), 'cstb'], w=[('ps', bnk[c0])], inc=(hh == 1 and i_ == nkt - 1))
                    for (c0, acc, ak, eng) in ((0, numacc, 'a_num', 'act' if bi == 0 else 'dve'), (128, denacc, 'a_den', 'act' if bi == 0 else 'dve')):
                        if bi == 0:
                            if eng == 'act':
                                kb.op('act', lambda e: e.activation(out=acc[:, qsl], in_=PS[bnk[c0]][:, 0:qn], func=AF.Copy),
                                      r=[('ps', bnk[c0])], w=[(ak, 0)], loose=True)
                            else:
                                kb.op('dve', lambda e: e.tensor_copy(out=acc[:, qsl], in_=PS[bnk[c0]][:, 0:qn]),
                                      r=[('ps', bnk[c0])], w=[(ak, 0)], loose=True)
                        else:
                            kb.op('dve', lambda e: e.tensor_tensor(out=acc[:, qsl], in0=PS[bnk[c0]][:, 0:qn], in1=acc[:, qsl], op=ALU.add),
                                  r=[('ps', bnk[c0]), (ak, bi - 1)], w=[(ak, bi)], loose=True)

                for hp in range(4):
                    items = []
                    for bi, dil in enumerate(DIL):
                        nsub = S // dil
                        nb = nsub // 128
                        if dil == 1:
                            qsrc, ksrc, qk_, kk_ = qR, kR, ('a_qR', hp), ('a_kR', hp)
                        else:
                            cc = state['cc'] % 2
                            state['cc'] += 1
                            qsrc, ksrc, qk_, kk_ = qc[cc], kc[cc], ('a_qc', cc), ('a_kc', cc)
                            kb.op('pool', lambda e: e.tensor_copy(out=qsrc[:].rearrange("p (r i) -> p r i", r=dil),
                                                                  in_=qR[:, hp, :].rearrange("p (i r) -> p r i", r=dil)),
                                  r=[('a_qR', hp)], w=[qk_])
                            kb.op('pool', lambda e: e.tensor_copy(out=ksrc[:].rearrange("p (r i) -> p r i", r=dil),
                                                                  in_=kR[:, hp, :].rearrange("p (i r) -> p r i", r=dil)),
                                  r=[('a_kR', hp)], w=[kk_])
                        if nb == 1:
                            blocks = [(0, 128, [(0, 'E')])]
                        else:
                            blocks = [(0, 64, [(0, 'F')])]
                            blocks += [(64 + 128 * j, 128, [(j, 'A'), (j + 1, 'B')]) for j in range(nb - 1)]
                            blocks += [(nsub - 64, 64, [(nb - 1, 'A')])]
                        for r_ in range(dil):
                            for (q0, qn, kts) in blocks:
                                items.append({'a': (hp, bi, dil, nb, r_, q0, qn, kts, qsrc, ksrc, qk_, kk_, r_ * nsub), 'it': state['it']})
                                state['it'] += 1
                    prev = None
                    for w_ in items:
                        emit_S(w_)
                        if prev is not None:
                            emit_PV(prev)
                        prev = w_
                    emit_PV(prev)
                    nk = [('a_num', b_) for b_ in range(3)]
                    dk_ = [('a_den', b_) for b_ in range(3)]
                    kb.op('dve', lambda e: e.reciprocal(out=denacc[:], in_=denacc[:]), r=dk_, w=dk_)
                    kb.op('dve', lambda e: e.tensor_tensor(out=catT[:, hp, :], in0=numacc[:], in1=denacc[:], op=ALU.mult),
                          r=nk + dk_, w=[('catT', hp)] + nk)
            kb.barrier()
            chk('attn')
            st.close()

        def ffn_phase(st0, l, xsrc, xdst, hnT):
            st = st0.enter_context(ExitStack())
            aT = sb(st, "f_aT", [128, NFT, S], BF16)
            with ExitStack() as s1:
                Wgs = [sb(s1, "f_wg%d" % i, [128, KT, 512], BF16) for i in range(2)]
                Wvs = [sb(s1, "f_wv%d" % i, [128, KT, 512], BF16) for i in range(2)]
                yp = [sb(s1, "f_yp%d" % i, [128, S + 2], F32) for i in range(2)]
                u = [sb(s1, "f_u%d" % i, [128, S], F32) for i in range(2)]
                gl = sb(s1, "f_gl", [128, S], BF16)
                for i in range(2):
                    kb.op('dve', lambda e: e.memset(yp[i][:, 0:1], 0.0), w=[('f_yp', i)])
                    kb.op('dve', lambda e: e.memset(yp[i][:, S + 1:S + 2], 0.0), w=[('f_yp', i)])
                pi = 0
                for c in range(NFT):
                    g4 = c % 4

                    def ldgrp(c_):
                        gi = (c_ // 4) % 2
                        n = min(4, NFT - c_) * 128
                        load_w(Wgs[gi][:, :, 0:n], w_up[l, :, c_ * 128:c_ * 128 + n], ('f_wg', gi))
                        load_w(Wvs[gi][:, :, 0:n], w_up[l, :, DFF + c_ * 128:DFF + c_ * 128 + n], ('f_wv', gi))
                    if c == 0:
                        ldgrp(0)
                    if g4 == 0 and c + 4 < NFT:
                        ldgrp(c + 4)
                    gi_ = (c // 4) % 2
                    Wg, Wv = Wgs[gi_], Wvs[gi_]
                    for part, (W, wk) in enumerate(((Wg, ('f_wg', gi_)), (Wv, ('f_wv', gi_)))):
                        ti = part * NFT + c
                        for ch in range(NCH):
                            b = pi % 4
                            pi += 1
                            for k in range(KT):
                                kb.op('pe', lambda e: e.matmul(PS[b][:], lhsT=W[:, k, g4 * 128:(g4 + 1) * 128], rhs=hnT[:, k, ch * CH:(ch + 1) * CH],
                                                               start=(k == 0), stop=(k == KT - 1)), r=[wk, ('hnT', ch)], w=[('ps', b)])
                            kb.op('act', lambda e: e.activation(out=yp[part][:, 1 + ch * CH:1 + (ch + 1) * CH], in_=PS[b][:], func=AF.Copy),
                                  r=[('ps', b)], w=[('f_yp', part)])
                        kb.op('act', lambda e: e.activation(out=u[part][:], in_=yp[part][:, 0:S], func=AF.Identity,
                                                            scale=fconv[:, l, ti, 0:1], bias=fconv[:, l, ti, 3:4]),
                              r=[('f_yp', part), 'fconv'], w=[('f_u', part)])
                        for jj in (1, 2):
                            kb.op('dve', lambda e: e.scalar_tensor_tensor(out=u[part][:], in0=yp[part][:, jj:jj + S], scalar=fconv[:, l, ti, jj:jj + 1],
                                                                          in1=u[part][:], op0=ALU.mult, op1=ALU.add),
                                  r=[('f_yp', part), 'fconv'], w=[('f_u', part)])
                    kb.op('act', lambda e: e.activation(out=gl[:], in_=u[0][:], func=AF.Gelu_apprx_tanh), r=[('f_u', 0)], w=['f_gl'])
                    kb.op('dve', lambda e: e.tensor_tensor(out=aT[:, c, :], in0=gl[:], in1=u[1][:], op=ALU.mult),
                          r=['f_gl', ('f_u', 1)], w=[('f_aT', c)])
            kb.barrier()
            chk('fup')
            with ExitStack() as s2:
                last = (l == nlayers - 1)
                epilogue(s2, w_down[l], NFT, lambda ci, tsl: aT[:, ci, tsl], lambda ci, c: [('f_aT', ci)], xsrc, xdst, l, 3,
                         che=256, hn_out=None if last else hnT, nxt=None if last else (l + 1, 0))
            kb.barrier()
            st.close()

        xcur = xT_in
        hnT = sb(top, "hnT", [128, KT, S], BF16)
        try:
          for l in range(nlayers if stop_at != 'init' else 0):
              xmid = xs[0]
              xnext = yT if l == nlayers - 1 else xs[1]
              with ExitStack() as sm_:
                  catT = sb(sm_, "catT", [128, KT, S], BF16)
                  with ExitStack() as sh_:
                      if l == 0:
                          with ExitStack() as s0:
                              norm_pre(s0, xcur, l, 0, hnT)
                      kb.barrier()
                      chk('norm')
                      mlstm_phase(sh_, l, hnT, catT)
                      attn_phase(sh_, l, hnT, catT)
                  if 'cat' in dbg_out and l == dbg.get('_layer', 0):
                      with ExitStack() as sd:
                          tmpf = sb(sd, "dbg_tmp", [128, S], F32)
                          for i in range(KT):
                              kb.op('act', lambda e: e.activation(out=tmpf[:], in_=catT[:, i, :], func=AF.Copy), r=[('catT', i)], w=['dbg_tmp'])
                              kb.dma('sp', dbg_out['cat'][i * 128:(i + 1) * 128, :], tmpf[:], r=['dbg_tmp'], w=['dbg_cat'])
                      kb.barrier()
                  with ExitStack() as se:
                      epilogue(se, w_out[l], KT, lambda ci, tsl: catT[:, ci, tsl], lambda ci, c: [('catT', ci)], xcur, xmid, l, 1,
                               che=CH, hn_out=hnT, nxt=(l, 2))
                  kb.barrier()
                  chk('ep1')
              with ExitStack() as sf:
                  ffn_phase(sf, l, xmid, xnext, hnT)
              xcur = xnext
        except _Stop:
            pass
        kb.barrier()
    stuck = kb.check_deadlock()
    if stuck:
        raise RuntimeError('static deadlock: %r' % (stuck,))
    return nc, list(dbg_out.keys())


def _host_prep(inputs):
    f = np.float32
    g = np.stack([inputs['mix_pre_g'], inputs['mix_post_g'], inputs['ffn_pre_g'], inputs['ffn_post_g']], axis=1)
    gvec = np.ascontiguousarray(g.reshape(2, 4, KT, 128).transpose(3, 0, 1, 2)).reshape(128, -1).astype(f)
    mc = np.concatenate([inputs['mlstm_conv_w'], inputs['mlstm_conv_b'][:, None, :]], axis=1)
    mconv = np.ascontiguousarray(mc.reshape(2, 6, 8, 128).transpose(3, 0, 2, 1)).reshape(128, -1).astype(f)
    fc = np.concatenate([inputs['ffn_conv_w'], inputs['ffn_conv_b'][:, None, :]], axis=1)
    fconv = np.ascontiguousarray(fc.reshape(2, 4, 44, 128).transpose(3, 0, 2, 1)).reshape(128, -1).astype(f)
    gateb = np.ascontiguousarray(np.broadcast_to(inputs['mlstm_gate_b'].reshape(1, -1), (128, 32))).astype(f)
    headg = np.ascontiguousarray(np.broadcast_to(inputs['mlstm_head_g'].reshape(1, -1), (128, 1024))).astype(f)
    p = np.arange(128)
    inv_freq = (10000.0 ** (-np.arange(0, 64, 2, dtype=np.float32) / 64)).astype(np.float32)
    ang = np.arange(S, dtype=np.float32)[None, :] * inv_freq[p % 32][:, None]
    cosT = np.cos(ang).astype(f)
    sgn = np.where((p % 64) < 32, -1.0, 1.0).astype(f)[:, None]
    sinT = (np.sin(ang) * sgn).astype(f)
    a = p[:, None]
    b = p[None, :]
    NEG = -30000.0
    cst = np.zeros((128, 9, 128), f)
    cst[:, 0] = 1.0
    cst[:, 1] = (a == b)
    cst[:, 2] = np.where(a >= b, 0.0, NEG)
    cst[:, 3] = np.where(a <= b, 0.0, NEG)
    cst[:, 4] = np.where(a <= b + 64, 0.0, NEG)
    cst[:, 5] = np.where(np.abs(a - b) <= 64, 0.0, NEG)
    cst[:, 6] = (a <= b)
    cst[:, 7] = (a >= b)
    A01 = (a >= b).astype(f); B01 = (a <= b).astype(f); F01 = (a <= b + 64).astype(f); E01 = (np.abs(a - b) <= 64).astype(f)
    ms = np.zeros((128, 4, 512), f)
    ms[:, 0] = np.concatenate([A01, B01, A01, B01], axis=1)
    ms[:, 1, 0:128] = np.concatenate([F01[:, :64], F01[:, :64]], axis=1)
    ms[:, 2, 0:128] = np.concatenate([A01[:, :64], A01[:, :64]], axis=1)
    ms[:, 3, 0:256] = np.concatenate([E01, E01], axis=1)
    shared = dict(mstrip=ms.reshape(128, -1), w_in=np.ascontiguousarray(inputs['w_in'], dtype=f), w_out=np.ascontiguousarray(inputs['w_out'], dtype=f),
                  w_up=np.ascontiguousarray(inputs['w_up'], dtype=f), w_down=np.ascontiguousarray(inputs['w_down'], dtype=f),
                  gvec=gvec, mconv=mconv, fconv=fconv, gateb=gateb, headg=headg, cosT=cosT, sinT=sinT,
                  cst=cst.reshape(128, -1))
    return shared


_NC_CACHE = {}


def kernel(**inputs):
    x = np.asarray(inputs['x'], dtype=np.float32)
    B = x.shape[0]
    shared = _host_prep(inputs)
    if 'nc' not in _NC_CACHE:
        _NC_CACHE['nc'] = build(2)[0]
    nc = _NC_CACHE['nc']
    in_maps = []
    for b in range(B):
        m = dict(shared)
        m['xT'] = np.ascontiguousarray(x[b].T)
        in_maps.append(m)
    res = run_bass_kernel_spmd(nc, in_maps, core_ids=list(range(B)))
    out = np.stack([np.ascontiguousarray(res.results[b]['yT'].T) for b in range(B)], axis=0)
    return out.astype(np.float32)
```

```python
import math
from contextlib import ExitStack
import numpy as np
import concourse.bass as bass
import concourse.mybir as mybir
from concourse.bass_utils import run_bass_kernel_spmd

F32 = mybir.dt.float32
BF16 = mybir.dt.bfloat16
AF = mybir.ActivationFunctionType
ALU = mybir.AluOpType

S = 2048
D = 1024
KT = 8
NCH = 4
CH = 512
DFF = 2816
NFT = 22
EPS = 1e-6
NDS = 24


def ssl(start, n, step):
    return slice(start, start + step * (n - 1) + 1, step)


class KB:
    def __init__(self, nc):
        self.nc = nc
        self.E = {'pe': nc.tensor, 'act': nc.scalar, 'dve': nc.vector, 'pool': nc.gpsimd, 'sp': nc.sync}
        self.sem = {e: nc.alloc_semaphore('c_' + e) for e in ('pe', 'act', 'dve', 'pool')}
        self.cnt = {e: 0 for e in self.sem}
        self.waited = {e: {} for e in self.E}
        self.dsems = [nc.alloc_semaphore('d%d' % i) for i in range(NDS)]
        self.dcnt = [0] * NDS
        self.dlast = [None] * NDS
        self.dnext = 0
        self.dnext_p = 0
        self.tk = {}
        self.dead = False
        self.prog = {e: [] for e in self.E}

    def _t(self, key):
        t = self.tk.get(key)
        if t is None:
            t = {'w': None, 'r': {}}
            self.tk[key] = t
        return t

    def _wait(self, e, ev):
        sem, val, sid = ev
        if self.waited[e].get(sid, 0) >= val:
            return
        self.E[e].wait_ge(sem, val)
        self.prog[e].append(('w', sid, val))
        self.waited[e][sid] = val

    def _deps(self, e, r, w, loose=False):
        evs = {}

        def add(ev):
            if ev is None:
                return
            if ev[2] not in evs or evs[ev[2]][1] < ev[1]:
                evs[ev[2]] = ev
        for k in r:
            add(self._t(k)['w'])
        for k in w:
            t = self._t(k)
            add(t['w'])
            for ev in t['r'].values():
                add(ev)
        for sid, ev in evs.items():
            if sid == e and (e == 'pe' or loose):
                continue
            self._wait(e, ev)

    def _record(self, ev, r, w):
        for k in r:
            self._t(k)['r'][ev[2]] = ev
        for k in w:
            t = self._t(k)
            t['w'] = ev
            t['r'] = {}

    def op(self, e, fn, r=(), w=(), loose=False, inc=True):
        if self.dead:
            return
        self._deps(e, r, w, loose)
        ins = fn(self.E[e])
        if inc:
            self.cnt[e] += 1
            ins.then_inc(self.sem[e], 1)
            self.prog[e].append(('i', e, 1))
            self._record((self.sem[e], self.cnt[e], e), r, w)
        else:
            self._record((self.sem[e], self.cnt[e] + 1, e), r, w)

    def dma(self, q, out, in_, r=(), w=()):
        if self.dead:
            return
        if q == 'pool':
            i = 16 + self.dnext_p
            self.dnext_p = (self.dnext_p + 1) % (NDS - 16)
        else:
            i = self.dnext
            self.dnext = (i + 1) % 16
        if self.dlast[i] is not None:
            self._wait(q, self.dlast[i])
        self._deps(q, r, w)
        ins = self.E[q].dma_start(out=out, in_=in_)
        self.dcnt[i] += 16
        ins.then_inc(self.dsems[i], 16)
        self.prog[q].append(('i', 'd%d' % i, 16))
        ev = (self.dsems[i], self.dcnt[i], 'd%d' % i)
        self.dlast[i] = ev
        self._record(ev, r, w)
        return ev

    def check_deadlock(self):
        pc = {e: 0 for e in self.prog}
        val = {}
        progress = True
        while progress:
            progress = False
            for e, p in self.prog.items():
                while pc[e] < len(p):
                    k, sid, v = p[pc[e]]
                    if k == 'w':
                        if val.get(sid, 0) < v:
                            break
                    else:
                        val[sid] = val.get(sid, 0) + v
                    pc[e] += 1
                    progress = True
        stuck = {e: (pc[e], len(p), p[pc[e]]) for e, p in self.prog.items() if pc[e] < len(p)}
        return stuck

    def barrier(self):
        if self.dead:
            return
        evs = [(self.sem[e], self.cnt[e], e) for e in self.sem if self.cnt[e] > 0]
        evs += [ev for ev in self.dlast if ev is not None]
        for e in self.E:
            for ev in evs:
                if ev[2] == e:
                    continue
                self._wait(e, ev)


class _Stop(Exception):
    pass


def build(nlayers=2, dbg=None, stop_at=None):
    dbg = dbg or {}
    nc = bass.Bass("TRN2", target_bir_lowering=False)
    kb = KB(nc)
    ein = lambda n, s, dt=F32: nc.dram_tensor(n, list(s), dt, kind="ExternalInput").ap()
    xT_in = ein("xT", [D, S])
    w_in = ein("w_in", [2, D, 3600])
    w_out = ein("w_out", [2, D, D])
    w_up = ein("w_up", [2, D, 2 * DFF])
    w_down = ein("w_down", [2, DFF, D])
    gvec_d = ein("gvec", [128, 2 * 4 * KT])
    mconv_d = ein("mconv", [128, 2 * 8 * 6])
    fconv_d = ein("fconv", [128, 2 * 44 * 4])
    gateb_d = ein("gateb", [128, 2 * 16])
    headg_d = ein("headg", [128, 2 * 512])
    cos_d = ein("cosT", [128, S])
    sin_d = ein("sinT", [128, S])
    mstrip_d = ein("mstrip", [128, 4 * 512])
    cst_d = ein("cst", [128, 9 * 128])
    yT = nc.dram_tensor("yT", [D, S], F32, kind="ExternalOutput").ap()
    xs = [nc.dram_tensor("xs%d" % i, [D, S], F32).ap() for i in range(2)]
    dbg_out = {}
    for name, shape in dbg.items():
        dbg_out[name] = nc.dram_tensor("dbg_" + name, list(shape), F32, kind="ExternalOutput").ap()

    pm = lambda ap: ap.rearrange("(k p) t -> p k t", p=128)

    with ExitStack() as top:
        uid = [0]

        def sb(st, name, shape, dt):
            uid[0] += 1
            return st.enter_context(nc.sbuf_tensor("s%d_%s" % (uid[0], name), list(shape), dt))
        gvec = sb(top, "gvec", [128, 2, 4, KT], F32)
        mconv = sb(top, "mconv", [128, 2, 8, 6], F32)
        fconv = sb(top, "fconv", [128, 2, 44, 4], F32)
        gateb = sb(top, "gateb", [128, 2, 16], F32)
        headg = sb(top, "headg", [128, 2, 512], BF16)
        cstb = sb(top, "cstb", [128, 9, 128], BF16)
        epsb = sb(top, "epsb", [128, 1], F32)
        psall = top.enter_context(nc.psum_tensor("psall", [128, 7, 512], F32))
        PS = [psall[:, i, :] for i in range(7)]
        PST = top.enter_context(nc.psum_tensor("pst", [128, 1024], BF16))
        for i in range(7):
            kb._t(('ps', i))
        kb.dma('sp', gvec[:].rearrange("p a b c -> p (a b c)"), gvec_d, w=['gvec'])
        kb.dma('sp', mconv[:].rearrange("p a b c -> p (a b c)"), mconv_d, w=['mconv'])
        kb.dma('sp', fconv[:].rearrange("p a b c -> p (a b c)"), fconv_d, w=['fconv'])
        kb.dma('sp', gateb[:].rearrange("p a b -> p (a b)"), gateb_d, w=['gateb'])
        kb.dma('pool', headg[:].rearrange("p a b -> p (a b)"), headg_d, w=['headg'])
        kb.dma('pool', cstb[:].rearrange("p a b -> p (a b)"), cst_d, w=['cstb'])
        cview = cst_d.rearrange("p (a b) -> p a b", b=128)
        kb.op('dve', lambda e: e.memset(epsb[:], EPS), w=['epsb'])
        ones_bf = cstb[:, 0, :]
        ident_bf = cstb[:, 1, :]
        MASK = {'A': cstb[:, 2, :], 'B': cstb[:, 3, :], 'F': cstb[:, 4, :], 'E': cstb[:, 5, :]}
        tri_bf = [cstb[:, 6, :], cstb[:, 7, :]]

        def chk(name):
            if stop_at == name:
                kb.barrier()
                kb.dead = True

        def dump(name, sb_ap, rkeys):
            if name in dbg_out:
                kb.dma('sp', dbg_out[name], sb_ap, r=rkeys, w=['dbg_' + name])

        def rstd_from_ps(ps_i, rstd_ap, rkey, n):
            kb.op('act', lambda e: e.activation(out=rstd_ap, in_=PS[ps_i][:, 0:rstd_ap.shape[1]], func=AF.Sqrt,
                                                scale=1.0 / n, bias=epsb[:, 0:1]),
                  r=[('ps', ps_i), 'epsb'], w=[rkey])
            kb.op('dve', lambda e: e.reciprocal(out=rstd_ap, in_=rstd_ap), r=[rkey], w=[rkey])

        def norm_pre(st, xsrc, l, j, hnT):
            xcs = [sb(st, "np_xc%d" % i, [128, KT, CH], F32) for i in range(2)]
            sq = sb(st, "np_sq", [128, KT, CH], BF16)
            rstd = sb(st, "np_rstd", [128, CH], F32)
            for c in range(NCH):
                xc = xcs[c % 2]
                xk = ('np_xc', c % 2)
                kb.dma('sp', xc[:], pm(xsrc)[:, :, c * CH:(c + 1) * CH], w=[xk])
                kb.op('act', lambda e: e.activation(out=sq[:], in_=xc[:], func=AF.Square), r=[xk], w=['np_sq'])
                for k in range(KT):
                    kb.op('pe', lambda e: e.matmul(PS[6][:], lhsT=ones_bf, rhs=sq[:, k, :], start=(k == 0), stop=(k == KT - 1)),
                          r=['np_sq', 'cstb'], w=[('ps', 6)])
                rstd_from_ps(6, rstd[:], 'np_rstd', D)
                for k in range(KT):
                    kb.op('dve', lambda e: e.scalar_tensor_tensor(out=hnT[:, k, c * CH:(c + 1) * CH], in0=xc[:, k, :],
                                                                  scalar=gvec[:, l, j, k:k + 1], in1=rstd[:],
                                                                  op0=ALU.mult, op1=ALU.mult),
                          r=[xk, 'np_rstd', 'gvec'], w=[('hnT', c)])

        def load_w(dst, src_rows_cols, wkey):
            kb.dma('pool', dst, src_rows_cols.rearrange("(k p) c -> p k c", p=128), w=[wkey])

        def epilogue(st, Wd, nct, rhs_fn, rkeys_fn, xsrc, xdst, l, j, che=CH, hn_out=None, nxt=None):
            W = sb(st, "ep_w", [128, nct, D], BF16)
            xcs = [sb(st, "ep_xc%d" % i, [128, KT, che], F32) for i in range(2)]
            ff = sb(st, "ep_ff", [128, KT, che], F32)
            sq = sb(st, "ep_sq", [128, KT, che], BF16)
            rstd = sb(st, "ep_rstd", [128, che], F32)
            if hn_out is not None:
                sq2 = sb(st, "ep_sq2", [128, KT, che], BF16)
                rstd2 = sb(st, "ep_rstd2", [128, che], F32)
            for hf in range(2):
                for k0 in range(0, nct, 8):
                    k1 = min(nct, k0 + 8)
                    load_w(W[:, k0:k1, hf * 512:(hf + 1) * 512], Wd[k0 * 128:k1 * 128, hf * 512:(hf + 1) * 512], ('ep_w', hf))
            state = {'pi': 0}
            nchunk = S // che

            def partA(c):
                tsl = slice(c * che, (c + 1) * che)
                xc = xcs[c % 2]
                xk = ('ep_xc', c % 2)
                kb.dma('sp', xc[:], pm(xsrc)[:, :, tsl], w=[xk])
                for m in range(KT):
                    b = state['pi'] % 4
                    state['pi'] += 1
                    for ci in range(nct):
                        kb.op('pe', lambda e: e.matmul(PS[b][:, 0:che], lhsT=W[:, ci, m * 128:(m + 1) * 128], rhs=rhs_fn(ci, tsl),
                                                       start=(ci == 0), stop=(ci == nct - 1)),
                              r=[('ep_w', m // 4)] + rkeys_fn(ci, c), w=[('ps', b)], inc=(ci == nct - 1))
                    kb.op('act', lambda e: e.activation(out=ff[:, m, :], in_=PS[b][:, 0:che], func=AF.Copy), r=[('ps', b)], w=[('ep_ff', m)])
                    kb.op('act', lambda e: e.activation(out=sq[:, m, :], in_=ff[:, m, :], func=AF.Square), r=[('ep_ff', m)], w=[('ep_sq', m)])

            def partB(c):
                tsl = slice(c * che, (c + 1) * che)
                xc = xcs[c % 2]
                xk = ('ep_xc', c % 2)
                for m in range(KT):
                    kb.op('pe', lambda e: e.matmul(PS[6][:, 0:che], lhsT=ones_bf, rhs=sq[:, m, :], start=(m == 0), stop=(m == KT - 1)),
                          r=[('ep_sq', m), 'cstb'], w=[('ps', 6)], inc=(m == KT - 1))
                rstd_from_ps(6, rstd[:], 'ep_rstd', D)
                for m in range(KT):
                    kb.op('dve', lambda e: e.scalar_tensor_tensor(out=ff[:, m, :], in0=ff[:, m, :], scalar=gvec[:, l, j, m:m + 1],
                                                                  in1=rstd[:], op0=ALU.mult, op1=ALU.mult),
                          r=['ep_rstd', 'gvec'], w=[('ep_ff', m)])
                    kb.op('dve', lambda e: e.tensor_tensor(out=xc[:, m, :], in0=xc[:, m, :], in1=ff[:, m, :], op=ALU.add),
                          r=[('ep_ff', m)], w=[xk])
                kb.dma('sp', pm(xdst)[:, :, tsl], xc[:], r=[xk], w=[('xdst', c)])

            def partC(c):
                if hn_out is None:
                    return
                tsl = slice(c * che, (c + 1) * che)
                xc = xcs[c % 2]
                xk = ('ep_xc', c % 2)
                l2, j2 = nxt
                kb.op('act', lambda e: e.activation(out=sq2[:], in_=xc[:], func=AF.Square), r=[xk], w=['ep_sq2'])
                for m in range(KT):
                    kb.op('pe', lambda e: e.matmul(PS[5][:, 0:che], lhsT=ones_bf, rhs=sq2[:, m, :], start=(m == 0), stop=(m == KT - 1)),
                          r=['ep_sq2', 'cstb'], w=[('ps', 5)], inc=(m == KT - 1))
                rstd_from_ps(5, rstd2[:], 'ep_rstd2', D)
                for m in range(KT):
                    kb.op('dve', lambda e: e.scalar_tensor_tensor(out=hn_out[:, m, tsl], in0=xc[:, m, :], scalar=gvec[:, l2, j2, m:m + 1],
                                                                  in1=rstd2[:], op0=ALU.mult, op1=ALU.mult),
                          r=[xk, 'ep_rstd2', 'gvec'], w=[('hnT', (c * che) // CH)])

            partA(0)
            for c in range(nchunk):
                partB(c)
                if c + 1 < nchunk:
                    partA(c + 1)
                partC(c)

        def mlstm_phase(st0, l, hnT, catT):
            st = st0.enter_context(ExitStack())
            qkT = sb(st, "m_qkT", [128, 8, S], BF16)
            kTM = sb(st, "m_kTM", [128, 16, 512], BF16)
            vaug = sb(st, "m_vaug", [128, 16, 4, 132], BF16)
            gsig = sb(st, "m_gsig", [128, 16, 512], BF16)
            gates = sb(st, "m_gates", [128, 16, 16], F32)
            l1 = sb(st, "m_l1", [128, 16, 8], F32)
            gd = sb(st, "m_gd", [128, 5, 2, 16, 4], F32)
            with ExitStack() as sp_:
                Wa = sb(sp_, "m_wa", [128, KT, 512], BF16)
                Wb = sb(sp_, "m_wb", [128, KT, 512], BF16)
                Wg = sb(sp_, "m_wg", [128, KT, 16], BF16)
                ypads = [sb(sp_, "m_ypad%d" % i, [128, S + 4], BF16) for i in range(2)]
                uaccs = [sb(sp_, "m_uacc%d" % i, [128, S], F32) for i in range(2)]
                sgt = sb(sp_, "m_sgt", [128, 512], BF16)
                load_w(Wa[:], w_in[l, :, 1536:2048], 'm_wa')
                load_w(Wb[:], w_in[l, :, 2048:2560], 'm_wb')
                load_w(Wg[:], w_in[l, :, 3584:3600], 'm_wg')
                for i in range(2):
                    kb.op('dve', lambda e: e.memset(ypads[i][:, 0:2], 0.0), w=[('m_ypad', i)])
                    kb.op('dve', lambda e: e.memset(ypads[i][:, S + 2:S + 4], 0.0), w=[('m_ypad', i)])
                kb.op('dve', lambda e: e.memset(vaug[:, :, :, 128:129], 1.0), w=['m_vaug1'])
                pi = 0
                for i in range(8):
                    W = Wa if i < 4 else Wb
                    wk = 'm_wa' if i < 4 else 'm_wb'
                    cs = (i % 4) * 128
                    ypad, uacc = ypads[i % 2], uaccs[i % 2]
                    yk, uk = ('m_ypad', i % 2), ('m_uacc', i % 2)
                    for c in range(NCH):
                        b = pi % 4
                        pi += 1
                        for k in range(KT):
                            kb.op('pe', lambda e: e.matmul(PS[b][:], lhsT=W[:, k, cs:cs + 128], rhs=hnT[:, k, c * CH:(c + 1) * CH],
                                                           start=(k == 0), stop=(k == KT - 1)),
                                  r=[wk, ('hnT', c)], w=[('ps', b)])
                        kb.op('act', lambda e: e.activation(out=ypad[:, 2 + c * CH:2 + (c + 1) * CH], in_=PS[b][:], func=AF.Copy),
                              r=[('ps', b)], w=[yk])
                    kb.op('act', lambda e: e.activation(out=uacc[:], in_=ypad[:, 0:S], func=AF.Identity,
                                                        scale=mconv[:, l, i, 0:1], bias=mconv[:, l, i, 5:6]),
                          r=[yk, 'mconv'], w=[uk])
                    for jj in range(1, 5):
                        kb.op('dve', lambda e: e.scalar_tensor_tensor(out=uacc[:], in0=ypad[:, jj:jj + S], scalar=mconv[:, l, i, jj:jj + 1],
                                                                      in1=uacc[:], op0=ALU.mult, op1=ALU.add),
                              r=[yk, 'mconv'], w=[uk])
                    kb.op('act', lambda e: e.activation(out=qkT[:, i, :], in_=uacc[:], func=AF.Silu), r=[uk], w=[('m_qkT', i)])
                if 'qk' in dbg_out:
                    for i in range(8):
                        kb.op('act', lambda e: e.activation(out=uaccs[0][:], in_=qkT[:, i, :], func=AF.Copy), r=[('m_qkT', i)], w=[('m_uacc', 0)])
                        dump_ap = dbg_out['qk'][i * 128:(i + 1) * 128, :]
                        kb.dma('sp', dump_ap, uaccs[0][:], r=[('m_uacc', 0)], w=['dbg_qk'])
                for c in range(16):
                    for h in range(4):
                        kb.op('pe', lambda e: e.transpose(PST[:, h * 128:(h + 1) * 128], qkT[:, 4 + h, c * 128:(c + 1) * 128], ident_bf),
                              r=[('m_qkT', 4 + h), 'cstb'], w=['pst'])
                    kb.op('act', lambda e: e.activation(out=kTM[:, c, :], in_=PST[:, 0:512], func=AF.Copy), r=['pst'], w=[('m_kTM', c)])
                load_w(Wa[:], w_in[l, :, 2560:3072], 'm_wa')
                load_w(Wb[:], w_in[l, :, 3072:3584], 'm_wb')
                for c in range(16):
                    tk = ('hnT', c // 4)
                    for k in range(KT):
                        kb.op('pe', lambda e: e.matmul(PS[0][:], lhsT=hnT[:, k, c * 128:(c + 1) * 128], rhs=Wa[:, k, :],
                                                       start=(k == 0), stop=(k == KT - 1)), r=['m_wa', tk], w=[('ps', 0)])
                    kb.op('act', lambda e: e.activation(out=vaug[:, c, :, 0:128], in_=PS[0][:].rearrange("p (h f) -> p h f", f=128), func=AF.Copy),
                          r=[('ps', 0)], w=[('m_vaug', c)])
                    for k in range(KT):
                        kb.op('pe', lambda e: e.matmul(PS[1][:], lhsT=hnT[:, k, c * 128:(c + 1) * 128], rhs=Wb[:, k, :],
                                                       start=(k == 0), stop=(k == KT - 1)), r=['m_wb', tk], w=[('ps', 1)])
                    kb.op('act', lambda e: e.activation(out=sgt[:], in_=PS[1][:], func=AF.Sigmoid), r=[('ps', 1)], w=['m_sgt'])
                    kb.op('dve', lambda e: e.tensor_tensor(out=gsig[:, c, :], in0=sgt[:], in1=headg[:, l, :], op=ALU.mult),
                          r=['m_sgt', 'headg'], w=[('m_gsig', c)])
                    for k in range(KT):
                        kb.op('pe', lambda e: e.matmul(PS[2][:, 0:16], lhsT=hnT[:, k, c * 128:(c + 1) * 128], rhs=Wg[:, k, :],
                                                       start=(k == 0), stop=(k == KT - 1)), r=['m_wg', tk], w=[('ps', 2)])
                    kb.op('dve', lambda e: e.tensor_tensor(out=gates[:, c, :], in0=PS[2][:, 0:16], in1=gateb[:, l, :], op=ALU.add),
                          r=[('ps', 2), 'gateb'], w=['m_gates'])
                for d_ in range(2):
                    kb.op('act', lambda e: e.activation(out=l1[:, :, 4 * d_:4 * d_ + 4], in_=gates[:, :, 8 * d_ + 4:8 * d_ + 8], func=AF.Exp, scale=-1.0),
                          r=['m_gates'], w=['m_l1'])
                kb.op('act', lambda e: e.activation(out=l1[:], in_=l1[:], func=AF.Ln, bias=1.0), r=['m_l1'], w=['m_l1'])
                l1h = sb(sp_, "m_l1h", [128, 2, 16, 8], BF16)
                l1r = sb(sp_, "m_l1r", [128, 16, 8], F32)
                kb.op('dve', lambda e: e.tensor_copy(out=l1h[:, 0], in_=l1[:]), r=['m_l1'], w=['m_l1h'])
                kb.op('dve', lambda e: e.tensor_copy(out=l1r[:], in_=l1h[:, 0]), r=['m_l1h'], w=['m_l1r'])
                kb.op('dve', lambda e: e.tensor_tensor(out=l1r[:], in0=l1[:], in1=l1r[:], op=ALU.subtract), r=['m_l1', 'm_l1r'], w=['m_l1r'])
                kb.op('dve', lambda e: e.tensor_copy(out=l1h[:, 1], in_=l1r[:]), r=['m_l1r'], w=['m_l1h'])
                for d_ in range(2):
                    for (bank, lhs) in ((3, tri_bf[d_]), (4, ones_bf)):
                        for hl in range(2):
                            kb.op('pe', lambda e: e.matmul(PS[bank][:, 64 * d_:64 * d_ + 64].rearrange("p (c h) -> p c h", h=4), lhsT=lhs,
                                                           rhs=l1h[:, hl, :, 4 * d_:4 * d_ + 4], start=(hl == 0), stop=(hl == 1)),
                                  r=['m_l1h', 'cstb'], w=[('ps', bank)])
                    na = gd[:, 0, d_]
                    kb.op('act', lambda e: e.activation(out=na, in_=PS[3][:, 64 * d_:64 * d_ + 64].rearrange("p (c h) -> p c h", h=4), func=AF.Copy),
                          r=[('ps', 3)], w=['m_gd'])
                    kb.op('dve', lambda e: e.tensor_tensor(out=gd[:, 1, d_], in0=gates[:, :, 8 * d_:8 * d_ + 4], in1=na, op=ALU.add),
                          r=['m_gates', 'm_gd'], w=['m_gd'])
                    kb.op('act', lambda e: e.activation(out=gd[:, 1, d_], in_=gd[:, 1, d_], func=AF.Exp, bias=-0.5 * math.log(128.0)),
                          r=['m_gd'], w=['m_gd'])
                    kb.op('act', lambda e: e.activation(out=gd[:, 2, d_], in_=na, func=AF.Exp), r=['m_gd'], w=['m_gd'])
                    kb.op('act', lambda e: e.activation(out=gd[:, 3, d_], in_=PS[4][:, 64 * d_:64 * d_ + 64].rearrange("p (c h) -> p c h", h=4),
                                                        func=AF.Exp, scale=-1.0), r=[('ps', 4)], w=['m_gd'])
                    kb.op('dve', lambda e: e.tensor_tensor(out=gd[:, 4, d_], in0=gd[:, 1, d_], in1=gd[:, 3, d_], op=ALU.mult),
                          r=['m_gd'], w=['m_gd'])
            kb.barrier()
            chk('mproj')
            with ExitStack() as ss_:
                hm = sb(ss_, "m_hm", [128, 16, 512], F32)
                Cst = sb(ss_, "m_C", [128, 8, 132], F32)
                Cbf = sb(ss_, "m_Cbf", [128, 8, 132], BF16)
                PT = [sb(ss_, "m_PT%d" % i, [128, 128], BF16) for i in range(4)]
                Kt = [sb(ss_, "m_Kt%d" % i, [128, 128], BF16) for i in range(4)]
                sm = sb(ss_, "m_sm", [128, 8, 4], F32)
                ssh = sb(ss_, "m_ssh", [128, 16, 4], F32)
                junk = sb(ss_, "m_junk", [128, 128], BF16)
                mot = sb(ss_, "m_mot", [128, 512], BF16)
                kb.op('dve', lambda e: e.memset(Cst[:], 0.0), w=[('m_C', i) for i in range(8)])
                kb.op('dve', lambda e: e.memset(Cbf[:], 0.0), w=[('m_Cbf', i) for i in range(8)])
                written = set()

                def scanA(w_):
                    (it, step, h, d_, c) = w_
                    hd = h * 2 + d_
                    tok = slice(c * 128, (c + 1) * 128)
                    bs, bn, bu = it % 2, 2 + it % 2, 4 + it % 2
                    pt, ktl = PT[it % 4], Kt[it % 4]
                    ptk, ktk = ('m_PT', it % 4), ('m_Kt', it % 4)
                    kb.op('pe', lambda e: e.matmul(PS[bs][:, 0:128], lhsT=qkT[:, 4 + h, tok], rhs=qkT[:, h, tok], start=True, stop=True),
                          r=[('m_qkT', h), ('m_qkT', 4 + h)], w=[('ps', bs)])
                    kb.op('dve', lambda e: e.scalar_tensor_tensor(out=pt[:], in0=PS[bs][:, 0:128], scalar=gd[:, 1, d_, c, h:h + 1],
                                                                  in1=tri_bf[d_], op0=ALU.mult, op1=ALU.mult),
                          r=[('ps', bs), 'm_gd', 'cstb'], w=[ptk])
                    kb.op('act', lambda e: e.activation(out=ktl[:], in_=kTM[:, c, h * 128:(h + 1) * 128], func=AF.Copy, scale=gd[:, 4, d_, c, h:h + 1]),
                          r=[('m_kTM', c), 'm_gd'], w=[ktk])
                    kb.op('pe', lambda e: e.matmul(PS[bn][:, 0:129], lhsT=pt[:], rhs=vaug[:, c, h, 0:129], start=True, stop=False),
                          r=[ptk, ('m_vaug', c), 'm_vaug1'], w=[('ps', bn)])
                    kb.op('pe', lambda e: e.matmul(PS[bn][:, 0:129], lhsT=qkT[:, h, tok], rhs=Cbf[:, hd, 0:129], start=False, stop=True),
                          r=[('m_qkT', h), ('m_Cbf', hd)], w=[('ps', bn)])
                    kb.op('pe', lambda e: e.matmul(PS[bu][:, 0:129], lhsT=ktl[:], rhs=vaug[:, c, h, 0:129], start=True, stop=True),
                          r=[ktk, ('m_vaug', c), 'm_vaug1'], w=[('ps', bu)])
                    smk = ('m_sm', hd)
                    kb.op('act', lambda e: e.activation(out=sm[:, hd, 0:1], in_=PS[bn][:, 128:129], func=AF.Abs), r=[('ps', bn)], w=[smk])

                def scanB(w_):
                    (it, step, h, d_, c) = w_
                    hd = h * 2 + d_
                    bs, bn, bu = it % 2, 2 + it % 2, 4 + it % 2
                    smk = ('m_sm', hd)
                    kb.op('dve', lambda e: e.tensor_tensor(out=sm[:, hd, 1:2], in0=sm[:, hd, 0:1], in1=gd[:, 2, d_, c, h:h + 1], op=ALU.max),
                          r=[smk, 'm_gd'], w=[smk])
                    kb.op('dve', lambda e: e.reciprocal(out=sm[:, hd, 2:3], in_=sm[:, hd, 1:2]), r=[smk], w=[smk])
                    hk = ('m_hm', c, h)
                    hdst = hm[:, c, h * 128:(h + 1) * 128]
                    if (c, h) not in written:
                        written.add((c, h))
                        kb.op('act', lambda e: e.activation(out=hdst, in_=PS[bn][:, 0:128], func=AF.Copy, scale=sm[:, hd, 2:3]),
                              r=[('ps', bn), smk], w=[hk])
                    else:
                        kb.op('dve', lambda e: e.scalar_tensor_tensor(out=hdst, in0=PS[bn][:, 0:128], scalar=sm[:, hd, 2:3], in1=hdst,
                                                                      op0=ALU.mult, op1=ALU.add),
                              r=[('ps', bn), smk], w=[hk])
                    kb.op('dve', lambda e: e.scalar_tensor_tensor(out=Cst[:, hd, 0:129], in0=Cst[:, hd, 0:129], scalar=gd[:, 3, d_, c, h:h + 1],
                                                                  in1=PS[bu][:, 0:129], op0=ALU.mult, op1=ALU.add),
                          r=[('ps', bu), 'm_gd'], w=[('m_C', hd)])
                    kb.op('act', lambda e: e.activation(out=Cbf[:, hd, 0:129], in_=Cst[:, hd, 0:129], func=AF.Copy),
                          r=[('m_C', hd)], w=[('m_Cbf', hd)])

                work = []
                for step in range(16):
                    for h in range(4):
                        for d_ in range(2):
                            work.append((len(work), step, h, d_, step if d_ == 0 else 15 - step))
                prevw = None
                for w_ in work:
                    scanA(w_)
                    if prevw is not None:
                        scanB(prevw)
                    prevw = w_
                scanB(prevw)
                if 'hm' in dbg_out:
                    kb.dma('sp', dbg_out['hm'].rearrange("(c p) f -> p c f", p=128), hm[:], r=[('m_hm', c, h) for c in range(16) for h in range(4)], w=['dbg_hm'])
                for c in range(16):
                    for h in range(4):
                        kb.op('act', lambda e: e.activation(out=junk[:], in_=hm[:, c, h * 128:(h + 1) * 128], func=AF.Square,
                                                            accum_out=ssh[:, c, h:h + 1]), r=[('m_hm', c, h)], w=['m_junk', 'm_ssh'])
                kb.op('act', lambda e: e.activation(out=ssh[:], in_=ssh[:], func=AF.Sqrt, scale=1.0 / 128, bias=epsb[:, 0:1]), r=['m_ssh', 'epsb'], w=['m_ssh'])
                kb.op('dve', lambda e: e.reciprocal(out=ssh[:], in_=ssh[:]), r=['m_ssh'], w=['m_ssh'])
                for c in range(16):
                    for h in range(4):
                        kb.op('dve', lambda e: e.scalar_tensor_tensor(out=mot[:, h * 128:(h + 1) * 128], in0=hm[:, c, h * 128:(h + 1) * 128],
                                                                      scalar=ssh[:, c, h:h + 1], in1=gsig[:, c, h * 128:(h + 1) * 128],
                                                                      op0=ALU.mult, op1=ALU.mult),
                              r=[('m_hm', c, h), 'm_ssh', ('m_gsig', c)], w=['m_mot'])
                    for h in range(4):
                        kb.op('pe', lambda e: e.transpose(PST[:, h * 128:(h + 1) * 128], mot[:, h * 128:(h + 1) * 128], ident_bf),
                              r=['m_mot', 'cstb'], w=['pst'])
                    kb.op('act', lambda e: e.activation(out=catT[:, 4:8, c * 128:(c + 1) * 128], in_=PST[:, 0:512].rearrange("p (h t) -> p h t", t=128), func=AF.Copy),
                          r=['pst'], w=[('catT', 4 + h_) for h_ in range(4)])
            kb.barrier()
            chk('mlstm')
            st.close()

        def attn_phase(st0, l, hnT, catT):
            st = st0.enter_context(ExitStack())
            qR = sb(st, "a_qR", [128, 4, S], BF16)
            kR = sb(st, "a_kR", [128, 4, S], BF16)
            with ExitStack() as s1:
                cosT = sb(s1, "a_cos", [128, S], F32)
                sinT = sb(s1, "a_sin", [128, S], F32)
                Wns = [sb(s1, "a_wn%d" % i, [128, KT, 512], BF16) for i in range(2)]
                Ws = sb(s1, "a_ws", [128, KT, 512], BF16)
                t1s = [sb(s1, "a_t1%d" % i, [128, CH], F32) for i in range(2)]
                t2s = [sb(s1, "a_t2%d" % i, [128, CH], F32) for i in range(2)]
                kb.dma('sp', cosT[:], cos_d, w=['a_cos'])
                kb.dma('sp', sinT[:], sin_d, w=['a_sin'])
                pi = 0
                for qk in range(2):
                    dst = qR if qk == 0 else kR
                    dk = 'a_qR' if qk == 0 else 'a_kR'
                    Wn = Wns[qk]
                    if qk == 0:
                        load_w(Wns[0][:], w_in[l, :, 0:512], ('a_wn', 0))
                        load_w(Wns[1][:], w_in[l, :, 512:1024], ('a_wn', 1))
                    wn5 = Wn[:].rearrange("p k (h two j) -> p k h two j", two=2, j=32)
                    ws5 = Ws[:].rearrange("p k (h two j) -> p k h two j", two=2, j=32)
                    for half in range(2):
                        kb.op('dve', lambda e: e.tensor_copy(out=ws5[:, :, :, half, :], in_=wn5[:, :, :, 1 - half, :]), r=[('a_wn', qk)], w=['a_ws'])
                    for hp in range(4):
                        for c in range(NCH):
                            ba, bb = (pi % 2) * 2, (pi % 2) * 2 + 1
                            t1, t2 = t1s[pi % 2], t2s[pi % 2]
                            t1k, t2k = ('a_t1', pi % 2), ('a_t2', pi % 2)
                            pi += 1
                            for (bk, W, wk) in ((ba, Wn, ('a_wn', qk)), (bb, Ws, 'a_ws')):
                                for k in range(KT):
                                    kb.op('pe', lambda e: e.matmul(PS[bk][:], lhsT=W[:, k, hp * 128:(hp + 1) * 128], rhs=hnT[:, k, c * CH:(c + 1) * CH],
                                                                   start=(k == 0), stop=(k == KT - 1)), r=[wk, ('hnT', c)], w=[('ps', bk)])
                            kb.op('dve', lambda e: e.tensor_tensor(out=t1[:], in0=PS[ba][:], in1=cosT[:, c * CH:(c + 1) * CH], op=ALU.mult),
                                  r=[('ps', ba), 'a_cos'], w=[t1k])
                            kb.op('dve', lambda e: e.tensor_tensor(out=t2[:], in0=PS[bb][:], in1=sinT[:, c * CH:(c + 1) * CH], op=ALU.mult),
                                  r=[('ps', bb), 'a_sin'], w=[t2k])
                            kb.op('dve', lambda e: e.tensor_tensor(out=dst[:, hp, c * CH:(c + 1) * CH], in0=t1[:], in1=t2[:], op=ALU.add),
                                  r=[t1k, t2k], w=[(dk, hp)])
            kb.barrier()
            chk('aproj')
            with ExitStack() as s2:
                vb = [sb(s2, "a_vb%d" % i, [128, 16, 512], BF16) for i in range(3)]
                Wv = sb(s2, "a_wv", [128, KT, 512], BF16)
                numacc = sb(s2, "a_num", [128, S], F32)
                denacc = sb(s2, "a_den", [128, S], F32)
                load_w(Wv[:], w_in[l, :, 1024:1536], 'a_wv')
                mstrip = sb(s2, "a_mstrip", [128, 4, 512], BF16)
                kb.dma('pool', mstrip[:].rearrange("p a b -> p (a b)"), mstrip_d, w=['mstrip'])
                DIL = (1, 4, 16)
                pi = 0
                for bi, dil in enumerate(DIL):
                    nb = (S // dil) // 128
                    for ti in range(16):
                        r_, j_ = divmod(ti, nb)
                        t0 = r_ + dil * 128 * j_
                        b = pi % 2
                        pi += 1
                        for k in range(KT):
                            kb.op('pe', lambda e: e.matmul(PS[b][:], lhsT=hnT[:, k, ssl(t0, 128, dil)], rhs=Wv[:, k, :],
                                                           start=(k == 0), stop=(k == KT - 1)),
                                  r=['a_wv'] + [('hnT', c) for c in range(NCH)], w=[('ps', b)])
                        kb.op('act', lambda e: e.activation(out=vb[bi][:, ti, :], in_=PS[b][:], func=AF.Copy), r=[('ps', b)], w=[('a_vb', bi)])
                qc = [sb(s2, "a_qc%d" % i, [128, S], BF16) for i in range(2)]
                kc = [sb(s2, "a_kc%d" % i, [128, S], BF16) for i in range(2)]
                pT3 = [sb(s2, "a_pTb%d" % i, [128, 512], BF16) for i in range(3)]
                state = {'it': 0, 'cc': 0}

                def emit_S(w_):
                    (hp, bi, dil, nb, r_, q0, qn, kts, qsrc, ksrc, qk_, kk_, sub0) = w_['a']
                    it = w_['it']
                    bset = it % 2
                    p_ = pT3[it % 3]
                    pk = ('a_pT', it % 3)
                    nkt = len(kts)
                    wdt = nkt * qn
                    for hh in range(2):
                        base = 64 * hh
                        bank = 2 * bset + hh
                        for i_, (kt, mk) in enumerate(kts):
                            slot = i_ * qn
                            if dil == 1:
                                lhs = ksrc[base:base + 64, hp, 128 * kt:128 * kt + 128]
                                rhs = qsrc[base:base + 64, hp, q0:q0 + qn]
                            else:
                                lhs = ksrc[base:base + 64, sub0 + 128 * kt:sub0 + 128 * kt + 128]
                                rhs = qsrc[base:base + 64, sub0 + q0:sub0 + q0 + qn]
                            kb.op('pe', lambda e: e.matmul(PS[bank][:, slot:slot + qn], lhsT=lhs, rhs=rhs, start=True, stop=True),
                                  r=[kk_, qk_], w=[('ps', bank)], inc=(hh == 1 and i_ == nkt - 1))
                    ncols = 2 * wdt
                    sidx = 3 if kts[0][1] == 'E' else (1 if kts[0][1] == 'F' else (0 if nkt == 2 else 2))
                    kb.op('act', lambda e: e.activation(out=p_[:, 0:ncols].rearrange("p (h w) -> p h w", h=2),
                                                        in_=psall[:, 2 * bset:2 * bset + 2, 0:wdt], func=AF.Exp, scale=0.125),
                          r=[('ps', 2 * bset), ('ps', 2 * bset + 1)], w=[pk])
                    kb.op('dve', lambda e: e.tensor_tensor(out=p_[:, 0:ncols], in0=p_[:, 0:ncols], in1=mstrip[:, sidx, 0:ncols], op=ALU.mult),
                          r=['mstrip'], w=[pk])

                def emit_PV(w_):
                    (hp, bi, dil, nb, r_, q0, qn, kts, qsrc, ksrc, qk_, kk_, sub0) = w_['a']
                    it = w_['it']
                    bnk = {0: 4 + (2 * it) % 3, 128: 4 + (2 * it + 1) % 3}
                    p_ = pT3[it % 3]
                    pk = ('a_pT', it % 3)
                    nkt = len(kts)
                    qsl = ssl(r_ + dil * q0, qn, dil)
                    for (c0, is_num) in ((0, True), (128, False)):
                        for hh in range(2):
                            base = 64 * hh
                            for i_, (kt, mk) in enumerate(kts):
                                slot = (hh * nkt + i_) * qn
                                hcol = (hp * 2 + hh) * 64
                                lhs = vb[bi][:, r_ * nb + kt, hcol:hcol + 64] if is_num else ones_bf[:, 0:64]
                                kb.op('pe', lambda e: e.matmul(PS[bnk[c0]][base:base + 64, 0:qn], lhsT=lhs, rhs=p_[:, slot:slot + qn],
                                                               start=(i_ == 0), stop=(i_ == nkt - 1), tile_position=(0, base)),
                                      r=[pk, ('a_vb', bi), 'cstb'], w=[('ps', bnk[c0])], inc=(hh == 1 and i_ == nkt - 1))
                    for (c0, acc, ak, eng) in ((0, numacc, 'a_num', 'act' if bi == 0 else 'dve'), (128, denacc, 'a_den', 'act' if bi == 0 else 'dve')):
                        if bi == 0:
                            if eng == 'act':
                                kb.op('act', lambda e: e.activation(out=acc[:, qsl], in_=PS[bnk[c0]][:, 0:qn], func=AF.Copy),
                                      r=[('ps', bnk[c0])], w=[(ak, 0)], loose=True)
                            else:
                                kb.op('dve', lambda e: e.tensor_copy(out=acc[:, qsl], in_=PS[bnk[c0]][:, 0:qn]),
                                      r=[('ps', bnk[c0])], w=[(ak, 0)], loose=True)
                        else:
                            kb.op('dve', lambda e: e.tensor_tensor(out=acc[:, qsl], in0=PS[bnk[c0]][:, 0:qn], in1=acc[:, qsl], op=ALU.add),
                                  r=[('ps', bnk[c0]), (ak, bi - 1)], w=[(ak, bi)], loose=True)

                for hp in range(4):
                    items = []
                    for bi, dil in enumerate(DIL):
                        nsub = S // dil
                        nb = nsub // 128
                        if dil == 1:
                            qsrc, ksrc, qk_, kk_ = qR, kR, ('a_qR', hp), ('a_kR', hp)
                        else:
                            cc = state['cc'] % 2
                            state['cc'] += 1
                            qsrc, ksrc, qk_, kk_ = qc[cc], kc[cc], ('a_qc', cc), ('a_kc', cc)
                            kb.op('pool', lambda e: e.tensor_copy(out=qsrc[:].rearrange("p (r i) -> p r i", r=dil),
                                                                  in_=qR[:, hp, :].rearrange("p (i r) -> p r i", r=dil)),
                                  r=[('a_qR', hp)], w=[qk_])
                            kb.op('pool', lambda e: e.tensor_copy(out=ksrc[:].rearrange("p (r i) -> p r i", r=dil),
                                                                  in_=kR[:, hp, :].rearrange("p (i r) -> p r i", r=dil)),
                                  r=[('a_kR', hp)], w=[kk_])
                        if nb == 1:
                            blocks = [(0, 128, [(0, 'E')])]
                        else:
                            blocks = [(0, 64, [(0, 'F')])]
                            blocks += [(64 + 128 * j, 128, [(j, 'A'), (j + 1, 'B')]) for j in range(nb - 1)]
                            blocks += [(nsub - 64, 64, [(nb - 1, 'A')])]
                        for r_ in range(dil):
                            for (q0, qn, kts) in blocks:
                                items.append({'a': (hp, bi, dil, nb, r_, q0, qn, kts, qsrc, ksrc, qk_, kk_, r_ * nsub), 'it': state['it']})
                                state['it'] += 1
                    prev = None
                    for w_ in items:
                        emit_S(w_)
                        if prev is not None:
                            emit_PV(prev)
                        prev = w_
                    emit_PV(prev)
                    nk = [('a_num', b_) for b_ in range(3)]
                    dk_ = [('a_den', b_) for b_ in range(3)]
                    kb.op('dve', lambda e: e.reciprocal(out=denacc[:], in_=denacc[:]), r=dk_, w=dk_)
                    kb.op('dve', lambda e: e.tensor_tensor(out=catT[:, hp, :], in0=numacc[:], in1=denacc[:], op=ALU.mult),
                          r=nk + dk_, w=[('catT', hp)] + nk)
            kb.barrier()
            chk('attn')
            st.close()

        def ffn_phase(st0, l, xsrc, xdst, hnT):
            st = st0.enter_context(ExitStack())
            aT = sb(st, "f_aT", [128, NFT, S], BF16)
            with ExitStack() as s1:
                Wgs = [sb(s1, "f_wg%d" % i, [128, KT, 512], BF16) for i in range(2)]
                Wvs = [sb(s1, "f_wv%d" % i, [128, KT, 512], BF16) for i in range(2)]
                yp = [sb(s1, "f_yp%d" % i, [128, S + 2], F32) for i in range(2)]
                u = [sb(s1, "f_u%d" % i, [128, S], F32) for i in range(2)]
                gl = sb(s1, "f_gl", [128, S], BF16)
                for i in range(2):
                    kb.op('dve', lambda e: e.memset(yp[i][:, 0:1], 0.0), w=[('f_yp', i)])
                    kb.op('dve', lambda e: e.memset(yp[i][:, S + 1:S + 2], 0.0), w=[('f_yp', i)])
                pi = 0
                for c in range(NFT):
                    g4 = c % 4

                    def ldgrp(c_):
                        gi = (c_ // 4) % 2
                        n = min(4, NFT - c_) * 128
                        load_w(Wgs[gi][:, :, 0:n], w_up[l, :, c_ * 128:c_ * 128 + n], ('f_wg', gi))
                        load_w(Wvs[gi][:, :, 0:n], w_up[l, :, DFF + c_ * 128:DFF + c_ * 128 + n], ('f_wv', gi))
                    if c == 0:
                        ldgrp(0)
                    if g4 == 0 and c + 4 < NFT:
                        ldgrp(c + 4)
                    gi_ = (c // 4) % 2
                    Wg, Wv = Wgs[gi_], Wvs[gi_]
                    for part, (W, wk) in enumerate(((Wg, ('f_wg', gi_)), (Wv, ('f_wv', gi_)))):
                        ti = part * NFT + c
                        for ch in range(NCH):
                            b = pi % 4
                            pi += 1
                            for k in range(KT):
                                kb.op('pe', lambda e: e.matmul(PS[b][:], lhsT=W[:, k, g4 * 128:(g4 + 1) * 128], rhs=hnT[:, k, ch * CH:(ch + 1) * CH],
                                                               start=(k == 0), stop=(k == KT - 1)), r=[wk, ('hnT', ch)], w=[('ps', b)])
                            kb.op('act', lambda e: e.activation(out=yp[part][:, 1 + ch * CH:1 + (ch + 1) * CH], in_=PS[b][:], func=AF.Copy),
                                  r=[('ps', b)], w=[('f_yp', part)])
                        kb.op('act', lambda e: e.activation(out=u[part][:], in_=yp[part][:, 0:S], func=AF.Identity,
                                                            scale=fconv[:, l, ti, 0:1], bias=fconv[:, l, ti, 3:4]),
                              r=[('f_yp', part), 'fconv'], w=[('f_u', part)])
                        for jj in (1, 2):
                            kb.op('dve', lambda e: e.scalar_tensor_tensor(out=u[part][:], in0=yp[part][:, jj:jj + S], scalar=fconv[:, l, ti, jj:jj + 1],
                                                                          in1=u[part][:], op0=ALU.mult, op1=ALU.add),
                                  r=[('f_yp', part), 'fconv'], w=[('f_u', part)])
                    kb.op('act', lambda e: e.activation(out=gl[:], in_=u[0][:], func=AF.Gelu_apprx_tanh), r=[('f_u', 0)], w=['f_gl'])
                    kb.op('dve', lambda e: e.tensor_tensor(out=aT[:, c, :], in0=gl[:], in1=u[1][:], op=ALU.mult),
                          r=['f_gl', ('f_u', 1)], w=[('f_aT', c)])
            kb.barrier()
            chk('fup')
            with ExitStack() as s2:
                last = (l == nlayers - 1)
                epilogue(s2, w_down[l], NFT, lambda ci, tsl: aT[:, ci, tsl], lambda ci, c: [('f_aT', ci)], xsrc, xdst, l, 3,
                         che=256, hn_out=None if last else hnT, nxt=None if last else (l + 1, 0))
            kb.barrier()
            st.close()

        xcur = xT_in
        hnT = sb(top, "hnT", [128, KT, S], BF16)
        try:
          for l in range(nlayers if stop_at != 'init' else 0):
              xmid = xs[0]
              xnext = yT if l == nlayers - 1 else xs[1]
              with ExitStack() as sm_:
                  catT = sb(sm_, "catT", [128, KT, S], BF16)
                  with ExitStack() as sh_:
                      if l == 0:
                          with ExitStack() as s0:
                              norm_pre(s0, xcur, l, 0, hnT)
                      kb.barrier()
                      chk('norm')
                      mlstm_phase(sh_, l, hnT, catT)
                      attn_phase(sh_, l, hnT, catT)
                  if 'cat' in dbg_out and l == dbg.get('_layer', 0):
                      with ExitStack() as sd:
                          tmpf = sb(sd, "dbg_tmp", [128, S], F32)
                          for i in range(KT):
                              kb.op('act', lambda e: e.activation(out=tmpf[:], in_=catT[:, i, :], func=AF.Copy), r=[('catT', i)], w=['dbg_tmp'])
                              kb.dma('sp', dbg_out['cat'][i * 128:(i + 1) * 128, :], tmpf[:], r=['dbg_tmp'], w=['dbg_cat'])
                      kb.barrier()
                  with ExitStack() as se:
                      epilogue(se, w_out[l], KT, lambda ci, tsl: catT[:, ci, tsl], lambda ci, c: [('catT', ci)], xcur, xmid, l, 1,
                               che=CH, hn_out=hnT, nxt=(l, 2))
                  kb.barrier()
                  chk('ep1')
              with ExitStack() as sf:
                  ffn_phase(sf, l, xmid, xnext, hnT)
              xcur = xnext
        except _Stop:
            pass
        kb.barrier()
    stuck = kb.check_deadlock()
    if stuck:
        raise RuntimeError('static deadlock: %r' % (stuck,))
    return nc, list(dbg_out.keys())


def _host_prep(inputs):
    f = np.float32
    g = np.stack([inputs['mix_pre_g'], inputs['mix_post_g'], inputs['ffn_pre_g'], inputs['ffn_post_g']], axis=1)
    gvec = np.ascontiguousarray(g.reshape(2, 4, KT, 128).transpose(3, 0, 1, 2)).reshape(128, -1).astype(f)
    mc = np.concatenate([inputs['mlstm_conv_w'], inputs['mlstm_conv_b'][:, None, :]], axis=1)
    mconv = np.ascontiguousarray(mc.reshape(2, 6, 8, 128).transpose(3, 0, 2, 1)).reshape(128, -1).astype(f)
    fc = np.concatenate([inputs['ffn_conv_w'], inputs['ffn_conv_b'][:, None, :]], axis=1)
    fconv = np.ascontiguousarray(fc.reshape(2, 4, 44, 128).transpose(3, 0, 2, 1)).reshape(128, -1).astype(f)
    gateb = np.ascontiguousarray(np.broadcast_to(inputs['mlstm_gate_b'].reshape(1, -1), (128, 32))).astype(f)
    headg = np.ascontiguousarray(np.broadcast_to(inputs['mlstm_head_g'].reshape(1, -1), (128, 1024))).astype(f)
    p = np.arange(128)
    inv_freq = (10000.0 ** (-np.arange(0, 64, 2, dtype=np.float32) / 64)).astype(np.float32)
    ang = np.arange(S, dtype=np.float32)[None, :] * inv_freq[p % 32][:, None]
    cosT = np.cos(ang).astype(f)
    sgn = np.where((p % 64) < 32, -1.0, 1.0).astype(f)[:, None]
    sinT = (np.sin(ang) * sgn).astype(f)
    a = p[:, None]
    b = p[None, :]
    NEG = -30000.0
    cst = np.zeros((128, 9, 128), f)
    cst[:, 0] = 1.0
    cst[:, 1] = (a == b)
    cst[:, 2] = np.where(a >= b, 0.0, NEG)
    cst[:, 3] = np.where(a <= b, 0.0, NEG)
    cst[:, 4] = np.where(a <= b + 64, 0.0, NEG)
    cst[:, 5] = np.where(np.abs(a - b) <= 64, 0.0, NEG)
    cst[:, 6] = (a <= b)
    cst[:, 7] = (a >= b)
    A01 = (a >= b).astype(f); B01 = (a <= b).astype(f); F01 = (a <= b + 64).astype(f); E01 = (np.abs(a - b) <= 64).astype(f)
    ms = np.zeros((128, 4, 512), f)
    ms[:, 0] = np.concatenate([A01, B01, A01, B01], axis=1)
    ms[:, 1, 0:128] = np.concatenate([F01[:, :64], F01[:, :64]], axis=1)
    ms[:, 2, 0:128] = np.concatenate([A01[:, :64], A01[:, :64]], axis=1)
    ms[:, 3, 0:256] = np.concatenate([E01, E01], axis=1)
    shared = dict(mstrip=ms.reshape(128, -1), w_in=np.ascontiguousarray(inputs['w_in'], dtype=f), w_out=np.ascontiguousarray(inputs['w_out'], dtype=f),
                  w_up=np.ascontiguousarray(inputs['w_up'], dtype=f), w_down=np.ascontiguousarray(inputs['w_down'], dtype=f),
                  gvec=gvec, mconv=mconv, fconv=fconv, gateb=gateb, headg=headg, cosT=cosT, sinT=sinT,
                  cst=cst.reshape(128, -1))
    return shared


_NC_CACHE = {}


def kernel(**inputs):
    x = np.asarray(inputs['x'], dtype=np.float32)
    B = x.shape[0]
    shared = _host_prep(inputs)
    if 'nc' not in _NC_CACHE:
        _NC_CACHE['nc'] = build(2)[0]
    nc = _NC_CACHE['nc']
    in_maps = []
    for b in range(B):
        m = dict(shared)
        m['xT'] = np.ascontiguousarray(x[b].T)
        in_maps.append(m)
    res = run_bass_kernel_spmd(nc, in_maps, core_ids=list(range(B)))
    out = np.stack([np.ascontiguousarray(res.results[b]['yT'].T) for b in range(B)], axis=0)
    return out.astype(np.float32)
```

```python
import math
from contextlib import ExitStack
import numpy as np
import concourse.bass as bass
import concourse.mybir as mybir
from concourse.bass_utils import run_bass_kernel_spmd

F32 = mybir.dt.float32
BF16 = mybir.dt.bfloat16
AF = mybir.ActivationFunctionType
ALU = mybir.AluOpType

S = 2048
D = 1024
KT = 8
NCH = 4
CH = 512
DFF = 2816
NFT = 22
EPS = 1e-6
NDS = 24


def ssl(start, n, step):
    return slice(start, start + step * (n - 1) + 1, step)


class KB:
    def __init__(self, nc):
        self.nc = nc
        self.E = {'pe': nc.tensor, 'act': nc.scalar, 'dve': nc.vector, 'pool': nc.gpsimd, 'sp': nc.sync}
        self.sem = {e: nc.alloc_semaphore('c_' + e) for e in ('pe', 'act', 'dve', 'pool')}
        self.cnt = {e: 0 for e in self.sem}
        self.waited = {e: {} for e in self.E}
        self.dsems = [nc.alloc_semaphore('d%d' % i) for i in range(NDS)]
        self.dcnt = [0] * NDS
        self.dlast = [None] * NDS
        self.dnext = 0
        self.dnext_p = 0
        self.tk = {}
        self.dead = False
        self.prog = {e: [] for e in self.E}

    def _t(self, key):
        t = self.tk.get(key)
        if t is None:
            t = {'w': None, 'r': {}}
            self.tk[key] = t
        return t

    def _wait(self, e, ev):
        sem, val, sid = ev
        if self.waited[e].get(sid, 0) >= val:
            return
        self.E[e].wait_ge(sem, val)
        self.prog[e].append(('w', sid, val))
        self.waited[e][sid] = val

    def _deps(self, e, r, w, loose=False):
        evs = {}

        def add(ev):
            if ev is None:
                return
            if ev[2] not in evs or evs[ev[2]][1] < ev[1]:
                evs[ev[2]] = ev
        for k in r:
            add(self._t(k)['w'])
        for k in w:
            t = self._t(k)
            add(t['w'])
            for ev in t['r'].values():
                add(ev)
        for sid, ev in evs.items():
            if sid == e and (e == 'pe' or loose):
                continue
            self._wait(e, ev)

    def _record(self, ev, r, w):
        for k in r:
            self._t(k)['r'][ev[2]] = ev
        for k in w:
            t = self._t(k)
            t['w'] = ev
            t['r'] = {}

    def op(self, e, fn, r=(), w=(), loose=False, inc=True):
        if self.dead:
            return
        self._deps(e, r, w, loose)
        ins = fn(self.E[e])
        if inc:
            self.cnt[e] += 1
            ins.then_inc(self.sem[e], 1)
            self.prog[e].append(('i', e, 1))
            self._record((self.sem[e], self.cnt[e], e), r, w)
        else:
            self._record((self.sem[e], self.cnt[e] + 1, e), r, w)

    def dma(self, q, out, in_, r=(), w=()):
        if self.dead:
            return
        if q == 'pool':
            i = 16 + self.dnext_p
            self.dnext_p = (self.dnext_p + 1) % (NDS - 16)
        else:
            i = self.dnext
            self.dnext = (i + 1) % 16
        if self.dlast[i] is not None:
            self._wait(q, self.dlast[i])
        self._deps(q, r, w)
        ins = self.E[q].dma_start(out=out, in_=in_)
        self.dcnt[i] += 16
        ins.then_inc(self.dsems[i], 16)
        self.prog[q].append(('i', 'd%d' % i, 16))
        ev = (self.dsems[i], self.dcnt[i], 'd%d' % i)
        self.dlast[i] = ev
        self._record(ev, r, w)
        return ev

    def check_deadlock(self):
        pc = {e: 0 for e in self.prog}
        val = {}
        progress = True
        while progress:
            progress = False
            for e, p in self.prog.items():
                while pc[e] < len(p):
                    k, sid, v = p[pc[e]]
                    if k == 'w':
                        if val.get(sid, 0) < v:
                            break
                    else:
                        val[sid] = val.get(sid, 0) + v
                    pc[e] += 1
                    progress = True
        stuck = {e: (pc[e], len(p), p[pc[e]]) for e, p in self.prog.items() if pc[e] < len(p)}
        return stuck

    def barrier(self):
        if self.dead:
            return
        evs = [(self.sem[e], self.cnt[e], e) for e in self.sem if self.cnt[e] > 0]
        evs += [ev for ev in self.dlast if ev is not None]
        for e in self.E:
            for ev in evs:
                if ev[2] == e:
                    continue
                self._wait(e, ev)


class _Stop(Exception):
    pass


def build(nlayers=2, dbg=None, stop_at=None):
    dbg = dbg or {}
    nc = bass.Bass("TRN2", target_bir_lowering=False)
    kb = KB(nc)
    ein = lambda n, s, dt=F32: nc.dram_tensor(n, list(s), dt, kind="ExternalInput").ap()
    xT_in = ein("xT", [D, S])
    w_in = ein("w_in", [2, D, 3600])
    w_out = ein("w_out", [2, D, D])
    w_up = ein("w_up", [2, D, 2 * DFF])
    w_down = ein("w_down", [2, DFF, D])
    gvec_d = ein("gvec", [128, 2 * 4 * KT])
    mconv_d = ein("mconv", [128, 2 * 8 * 6])
    fconv_d = ein("fconv", [128, 2 * 44 * 4])
    gateb_d = ein("gateb", [128, 2 * 16])
    headg_d = ein("headg", [128, 2 * 512])
    cos_d = ein("cosT", [128, S])
    sin_d = ein("sinT", [128, S])
    mstrip_d = ein("mstrip", [128, 4 * 512])
    cst_d = ein("cst", [128, 9 * 128])
    yT = nc.dram_tensor("yT", [D, S], F32, kind="ExternalOutput").ap()
    xs = [nc.dram_tensor("xs%d" % i, [D, S], F32).ap() for i in range(2)]
    dbg_out = {}
    for name, shape in dbg.items():
        dbg_out[name] = nc.dram_tensor("dbg_" + name, list(shape), F32, kind="ExternalOutput").ap()

    pm = lambda ap: ap.rearrange("(k p) t -> p k t", p=128)

    with ExitStack() as top:
        uid = [0]

        def sb(st, name, shape, dt):
            uid[0] += 1
            return st.enter_context(nc.sbuf_tensor("s%d_%s" % (uid[0], name), list(shape), dt))
        gvec = sb(top, "gvec", [128, 2, 4, KT], F32)
        mconv = sb(top, "mconv", [128, 2, 8, 6], F32)
        fconv = sb(top, "fconv", [128, 2, 44, 4], F32)
        gateb = sb(top, "gateb", [128, 2, 16], F32)
        headg = sb(top, "headg", [128, 2, 512], BF16)
        cstb = sb(top, "cstb", [128, 9, 128], BF16)
        epsb = sb(top, "epsb", [128, 1], F32)
        psall = top.enter_context(nc.psum_tensor("psall", [128, 7, 512], F32))
        PS = [psall[:, i, :] for i in range(7)]
        PST = top.enter_context(nc.psum_tensor("pst", [128, 1024], BF16))
        for i in range(7):
            kb._t(('ps', i))
        kb.dma('sp', gvec[:].rearrange("p a b c -> p (a b c)"), gvec_d, w=['gvec'])
        kb.dma('sp', mconv[:].rearrange("p a b c -> p (a b c)"), mconv_d, w=['mconv'])
        kb.dma('sp', fconv[:].rearrange("p a b c -> p (a b c)"), fconv_d, w=['fconv'])
        kb.dma('sp', gateb[:].rearrange("p a b -> p (a b)"), gateb_d, w=['gateb'])
        kb.dma('pool', headg[:].rearrange("p a b -> p (a b)"), headg_d, w=['headg'])
        kb.dma('pool', cstb[:].rearrange("p a b -> p (a b)"), cst_d, w=['cstb'])
        cview = cst_d.rearrange("p (a b) -> p a b", b=128)
        kb.op('dve', lambda e: e.memset(epsb[:], EPS), w=['epsb'])
        ones_bf = cstb[:, 0, :]
        ident_bf = cstb[:, 1, :]
        MASK = {'A': cstb[:, 2, :], 'B': cstb[:, 3, :], 'F': cstb[:, 4, :], 'E': cstb[:, 5, :]}
        tri_bf = [cstb[:, 6, :], cstb[:, 7, :]]

        def chk(name):
            if stop_at == name:
                kb.barrier()
                kb.dead = True

        def dump(name, sb_ap, rkeys):
            if name in dbg_out:
                kb.dma('sp', dbg_out[name], sb_ap, r=rkeys, w=['dbg_' + name])

        def rstd_from_ps(ps_i, rstd_ap, rkey, n):
            kb.op('act', lambda e: e.activation(out=rstd_ap, in_=PS[ps_i][:, 0:rstd_ap.shape[1]], func=AF.Ln,
                                                scale=1.0 / n, bias=epsb[:, 0:1]),
                  r=[('ps', ps_i), 'epsb'], w=[rkey])
            kb.op('act', lambda e: e.activation(out=rstd_ap, in_=rstd_ap, func=AF.Exp, scale=-0.5), r=[rkey], w=[rkey])

        def norm_pre(st, xsrc, l, j, hnT):
            xcs = [sb(st, "np_xc%d" % i, [128, KT, CH], F32) for i in range(2)]
            sq = sb(st, "np_sq", [128, KT, CH], BF16)
            rstd = sb(st, "np_rstd", [128, CH], F32)
            for c in range(NCH):
                xc = xcs[c % 2]
                xk = ('np_xc', c % 2)
                kb.dma('sp', xc[:], pm(xsrc)[:, :, c * CH:(c + 1) * CH], w=[xk])
                kb.op('act', lambda e: e.activation(out=sq[:], in_=xc[:], func=AF.Square), r=[xk], w=['np_sq'])
                for k in range(KT):
                    kb.op('pe', lambda e: e.matmul(PS[6][:], lhsT=ones_bf, rhs=sq[:, k, :], start=(k == 0), stop=(k == KT - 1)),
                          r=['np_sq', 'cstb'], w=[('ps', 6)])
                rstd_from_ps(6, rstd[:], 'np_rstd', D)
                for k in range(KT):
                    kb.op('dve', lambda e: e.scalar_tensor_tensor(out=hnT[:, k, c * CH:(c + 1) * CH], in0=xc[:, k, :],
                                                                  scalar=gvec[:, l, j, k:k + 1], in1=rstd[:],
                                                                  op0=ALU.mult, op1=ALU.mult),
                          r=[xk, 'np_rstd', 'gvec'], w=[('hnT', c)])

        def load_w(dst, src_rows_cols, wkey):
            kb.dma('pool', dst, src_rows_cols.rearrange("(k p) c -> p k c", p=128), w=[wkey])

        def epilogue(st, Wd, nct, rhs_fn, rkeys_fn, xsrc, xdst, l, j, che=CH, hn_out=None, nxt=None):
            W = sb(st, "ep_w", [128, nct, D], BF16)
            xcs = [sb(st, "ep_xc%d" % i, [128, KT, che], F32) for i in range(2)]
            ff = sb(st, "ep_ff", [128, KT, che], F32)
            sq = sb(st, "ep_sq", [128, KT, che], BF16)
            rstd = sb(st, "ep_rstd", [128, che], F32)
            if hn_out is not None:
                sq2 = sb(st, "ep_sq2", [128, KT, che], BF16)
                rstd2 = sb(st, "ep_rstd2", [128, che], F32)
            for hf in range(2):
                for k0 in range(0, nct, 8):
                    k1 = min(nct, k0 + 8)
                    load_w(W[:, k0:k1, hf * 512:(hf + 1) * 512], Wd[k0 * 128:k1 * 128, hf * 512:(hf + 1) * 512], ('ep_w', hf))
            state = {'pi': 0}
            nchunk = S // che

            def partA(c):
                tsl = slice(c * che, (c + 1) * che)
                xc = xcs[c % 2]
                xk = ('ep_xc', c % 2)
                kb.dma('sp', xc[:], pm(xsrc)[:, :, tsl], w=[xk])
                for m in range(KT):
                    b = state['pi'] % 4
                    state['pi'] += 1
                    for ci in range(nct):
                        kb.op('pe', lambda e: e.matmul(PS[b][:, 0:che], lhsT=W[:, ci, m * 128:(m + 1) * 128], rhs=rhs_fn(ci, tsl),
                                                       start=(ci == 0), stop=(ci == nct - 1)),
                              r=[('ep_w', m // 4)] + rkeys_fn(ci, c), w=[('ps', b)], inc=(ci == nct - 1))
                    kb.op('act', lambda e: e.activation(out=ff[:, m, :], in_=PS[b][:, 0:che], func=AF.Copy), r=[('ps', b)], w=[('ep_ff', m)])
                    kb.op('act', lambda e: e.activation(out=sq[:, m, :], in_=ff[:, m, :], func=AF.Square), r=[('ep_ff', m)], w=[('ep_sq', m)])

            def partB(c):
                tsl = slice(c * che, (c + 1) * che)
                xc = xcs[c % 2]
                xk = ('ep_xc', c % 2)
                for m in range(KT):
                    kb.op('pe', lambda e: e.matmul(PS[6][:, 0:che], lhsT=ones_bf, rhs=sq[:, m, :], start=(m == 0), stop=(m == KT - 1)),
                          r=[('ep_sq', m), 'cstb'], w=[('ps', 6)], inc=(m == KT - 1))
                rstd_from_ps(6, rstd[:], 'ep_rstd', D)
                for m in range(KT):
                    kb.op('dve', lambda e: e.scalar_tensor_tensor(out=ff[:, m, :], in0=ff[:, m, :], scalar=gvec[:, l, j, m:m + 1],
                                                                  in1=rstd[:], op0=ALU.mult, op1=ALU.mult),
                          r=['ep_rstd', 'gvec'], w=[('ep_ff', m)])
                    kb.op('dve', lambda e: e.tensor_tensor(out=xc[:, m, :], in0=xc[:, m, :], in1=ff[:, m, :], op=ALU.add),
                          r=[('ep_ff', m)], w=[xk])
                kb.dma('sp', pm(xdst)[:, :, tsl], xc[:], r=[xk], w=[('xdst', c)])

            def partC(c):
                if hn_out is None:
                    return
                tsl = slice(c * che, (c + 1) * che)
                xc = xcs[c % 2]
                xk = ('ep_xc', c % 2)
                l2, j2 = nxt
                kb.op('act', lambda e: e.activation(out=sq2[:], in_=xc[:], func=AF.Square), r=[xk], w=['ep_sq2'])
                for m in range(KT):
                    kb.op('pe', lambda e: e.matmul(PS[5][:, 0:che], lhsT=ones_bf, rhs=sq2[:, m, :], start=(m == 0), stop=(m == KT - 1)),
                          r=['ep_sq2', 'cstb'], w=[('ps', 5)], inc=(m == KT - 1))
                rstd_from_ps(5, rstd2[:], 'ep_rstd2', D)
                for m in range(KT):
                    kb.op('dve', lambda e: e.scalar_tensor_tensor(out=hn_out[:, m, tsl], in0=xc[:, m, :], scalar=gvec[:, l2, j2, m:m + 1],
                                                                  in1=rstd2[:], op0=ALU.mult, op1=ALU.mult),
                          r=[xk, 'ep_rstd2', 'gvec'], w=[('hnT', (c * che) // CH)])

            partA(0)
            for c in range(nchunk):
                partB(c)
                if c + 1 < nchunk:
                    partA(c + 1)
                partC(c)

        def mlstm_phase(st0, l, hnT, catT):
            st = st0.enter_context(ExitStack())
            qkT = sb(st, "m_qkT", [128, 8, S], BF16)
            kTM = sb(st, "m_kTM", [128, 16, 512], BF16)
            vaug = sb(st, "m_vaug", [128, 16, 4, 132], BF16)
            gsig = sb(st, "m_gsig", [128, 16, 512], BF16)
            gates = sb(st, "m_gates", [128, 16, 16], F32)
            l1 = sb(st, "m_l1", [128, 16, 8], F32)
            gd = sb(st, "m_gd", [128, 5, 2, 16, 4], F32)
            with ExitStack() as sp_:
                Wa = sb(sp_, "m_wa", [128, KT, 512], BF16)
                Wb = sb(sp_, "m_wb", [128, KT, 512], BF16)
                Wg = sb(sp_, "m_wg", [128, KT, 16], BF16)
                ypads = [sb(sp_, "m_ypad%d" % i, [128, S + 4], BF16) for i in range(2)]
                uaccs = [sb(sp_, "m_uacc%d" % i, [128, S], F32) for i in range(2)]
                sgt = sb(sp_, "m_sgt", [128, 512], BF16)
                load_w(Wa[:], w_in[l, :, 1536:2048], 'm_wa')
                load_w(Wb[:], w_in[l, :, 2048:2560], 'm_wb')
                load_w(Wg[:], w_in[l, :, 3584:3600], 'm_wg')
                for i in range(2):
                    kb.op('dve', lambda e: e.memset(ypads[i][:, 0:2], 0.0), w=[('m_ypad', i)])
                    kb.op('dve', lambda e: e.memset(ypads[i][:, S + 2:S + 4], 0.0), w=[('m_ypad', i)])
                kb.op('dve', lambda e: e.memset(vaug[:, :, :, 128:129], 1.0), w=['m_vaug1'])
                cst_ = {'pi': 0}

                def convA(i):
                    W = Wa if i < 4 else Wb
                    wk = 'm_wa' if i < 4 else 'm_wb'
                    cs = (i % 4) * 128
                    ypad, uacc = ypads[i % 2], uaccs[i % 2]
                    yk, uk = ('m_ypad', i % 2), ('m_uacc', i % 2)
                    for c in range(NCH):
                        b = cst_['pi'] % 4
                        cst_['pi'] += 1
                        for k in range(KT):
                            kb.op('pe', lambda e: e.matmul(PS[b][:], lhsT=W[:, k, cs:cs + 128], rhs=hnT[:, k, c * CH:(c + 1) * CH],
                                                           start=(k == 0), stop=(k == KT - 1)),
                                  r=[wk, ('hnT', c)], w=[('ps', b)], inc=(k == KT - 1))
                        kb.op('act', lambda e: e.activation(out=ypad[:, 2 + c * CH:2 + (c + 1) * CH], in_=PS[b][:], func=AF.Copy),
                              r=[('ps', b)], w=[yk])
                    kb.op('act', lambda e: e.activation(out=uacc[:], in_=ypad[:, 0:S], func=AF.Identity,
                                                        scale=mconv[:, l, i, 0:1], bias=mconv[:, l, i, 5:6]),
                          r=[yk, 'mconv'], w=[uk])

                def convB(i):
                    ypad, uacc = ypads[i % 2], uaccs[i % 2]
                    yk, uk = ('m_ypad', i % 2), ('m_uacc', i % 2)
                    for jj in range(1, 5):
                        kb.op('dve', lambda e: e.scalar_tensor_tensor(out=uacc[:], in0=ypad[:, jj:jj + S], scalar=mconv[:, l, i, jj:jj + 1],
                                                                      in1=uacc[:], op0=ALU.mult, op1=ALU.add),
                              r=[yk, 'mconv'], w=[uk])

                def convC(i):
                    uacc, uk = uaccs[i % 2], ('m_uacc', i % 2)
                    kb.op('act', lambda e: e.activation(out=qkT[:, i, :], in_=uacc[:], func=AF.Silu), r=[uk], w=[('m_qkT', i)])

                convA(0)
                for i in range(8):
                    convB(i)
                    if i + 1 < 8:
                        convA(i + 1)
                    convC(i)
                if 'qk' in dbg_out:
                    for i in range(8):
                        kb.op('act', lambda e: e.activation(out=uaccs[0][:], in_=qkT[:, i, :], func=AF.Copy), r=[('m_qkT', i)], w=[('m_uacc', 0)])
                        dump_ap = dbg_out['qk'][i * 128:(i + 1) * 128, :]
                        kb.dma('sp', dump_ap, uaccs[0][:], r=[('m_uacc', 0)], w=['dbg_qk'])
                for c in range(16):
                    for h in range(4):
                        kb.op('pe', lambda e: e.transpose(PST[:, h * 128:(h + 1) * 128], qkT[:, 4 + h, c * 128:(c + 1) * 128], ident_bf),
                              r=[('m_qkT', 4 + h), 'cstb'], w=['pst'])
                    kb.op('act', lambda e: e.activation(out=kTM[:, c, :], in_=PST[:, 0:512], func=AF.Copy), r=['pst'], w=[('m_kTM', c)])
                load_w(Wa[:], w_in[l, :, 2560:3072], 'm_wa')
                load_w(Wb[:], w_in[l, :, 3072:3584], 'm_wb')
                for c in range(16):
                    tk = ('hnT', c // 4)
                    for k in range(KT):
                        kb.op('pe', lambda e: e.matmul(PS[0][:], lhsT=hnT[:, k, c * 128:(c + 1) * 128], rhs=Wa[:, k, :],
                                                       start=(k == 0), stop=(k == KT - 1)), r=['m_wa', tk], w=[('ps', 0)])
                    kb.op('act', lambda e: e.activation(out=vaug[:, c, :, 0:128], in_=PS[0][:].rearrange("p (h f) -> p h f", f=128), func=AF.Copy),
                          r=[('ps', 0)], w=[('m_vaug', c)])
                    for k in range(KT):
                        kb.op('pe', lambda e: e.matmul(PS[1][:], lhsT=hnT[:, k, c * 128:(c + 1) * 128], rhs=Wb[:, k, :],
                                                       start=(k == 0), stop=(k == KT - 1)), r=['m_wb', tk], w=[('ps', 1)])
                    kb.op('act', lambda e: e.activation(out=sgt[:], in_=PS[1][:], func=AF.Sigmoid), r=[('ps', 1)], w=['m_sgt'])
                    kb.op('dve', lambda e: e.tensor_tensor(out=gsig[:, c, :], in0=sgt[:], in1=headg[:, l, :], op=ALU.mult),
                          r=['m_sgt', 'headg'], w=[('m_gsig', c)])
                    for k in range(KT):
                        kb.op('pe', lambda e: e.matmul(PS[2][:, 0:16], lhsT=hnT[:, k, c * 128:(c + 1) * 128], rhs=Wg[:, k, :],
                                                       start=(k == 0), stop=(k == KT - 1)), r=['m_wg', tk], w=[('ps', 2)])
                    kb.op('dve', lambda e: e.tensor_tensor(out=gates[:, c, :], in0=PS[2][:, 0:16], in1=gateb[:, l, :], op=ALU.add),
                          r=[('ps', 2), 'gateb'], w=['m_gates'])
                for d_ in range(2):
                    kb.op('act', lambda e: e.activation(out=l1[:, :, 4 * d_:4 * d_ + 4], in_=gates[:, :, 8 * d_ + 4:8 * d_ + 8], func=AF.Exp, scale=-1.0),
                          r=['m_gates'], w=['m_l1'])
                kb.op('act', lambda e: e.activation(out=l1[:], in_=l1[:], func=AF.Ln, bias=1.0), r=['m_l1'], w=['m_l1'])
                l1h = sb(sp_, "m_l1h", [128, 2, 16, 8], BF16)
                l1r = sb(sp_, "m_l1r", [128, 16, 8], F32)
                kb.op('dve', lambda e: e.tensor_copy(out=l1h[:, 0], in_=l1[:]), r=['m_l1'], w=['m_l1h'])
                kb.op('dve', lambda e: e.tensor_copy(out=l1r[:], in_=l1h[:, 0]), r=['m_l1h'], w=['m_l1r'])
                kb.op('dve', lambda e: e.tensor_tensor(out=l1r[:], in0=l1[:], in1=l1r[:], op=ALU.subtract), r=['m_l1', 'm_l1r'], w=['m_l1r'])
                kb.op('dve', lambda e: e.tensor_copy(out=l1h[:, 1], in_=l1r[:]), r=['m_l1r'], w=['m_l1h'])
                for d_ in range(2):
                    for (bank, lhs) in ((3, tri_bf[d_]), (4, ones_bf)):
                        for hl in range(2):
                            kb.op('pe', lambda e: e.matmul(PS[bank][:, 64 * d_:64 * d_ + 64].rearrange("p (c h) -> p c h", h=4), lhsT=lhs,
                                                           rhs=l1h[:, hl, :, 4 * d_:4 * d_ + 4], start=(hl == 0), stop=(hl == 1)),
                                  r=['m_l1h', 'cstb'], w=[('ps', bank)])
                    na = gd[:, 0, d_]
                    kb.op('act', lambda e: e.activation(out=na, in_=PS[3][:, 64 * d_:64 * d_ + 64].rearrange("p (c h) -> p c h", h=4), func=AF.Copy),
                          r=[('ps', 3)], w=['m_gd'])
                    kb.op('dve', lambda e: e.tensor_tensor(out=gd[:, 1, d_], in0=gates[:, :, 8 * d_:8 * d_ + 4], in1=na, op=ALU.add),
                          r=['m_gates', 'm_gd'], w=['m_gd'])
                    kb.op('act', lambda e: e.activation(out=gd[:, 1, d_], in_=gd[:, 1, d_], func=AF.Exp, bias=-0.5 * math.log(128.0)),
                          r=['m_gd'], w=['m_gd'])
                    kb.op('act', lambda e: e.activation(out=gd[:, 2, d_], in_=na, func=AF.Exp), r=['m_gd'], w=['m_gd'])
                    kb.op('act', lambda e: e.activation(out=gd[:, 3, d_], in_=PS[4][:, 64 * d_:64 * d_ + 64].rearrange("p (c h) -> p c h", h=4),
                                                        func=AF.Exp, scale=-1.0), r=[('ps', 4)], w=['m_gd'])
                    kb.op('dve', lambda e: e.tensor_tensor(out=gd[:, 4, d_], in0=gd[:, 1, d_], in1=gd[:, 3, d_], op=ALU.mult),
                          r=['m_gd'], w=['m_gd'])
            kb.barrier()
            chk('mproj')
            with ExitStack() as ss_:
                hm = sb(ss_, "m_hm", [128, 16, 512], F32)
                Cst = sb(ss_, "m_C", [128, 8, 132], F32)
                Cbf = sb(ss_, "m_Cbf", [128, 8, 132], BF16)
                PT = [sb(ss_, "m_PT%d" % i, [128, 128], BF16) for i in range(4)]
                Kt = [sb(ss_, "m_Kt%d" % i, [128, 128], BF16) for i in range(4)]
                sm = sb(ss_, "m_sm", [128, 8, 4], F32)
                ssh = sb(ss_, "m_ssh", [128, 16, 4], F32)
                junk = sb(ss_, "m_junk", [128, 128], BF16)
                mot = sb(ss_, "m_mot", [128, 512], BF16)
                kb.op('dve', lambda e: e.memset(Cst[:], 0.0), w=[('m_C', i) for i in range(8)])
                kb.op('dve', lambda e: e.memset(Cbf[:], 0.0), w=[('m_Cbf', i) for i in range(8)])
                written = set()

                def scanA(w_):
                    (it, step, h, d_, c) = w_
                    hd = h * 2 + d_
                    tok = slice(c * 128, (c + 1) * 128)
                    bs, bn, bu = it % 2, 2 + it % 2, 4 + it % 2
                    pt, ktl = PT[it % 4], Kt[it % 4]
                    ptk, ktk = ('m_PT', it % 4), ('m_Kt', it % 4)
                    kb.op('pe', lambda e: e.matmul(PS[bs][:, 0:128], lhsT=qkT[:, 4 + h, tok], rhs=qkT[:, h, tok], start=True, stop=True),
                          r=[('m_qkT', h), ('m_qkT', 4 + h)], w=[('ps', bs)])
                    kb.op('dve', lambda e: e.scalar_tensor_tensor(out=pt[:], in0=PS[bs][:, 0:128], scalar=gd[:, 1, d_, c, h:h + 1],
                                                                  in1=tri_bf[d_], op0=ALU.mult, op1=ALU.mult),
                          r=[('ps', bs), 'm_gd', 'cstb'], w=[ptk])
                    kb.op('act', lambda e: e.activation(out=ktl[:], in_=kTM[:, c, h * 128:(h + 1) * 128], func=AF.Copy, scale=gd[:, 4, d_, c, h:h + 1]),
                          r=[('m_kTM', c), 'm_gd'], w=[ktk])
                    kb.op('pe', lambda e: e.matmul(PS[bn][:, 0:129], lhsT=pt[:], rhs=vaug[:, c, h, 0:129], start=True, stop=False),
                          r=[ptk, ('m_vaug', c), 'm_vaug1'], w=[('ps', bn)])
                    kb.op('pe', lambda e: e.matmul(PS[bn][:, 0:129], lhsT=qkT[:, h, tok], rhs=Cbf[:, hd, 0:129], start=False, stop=True),
                          r=[('m_qkT', h), ('m_Cbf', hd)], w=[('ps', bn)])
                    kb.op('pe', lambda e: e.matmul(PS[bu][:, 0:129], lhsT=ktl[:], rhs=vaug[:, c, h, 0:129], start=True, stop=True),
                          r=[ktk, ('m_vaug', c), 'm_vaug1'], w=[('ps', bu)])
                    smk = ('m_sm', hd)
                    kb.op('act', lambda e: e.activation(out=sm[:, hd, 0:1], in_=PS[bn][:, 128:129], func=AF.Abs), r=[('ps', bn)], w=[smk])

                def scanB(w_):
                    (it, step, h, d_, c) = w_
                    hd = h * 2 + d_
                    bs, bn, bu = it % 2, 2 + it % 2, 4 + it % 2
                    smk = ('m_sm', hd)
                    kb.op('dve', lambda e: e.tensor_tensor(out=sm[:, hd, 1:2], in0=sm[:, hd, 0:1], in1=gd[:, 2, d_, c, h:h + 1], op=ALU.max),
                          r=[smk, 'm_gd'], w=[smk])
                    kb.op('dve', lambda e: e.reciprocal(out=sm[:, hd, 2:3], in_=sm[:, hd, 1:2]), r=[smk], w=[smk])
                    hk = ('m_hm', c, h)
                    hdst = hm[:, c, h * 128:(h + 1) * 128]
                    if (c, h) not in written:
                        written.add((c, h))
                        kb.op('act', lambda e: e.activation(out=hdst, in_=PS[bn][:, 0:128], func=AF.Copy, scale=sm[:, hd, 2:3]),
                              r=[('ps', bn), smk], w=[hk])
                    else:
                        kb.op('dve', lambda e: e.scalar_tensor_tensor(out=hdst, in0=PS[bn][:, 0:128], scalar=sm[:, hd, 2:3], in1=hdst,
                                                                      op0=ALU.mult, op1=ALU.add),
                              r=[('ps', bn), smk], w=[hk])
                    kb.op('dve', lambda e: e.scalar_tensor_tensor(out=Cst[:, hd, 0:129], in0=Cst[:, hd, 0:129], scalar=gd[:, 3, d_, c, h:h + 1],
                                                                  in1=PS[bu][:, 0:129], op0=ALU.mult, op1=ALU.add),
                          r=[('ps', bu), 'm_gd'], w=[('m_C', hd)])
                    kb.op('act', lambda e: e.activation(out=Cbf[:, hd, 0:129], in_=Cst[:, hd, 0:129], func=AF.Copy),
                          r=[('m_C', hd)], w=[('m_Cbf', hd)])

                work = []
                for step in range(16):
                    for h in range(4):
                        for d_ in range(2):
                            work.append((len(work), step, h, d_, step if d_ == 0 else 15 - step))
                prevw = None
                for w_ in work:
                    scanA(w_)
                    if prevw is not None:
                        scanB(prevw)
                    prevw = w_
                scanB(prevw)
                if 'hm' in dbg_out:
                    kb.dma('sp', dbg_out['hm'].rearrange("(c p) f -> p c f", p=128), hm[:], r=[('m_hm', c, h) for c in range(16) for h in range(4)], w=['dbg_hm'])
                for c in range(16):
                    for h in range(4):
                        kb.op('act', lambda e: e.activation(out=junk[:], in_=hm[:, c, h * 128:(h + 1) * 128], func=AF.Square,
                                                            accum_out=ssh[:, c, h:h + 1]), r=[('m_hm', c, h)], w=['m_junk', 'm_ssh'])
                kb.op('act', lambda e: e.activation(out=ssh[:], in_=ssh[:], func=AF.Sqrt, scale=1.0 / 128, bias=epsb[:, 0:1]), r=['m_ssh', 'epsb'], w=['m_ssh'])
                kb.op('dve', lambda e: e.reciprocal(out=ssh[:], in_=ssh[:]), r=['m_ssh'], w=['m_ssh'])
                for c in range(16):
                    for h in range(4):
                        kb.op('dve', lambda e: e.scalar_tensor_tensor(out=mot[:, h * 128:(h + 1) * 128], in0=hm[:, c, h * 128:(h + 1) * 128],
                                                                      scalar=ssh[:, c, h:h + 1], in1=gsig[:, c, h * 128:(h + 1) * 128],
                                                                      op0=ALU.mult, op1=ALU.mult),
                              r=[('m_hm', c, h), 'm_ssh', ('m_gsig', c)], w=['m_mot'])
                    for h in range(4):
                        kb.op('pe', lambda e: e.transpose(PST[:, h * 128:(h + 1) * 128], mot[:, h * 128:(h + 1) * 128], ident_bf),
                              r=['m_mot', 'cstb'], w=['pst'])
                    kb.op('act', lambda e: e.activation(out=catT[:, 4:8, c * 128:(c + 1) * 128], in_=PST[:, 0:512].rearrange("p (h t) -> p h t", t=128), func=AF.Copy),
                          r=['pst'], w=[('catT', 4 + h_) for h_ in range(4)])
            kb.barrier()
            chk('mlstm')
            st.close()

        def attn_phase(st0, l, hnT, catT):
            st = st0.enter_context(ExitStack())
            qR = sb(st, "a_qR", [128, 4, S], BF16)
            kR = sb(st, "a_kR", [128, 4, S], BF16)
            with ExitStack() as s1:
                cosT = sb(s1, "a_cos", [128, S], F32)
                sinT = sb(s1, "a_sin", [128, S], F32)
                Wns = [sb(s1, "a_wn%d" % i, [128, KT, 512], BF16) for i in range(2)]
                Ws = sb(s1, "a_ws", [128, KT, 512], BF16)
                t1s = [sb(s1, "a_t1%d" % i, [128, CH], F32) for i in range(2)]
                t2s = [sb(s1, "a_t2%d" % i, [128, CH], F32) for i in range(2)]
                kb.dma('sp', cosT[:], cos_d, w=['a_cos'])
                kb.dma('sp', sinT[:], sin_d, w=['a_sin'])
                pi = 0
                for qk in range(2):
                    dst = qR if qk == 0 else kR
                    dk = 'a_qR' if qk == 0 else 'a_kR'
                    Wn = Wns[qk]
                    if qk == 0:
                        load_w(Wns[0][:], w_in[l, :, 0:512], ('a_wn', 0))
                        load_w(Wns[1][:], w_in[l, :, 512:1024], ('a_wn', 1))
                    wn5 = Wn[:].rearrange("p k (h two j) -> p k h two j", two=2, j=32)
                    ws5 = Ws[:].rearrange("p k (h two j) -> p k h two j", two=2, j=32)
                    for half in range(2):
                        kb.op('dve', lambda e: e.tensor_copy(out=ws5[:, :, :, half, :], in_=wn5[:, :, :, 1 - half, :]), r=[('a_wn', qk)], w=['a_ws'])
                    for hp in range(4):
                        for c in range(NCH):
                            ba, bb = (pi % 2) * 2, (pi % 2) * 2 + 1
                            t1, t2 = t1s[pi % 2], t2s[pi % 2]
                            t1k, t2k = ('a_t1', pi % 2), ('a_t2', pi % 2)
                            pi += 1
                            for (bk, W, wk) in ((ba, Wn, ('a_wn', qk)), (bb, Ws, 'a_ws')):
                                for k in range(KT):
                                    kb.op('pe', lambda e: e.matmul(PS[bk][:], lhsT=W[:, k, hp * 128:(hp + 1) * 128], rhs=hnT[:, k, c * CH:(c + 1) * CH],
                                                                   start=(k == 0), stop=(k == KT - 1)), r=[wk, ('hnT', c)], w=[('ps', bk)])
                            kb.op('dve', lambda e: e.tensor_tensor(out=t1[:], in0=PS[ba][:], in1=cosT[:, c * CH:(c + 1) * CH], op=ALU.mult),
                                  r=[('ps', ba), 'a_cos'], w=[t1k])
                            kb.op('dve', lambda e: e.tensor_tensor(out=t2[:], in0=PS[bb][:], in1=sinT[:, c * CH:(c + 1) * CH], op=ALU.mult),
                                  r=[('ps', bb), 'a_sin'], w=[t2k])
                            kb.op('dve', lambda e: e.tensor_tensor(out=dst[:, hp, c * CH:(c + 1) * CH], in0=t1[:], in1=t2[:], op=ALU.add),
                                  r=[t1k, t2k], w=[(dk, hp)])
            kb.barrier()
            chk('aproj')
            with ExitStack() as s2:
                vb = [sb(s2, "a_vb%d" % i, [128, 16, 512], BF16) for i in range(3)]
                Wv = sb(s2, "a_wv", [128, KT, 512], BF16)
                numacc = sb(s2, "a_num", [128, S], F32)
                denacc = sb(s2, "a_den", [128, S], F32)
                load_w(Wv[:], w_in[l, :, 1024:1536], 'a_wv')
                mstrip = sb(s2, "a_mstrip", [128, 4, 512], BF16)
                kb.dma('pool', mstrip[:].rearrange("p a b -> p (a b)"), mstrip_d, w=['mstrip'])
                DIL = (1, 4, 16)
                pi = 0
                for bi, dil in enumerate(DIL):
                    nb = (S // dil) // 128
                    for ti in range(16):
                        r_, j_ = divmod(ti, nb)
                        t0 = r_ + dil * 128 * j_
                        b = pi % 2
                        pi += 1
                        for k in range(KT):
                            kb.op('pe', lambda e: e.matmul(PS[b][:], lhsT=hnT[:, k, ssl(t0, 128, dil)], rhs=Wv[:, k, :],
                                                           start=(k == 0), stop=(k == KT - 1)),
                                  r=['a_wv'] + [('hnT', c) for c in range(NCH)], w=[('ps', b)])
                        kb.op('act', lambda e: e.activation(out=vb[bi][:, ti, :], in_=PS[b][:], func=AF.Copy), r=[('ps', b)], w=[('a_vb', bi)])
                qc = [sb(s2, "a_qc%d" % i, [128, S], BF16) for i in range(2)]
                kc = [sb(s2, "a_kc%d" % i, [128, S], BF16) for i in range(2)]
                pT3 = [sb(s2, "a_pTb%d" % i, [128, 512], BF16) for i in range(3)]
                state = {'it': 0, 'cc': 0}

                def emit_S(w_):
                    (hp, bi, dil, nb, r_, q0, qn, kts, qsrc, ksrc, qk_, kk_, sub0) = w_['a']
                    it = w_['it']
                    bset = it % 2
                    p_ = pT3[it % 3]
                    pk = ('a_pT', it % 3)
                    nkt = len(kts)
                    wdt = nkt * qn
                    for hh in range(2):
                        base = 64 * hh
                        bank = 2 * bset + hh
                        for i_, (kt, mk) in enumerate(kts):
                            slot = i_ * qn
                            if dil == 1:
                                lhs = ksrc[base:base + 64, hp, 128 * kt:128 * kt + 128]
                                rhs = qsrc[base:base + 64, hp, q0:q0 + qn]
                            else:
                                lhs = ksrc[base:base + 64, sub0 + 128 * kt:sub0 + 128 * kt + 128]
                                rhs = qsrc[base:base + 64, sub0 + q0:sub0 + q0 + qn]
                            kb.op('pe', lambda e: e.matmul(PS[bank][:, slot:slot + qn], lhsT=lhs, rhs=rhs, start=True, stop=True),
                                  r=[kk_, qk_], w=[('ps', bank)], inc=(hh == 1 and i_ == nkt - 1))
                    ncols = 2 * wdt
                    sidx = 3 if kts[0][1] == 'E' else (1 if kts[0][1] == 'F' else (0 if nkt == 2 else 2))
                    kb.op('act', lambda e: e.activation(out=p_[:, 0:ncols].rearrange("p (h w) -> p h w", h=2),
                                                        in_=psall[:, 2 * bset:2 * bset + 2, 0:wdt], func=AF.Exp, scale=0.125),
                          r=[('ps', 2 * bset), ('ps', 2 * bset + 1)], w=[pk])
                    kb.op('dve', lambda e: e.tensor_tensor(out=p_[:, 0:ncols], in0=p_[:, 0:ncols], in1=mstrip[:, sidx, 0:ncols], op=ALU.mult),
                          r=['mstrip'], w=[pk])

                def emit_PV(w_):
                    (hp, bi, dil, nb, r_, q0, qn, kts, qsrc, ksrc, qk_, kk_, sub0) = w_['a']
                    it = w_['it']
                    bnk = {0: 4 + (2 * it) % 3, 128: 4 + (2 * it + 1) % 3}
                    p_ = pT3[it % 3]
                    pk = ('a_pT', it % 3)
                    nkt = len(kts)
                    qsl = ssl(r_ + dil * q0, qn, dil)
                    for (c0, is_num) in ((0, True), (128, False)):
                        for hh in range(2):
                            base = 64 * hh
                            for i_, (kt, mk) in enumerate(kts):
                                slot = (hh * nkt + i_) * qn
                                hcol = (hp * 2 + hh) * 64
                                lhs = vb[bi][:, r_ * nb + kt, hcol:hcol + 64] if is_num else ones_bf[:, 0:64]
                                kb.op('pe', lambda e: e.matmul(PS[bnk[c0]][base:base + 64, 0:qn], lhsT=lhs, rhs=p_[:, slot:slot + qn],
                                                               start=(i_ == 0), stop=(i_ == nkt - 1), tile_position=(0, base)),
                                      r=[pk, ('a_vb', bi), 'cstb'], w=[('ps', bnk[c0])], inc=(hh == 1 and i_ == nkt - 1))
                    for (c0, acc, ak, eng) in ((0, numacc, 'a_num', 'act' if bi == 0 else 'dve'), (128, denacc, 'a_den', 'act' if bi == 0 else 'dve')):
                        if bi == 0:
                            if eng == 'act':
                                kb.op('act', lambda e: e.activation(out=acc[:, qsl], in_=PS[bnk[c0]][:, 0:qn], func=AF.Copy),
                                      r=[('ps', bnk[c0])], w=[(ak, 0)], loose=True)
                            else:
                                kb.op('dve', lambda e: e.tensor_copy(out=acc[:, qsl], in_=PS[bnk[c0]][:, 0:qn]),
                                      r=[('ps', bnk[c0])], w=[(ak, 0)], loose=True)
                        else:
                            kb.op('dve', lambda e: e.tensor_tensor(out=acc[:, qsl], in0=PS[bnk[c0]][:, 0:qn], in1=acc[:, qsl], op=ALU.add),
                                  r=[('ps', bnk[c0]), (ak, bi - 1)], w=[(ak, bi)], loose=True)

                for hp in range(4):
                    items = []
                    for bi, dil in enumerate(DIL):
                        nsub = S // dil
                        nb = nsub // 128
                        if dil == 1:
                            qsrc, ksrc, qk_, kk_ = qR, kR, ('a_qR', hp), ('a_kR', hp)
                        else:
                            cc = state['cc'] % 2
                            state['cc'] += 1
                            qsrc, ksrc, qk_, kk_ = qc[cc], kc[cc], ('a_qc', cc), ('a_kc', cc)
                            kb.op('pool', lambda e: e.tensor_copy(out=qsrc[:].rearrange("p (r i) -> p r i", r=dil),
                                                                  in_=qR[:, hp, :].rearrange("p (i r) -> p r i", r=dil)),
                                  r=[('a_qR', hp)], w=[qk_])
                            kb.op('pool', lambda e: e.tensor_copy(out=ksrc[:].rearrange("p (r i) -> p r i", r=dil),
                                                                  in_=kR[:, hp, :].rearrange("p (i r) -> p r i", r=dil)),
                                  r=[('a_kR', hp)], w=[kk_])
                        if nb == 1:
                            blocks = [(0, 128, [(0, 'E')])]
                        else:
                            blocks = [(0, 64, [(0, 'F')])]
                            blocks += [(64 + 128 * j, 128, [(j, 'A'), (j + 1, 'B')]) for j in range(nb - 1)]
                            blocks += [(nsub - 64, 64, [(nb - 1, 'A')])]
                        for r_ in range(dil):
                            for (q0, qn, kts) in blocks:
                                items.append({'a': (hp, bi, dil, nb, r_, q0, qn, kts, qsrc, ksrc, qk_, kk_, r_ * nsub), 'it': state['it']})
                                state['it'] += 1
                    prev = None
                    for w_ in items:
                        emit_S(w_)
                        if prev is not None:
                            emit_PV(prev)
                        prev = w_
                    emit_PV(prev)
                    nk = [('a_num', b_) for b_ in range(3)]
                    dk_ = [('a_den', b_) for b_ in range(3)]
                    kb.op('act', lambda e: e.activation(out=denacc[:], in_=denacc[:], func=AF.Ln), r=dk_, w=dk_)
                    kb.op('act', lambda e: e.activation(out=denacc[:], in_=denacc[:], func=AF.Exp, scale=-1.0), r=dk_, w=dk_)
                    kb.op('dve', lambda e: e.tensor_tensor(out=catT[:, hp, :], in0=numacc[:], in1=denacc[:], op=ALU.mult),
                          r=nk + dk_, w=[('catT', hp)] + nk)
            kb.barrier()
            chk('attn')
            st.close()

        def ffn_phase(st0, l, xsrc, xdst, hnT):
            st = st0.enter_context(ExitStack())
            aT = sb(st, "f_aT", [128, NFT, S], BF16)
            with ExitStack() as s1:
                Wgs = [sb(s1, "f_wg%d" % i, [128, KT, 512], BF16) for i in range(2)]
                Wvs = [sb(s1, "f_wv%d" % i, [128, KT, 512], BF16) for i in range(2)]
                yp = [sb(s1, "f_yp%d" % i, [128, S + 2], F32) for i in range(2)]
                u = [sb(s1, "f_u%d" % i, [128, S], F32) for i in range(2)]
                gl = sb(s1, "f_gl", [128, S], BF16)
                for i in range(2):
                    kb.op('dve', lambda e: e.memset(yp[i][:, 0:1], 0.0), w=[('f_yp', i)])
                    kb.op('dve', lambda e: e.memset(yp[i][:, S + 1:S + 2], 0.0), w=[('f_yp', i)])
                pi = 0
                for c in range(NFT):
                    g4 = c % 4

                    def ldgrp(c_):
                        gi = (c_ // 4) % 2
                        n = min(4, NFT - c_) * 128
                        load_w(Wgs[gi][:, :, 0:n], w_up[l, :, c_ * 128:c_ * 128 + n], ('f_wg', gi))
                        load_w(Wvs[gi][:, :, 0:n], w_up[l, :, DFF + c_ * 128:DFF + c_ * 128 + n], ('f_wv', gi))
                    if c == 0:
                        ldgrp(0)
                    if g4 == 0 and c + 4 < NFT:
                        ldgrp(c + 4)
                    gi_ = (c // 4) % 2
                    Wg, Wv = Wgs[gi_], Wvs[gi_]
                    for part, (W, wk) in enumerate(((Wg, ('f_wg', gi_)), (Wv, ('f_wv', gi_)))):
                        ti = part * NFT + c
                        for ch in range(NCH):
                            b = pi % 4
                            pi += 1
                            for k in range(KT):
                                kb.op('pe', lambda e: e.matmul(PS[b][:], lhsT=W[:, k, g4 * 128:(g4 + 1) * 128], rhs=hnT[:, k, ch * CH:(ch + 1) * CH],
                                                               start=(k == 0), stop=(k == KT - 1)), r=[wk, ('hnT', ch)], w=[('ps', b)])
                            kb.op('act', lambda e: e.activation(out=yp[part][:, 1 + ch * CH:1 + (ch + 1) * CH], in_=PS[b][:], func=AF.Copy),
                                  r=[('ps', b)], w=[('f_yp', part)])
                        kb.op('act', lambda e: e.activation(out=u[part][:], in_=yp[part][:, 0:S], func=AF.Identity,
                                                            scale=fconv[:, l, ti, 0:1], bias=fconv[:, l, ti, 3:4]),
                              r=[('f_yp', part), 'fconv'], w=[('f_u', part)])
                        for jj in (1, 2):
                            kb.op('dve', lambda e: e.scalar_tensor_tensor(out=u[part][:], in0=yp[part][:, jj:jj + S], scalar=fconv[:, l, ti, jj:jj + 1],
                                                                          in1=u[part][:], op0=ALU.mult, op1=ALU.add),
                                  r=[('f_yp', part), 'fconv'], w=[('f_u', part)])
                    kb.op('act', lambda e: e.activation(out=gl[:], in_=u[0][:], func=AF.Gelu_apprx_tanh), r=[('f_u', 0)], w=['f_gl'])
                    kb.op('dve', lambda e: e.tensor_tensor(out=aT[:, c, :], in0=gl[:], in1=u[1][:], op=ALU.mult),
                          r=['f_gl', ('f_u', 1)], w=[('f_aT', c)])
            kb.barrier()
            chk('fup')
            with ExitStack() as s2:
                last = (l == nlayers - 1)
                epilogue(s2, w_down[l], NFT, lambda ci, tsl: aT[:, ci, tsl], lambda ci, c: [('f_aT', ci)], xsrc, xdst, l, 3,
                         che=256, hn_out=None if last else hnT, nxt=None if last else (l + 1, 0))
            kb.barrier()
            st.close()

        xcur = xT_in
        hnT = sb(top, "hnT", [128, KT, S], BF16)
        try:
          for l in range(nlayers if stop_at != 'init' else 0):
              xmid = xs[0]
              xnext = yT if l == nlayers - 1 else xs[1]
              with ExitStack() as sm_:
                  catT = sb(sm_, "catT", [128, KT, S], BF16)
                  with ExitStack() as sh_:
                      if l == 0:
                          with ExitStack() as s0:
                              norm_pre(s0, xcur, l, 0, hnT)
                      kb.barrier()
                      chk('norm')
                      mlstm_phase(sh_, l, hnT, catT)
                      attn_phase(sh_, l, hnT, catT)
                  if 'cat' in dbg_out and l == dbg.get('_layer', 0):
                      with ExitStack() as sd:
                          tmpf = sb(sd, "dbg_tmp", [128, S], F32)
                          for i in range(KT):
                              kb.op('act', lambda e: e.activation(out=tmpf[:], in_=catT[:, i, :], func=AF.Copy), r=[('catT', i)], w=['dbg_tmp'])
                              kb.dma('sp', dbg_out['cat'][i * 128:(i + 1) * 128, :], tmpf[:], r=['dbg_tmp'], w=['dbg_cat'])
                      kb.barrier()
                  with ExitStack() as se:
                      epilogue(se, w_out[l], KT, lambda ci, tsl: catT[:, ci, tsl], lambda ci, c: [('catT', ci)], xcur, xmid, l, 1,
                               che=CH, hn_out=hnT, nxt=(l, 2))
                  kb.barrier()
                  chk('ep1')
              with ExitStack() as sf:
                  ffn_phase(sf, l, xmid, xnext, hnT)
              xcur = xnext
        except _Stop:
            pass
        kb.barrier()
    stuck = kb.check_deadlock()
    if stuck:
        raise RuntimeError('static deadlock: %r' % (stuck,))
    return nc, list(dbg_out.keys())


def _host_prep(inputs):
    f = np.float32
    g = np.stack([inputs['mix_pre_g'], inputs['mix_post_g'], inputs['ffn_pre_g'], inputs['ffn_post_g']], axis=1)
    gvec = np.ascontiguousarray(g.reshape(2, 4, KT, 128).transpose(3, 0, 1, 2)).reshape(128, -1).astype(f)
    mc = np.concatenate([inputs['mlstm_conv_w'], inputs['mlstm_conv_b'][:, None, :]], axis=1)
    mconv = np.ascontiguousarray(mc.reshape(2, 6, 8, 128).transpose(3, 0, 2, 1)).reshape(128, -1).astype(f)
    fc = np.concatenate([inputs['ffn_conv_w'], inputs['ffn_conv_b'][:, None, :]], axis=1)
    fconv = np.ascontiguousarray(fc.reshape(2, 4, 44, 128).transpose(3, 0, 2, 1)).reshape(128, -1).astype(f)
    gateb = np.ascontiguousarray(np.broadcast_to(inputs['mlstm_gate_b'].reshape(1, -1), (128, 32))).astype(f)
    headg = np.ascontiguousarray(np.broadcast_to(inputs['mlstm_head_g'].reshape(1, -1), (128, 1024))).astype(f)
    p = np.arange(128)
    inv_freq = (10000.0 ** (-np.arange(0, 64, 2, dtype=np.float32) / 64)).astype(np.float32)
    ang = np.arange(S, dtype=np.float32)[None, :] * inv_freq[p % 32][:, None]
    cosT = np.cos(ang).astype(f)
    sgn = np.where((p % 64) < 32, -1.0, 1.0).astype(f)[:, None]
    sinT = (np.sin(ang) * sgn).astype(f)
    a = p[:, None]
    b = p[None, :]
    NEG = -30000.0
    cst = np.zeros((128, 9, 128), f)
    cst[:, 0] = 1.0
    cst[:, 1] = (a == b)
    cst[:, 2] = np.where(a >= b, 0.0, NEG)
    cst[:, 3] = np.where(a <= b, 0.0, NEG)
    cst[:, 4] = np.where(a <= b + 64, 0.0, NEG)
    cst[:, 5] = np.where(np.abs(a - b) <= 64, 0.0, NEG)
    cst[:, 6] = (a <= b)
    cst[:, 7] = (a >= b)
    A01 = (a >= b).astype(f); B01 = (a <= b).astype(f); F01 = (a <= b + 64).astype(f); E01 = (np.abs(a - b) <= 64).astype(f)
    ms = np.zeros((128, 4, 512), f)
    ms[:, 0] = np.concatenate([A01, B01, A01, B01], axis=1)
    ms[:, 1, 0:128] = np.concatenate([F01[:, :64], F01[:, :64]], axis=1)
    ms[:, 2, 0:128] = np.concatenate([A01[:, :64], A01[:, :64]], axis=1)
    ms[:, 3, 0:256] = np.concatenate([E01, E01], axis=1)
    shared = dict(mstrip=ms.reshape(128, -1), w_in=np.ascontiguousarray(inputs['w_in'], dtype=f), w_out=np.ascontiguousarray(inputs['w_out'], dtype=f),
                  w_up=np.ascontiguousarray(inputs['w_up'], dtype=f), w_down=np.ascontiguousarray(inputs['w_down'], dtype=f),
                  gvec=gvec, mconv=mconv, fconv=fconv, gateb=gateb, headg=headg, cosT=cosT, sinT=sinT,
                  cst=cst.reshape(128, -1))
    return shared


_NC_CACHE = {}


def kernel(**inputs):
    x = np.asarray(inputs['x'], dtype=np.float32)
    B = x.shape[0]
    shared = _host_prep(inputs)
    if 'nc' not in _NC_CACHE:
        _NC_CACHE['nc'] = build(2)[0]
    nc = _NC_CACHE['nc']
    in_maps = []
    for b in range(B):
        m = dict(shared)
        m['xT'] = np.ascontiguousarray(x[b].T)
        in_maps.append(m)
    res = run_bass_kernel_spmd(nc, in_maps, core_ids=list(range(B)))
    out = np.stack([np.ascontiguousarray(res.results[b]['yT'].T) for b in range(B)], axis=0)
    return out.astype(np.float32)
```

```python
import math
from contextlib import ExitStack
import numpy as np
import concourse.bass as bass
import concourse.mybir as mybir
from concourse.bass_utils import run_bass_kernel_spmd

F32 = mybir.dt.float32
BF16 = mybir.dt.bfloat16
AF = mybir.ActivationFunctionType
ALU = mybir.AluOpType

S = 2048
D = 1024
KT = 8
NCH = 4
CH = 512
DFF = 2816
NFT = 22
EPS = 1e-6
NDS = 24


def ssl(start, n, step):
    return slice(start, start + step * (n - 1) + 1, step)


class KB:
    def __init__(self, nc):
        self.nc = nc
        self.E = {'pe': nc.tensor, 'act': nc.scalar, 'dve': nc.vector, 'pool': nc.gpsimd, 'sp': nc.sync}
        self.sem = {e: nc.alloc_semaphore('c_' + e) for e in ('pe', 'act', 'dve', 'pool')}
        self.cnt = {e: 0 for e in self.sem}
        self.waited = {e: {} for e in self.E}
        self.dsems = [nc.alloc_semaphore('d%d' % i) for i in range(NDS)]
        self.dcnt = [0] * NDS
        self.dlast = [None] * NDS
        self.dnext = 0
        self.dnext_p = 0
        self.tk = {}
        self.dead = False
        self.prog = {e: [] for e in self.E}

    def _t(self, key):
        t = self.tk.get(key)
        if t is None:
            t = {'w': None, 'r': {}}
            self.tk[key] = t
        return t

    def _wait(self, e, ev):
        sem, val, sid = ev
        if self.waited[e].get(sid, 0) >= val:
            return
        self.E[e].wait_ge(sem, val)
        self.prog[e].append(('w', sid, val))
        self.waited[e][sid] = val

    def _deps(self, e, r, w, loose=False):
        evs = {}

        def add(ev):
            if ev is None:
                return
            if ev[2] not in evs or evs[ev[2]][1] < ev[1]:
                evs[ev[2]] = ev
        for k in r:
            add(self._t(k)['w'])
        for k in w:
            t = self._t(k)
            add(t['w'])
            for ev in t['r'].values():
                add(ev)
        for sid, ev in evs.items():
            if sid == e and (e == 'pe' or loose):
                continue
            self._wait(e, ev)

    def _record(self, ev, r, w):
        for k in r:
            self._t(k)['r'][ev[2]] = ev
        for k in w:
            t = self._t(k)
            t['w'] = ev
            t['r'] = {}

    def op(self, e, fn, r=(), w=(), loose=False, inc=True):
        if self.dead:
            return
        self._deps(e, r, w, loose)
        ins = fn(self.E[e])
        if inc:
            self.cnt[e] += 1
            ins.then_inc(self.sem[e], 1)
            self.prog[e].append(('i', e, 1))
            self._record((self.sem[e], self.cnt[e], e), r, w)
        else:
            self._record((self.sem[e], self.cnt[e] + 1, e), r, w)

    def dma(self, q, out, in_, r=(), w=()):
        if self.dead:
            return
        if q == 'pool':
            i = 16 + self.dnext_p
            self.dnext_p = (self.dnext_p + 1) % (NDS - 16)
        else:
            i = self.dnext
            self.dnext = (i + 1) % 16
        if self.dlast[i] is not None:
            self._wait(q, self.dlast[i])
        self._deps(q, r, w)
        ins = self.E[q].dma_start(out=out, in_=in_)
        self.dcnt[i] += 16
        ins.then_inc(self.dsems[i], 16)
        self.prog[q].append(('i', 'd%d' % i, 16))
        ev = (self.dsems[i], self.dcnt[i], 'd%d' % i)
        self.dlast[i] = ev
        self._record(ev, r, w)
        return ev

    def check_deadlock(self):
        pc = {e: 0 for e in self.prog}
        val = {}
        progress = True
        while progress:
            progress = False
            for e, p in self.prog.items():
                while pc[e] < len(p):
                    k, sid, v = p[pc[e]]
                    if k == 'w':
                        if val.get(sid, 0) < v:
                            break
                    else:
                        val[sid] = val.get(sid, 0) + v
                    pc[e] += 1
                    progress = True
        stuck = {e: (pc[e], len(p), p[pc[e]]) for e, p in self.prog.items() if pc[e] < len(p)}
        return stuck

    def barrier(self):
        if self.dead:
            return
        evs = [(self.sem[e], self.cnt[e], e) for e in self.sem if self.cnt[e] > 0]
        evs += [ev for ev in self.dlast if ev is not None]
        for e in self.E:
            for ev in evs:
                if ev[2] == e:
                    continue
                self._wait(e, ev)


class _Stop(Exception):
    pass


def build(nlayers=2, dbg=None, stop_at=None):
    dbg = dbg or {}
    nc = bass.Bass("TRN2", target_bir_lowering=False)
    kb = KB(nc)
    ein = lambda n, s, dt=F32: nc.dram_tensor(n, list(s), dt, kind="ExternalInput").ap()
    xT_in = ein("xT", [D, S])
    w_in = ein("w_in", [2, D, 3600])
    w_out = ein("w_out", [2, D, D])
    w_up = ein("w_up", [2, D, 2 * DFF])
    w_down = ein("w_down", [2, DFF, D])
    gvec_d = ein("gvec", [128, 2 * 4 * KT])
    mconv_d = ein("mconv", [128, 2 * 8 * 6])
    fconv_d = ein("fconv", [128, 2 * 44 * 4])
    gateb_d = ein("gateb", [128, 2 * 16])
    headg_d = ein("headg", [128, 2 * 512])
    cos_d = ein("cosT", [128, S])
    sin_d = ein("sinT", [128, S])
    mstrip_d = ein("mstrip", [128, 4 * 512])
    cst_d = ein("cst", [128, 9 * 128])
    yT = nc.dram_tensor("yT", [D, S], F32, kind="ExternalOutput").ap()
    xs = [nc.dram_tensor("xs%d" % i, [D, S], F32).ap() for i in range(2)]
    dbg_out = {}
    for name, shape in dbg.items():
        dbg_out[name] = nc.dram_tensor("dbg_" + name, list(shape), F32, kind="ExternalOutput").ap()

    pm = lambda ap: ap.rearrange("(k p) t -> p k t", p=128)

    with ExitStack() as top:
        uid = [0]

        def sb(st, name, shape, dt):
            uid[0] += 1
            return st.enter_context(nc.sbuf_tensor("s%d_%s" % (uid[0], name), list(shape), dt))
        gvec = sb(top, "gvec", [128, 2, 4, KT], F32)
        mconv = sb(top, "mconv", [128, 2, 8, 6], F32)
        fconv = sb(top, "fconv", [128, 2, 44, 4], F32)
        gateb = sb(top, "gateb", [128, 2, 16], F32)
        headg = sb(top, "headg", [128, 2, 512], BF16)
        cstb = sb(top, "cstb", [128, 9, 128], BF16)
        epsb = sb(top, "epsb", [128, 1], F32)
        psall = top.enter_context(nc.psum_tensor("psall", [128, 7, 512], F32))
        PS = [psall[:, i, :] for i in range(7)]
        PST = top.enter_context(nc.psum_tensor("pst", [128, 1024], BF16))
        for i in range(7):
            kb._t(('ps', i))
        kb.dma('sp', gvec[:].rearrange("p a b c -> p (a b c)"), gvec_d, w=['gvec'])
        kb.dma('sp', mconv[:].rearrange("p a b c -> p (a b c)"), mconv_d, w=['mconv'])
        kb.dma('sp', fconv[:].rearrange("p a b c -> p (a b c)"), fconv_d, w=['fconv'])
        kb.dma('sp', gateb[:].rearrange("p a b -> p (a b)"), gateb_d, w=['gateb'])
        kb.dma('pool', headg[:].rearrange("p a b -> p (a b)"), headg_d, w=['headg'])
        kb.dma('pool', cstb[:].rearrange("p a b -> p (a b)"), cst_d, w=['cstb'])
        cview = cst_d.rearrange("p (a b) -> p a b", b=128)
        kb.op('dve', lambda e: e.memset(epsb[:], EPS), w=['epsb'])
        ones_bf = cstb[:, 0, :]
        ident_bf = cstb[:, 1, :]
        MASK = {'A': cstb[:, 2, :], 'B': cstb[:, 3, :], 'F': cstb[:, 4, :], 'E': cstb[:, 5, :]}
        tri_bf = [cstb[:, 6, :], cstb[:, 7, :]]

        def chk(name):
            if stop_at == name:
                kb.barrier()
                kb.dead = True

        def dump(name, sb_ap, rkeys):
            if name in dbg_out:
                kb.dma('sp', dbg_out[name], sb_ap, r=rkeys, w=['dbg_' + name])

        def rstd_from_ps(ps_i, rstd_ap, rkey, n):
            kb.op('act', lambda e: e.activation(out=rstd_ap, in_=PS[ps_i][:, 0:rstd_ap.shape[1]], func=AF.Ln,
                                                scale=1.0 / n, bias=epsb[:, 0:1]),
                  r=[('ps', ps_i), 'epsb'], w=[rkey])
            kb.op('act', lambda e: e.activation(out=rstd_ap, in_=rstd_ap, func=AF.Exp, scale=-0.5), r=[rkey], w=[rkey])

        def norm_pre(st, xsrc, l, j, hnT):
            xcs = [sb(st, "np_xc%d" % i, [128, KT, CH], F32) for i in range(2)]
            sq = sb(st, "np_sq", [128, KT, CH], BF16)
            rstd = sb(st, "np_rstd", [128, CH], F32)
            for c in range(NCH):
                xc = xcs[c % 2]
                xk = ('np_xc', c % 2)
                kb.dma('sp', xc[:], pm(xsrc)[:, :, c * CH:(c + 1) * CH], w=[xk])
                kb.op('act', lambda e: e.activation(out=sq[:], in_=xc[:], func=AF.Square), r=[xk], w=['np_sq'])
                for k in range(KT):
                    kb.op('pe', lambda e: e.matmul(PS[6][:], lhsT=ones_bf, rhs=sq[:, k, :], start=(k == 0), stop=(k == KT - 1)),
                          r=['np_sq', 'cstb'], w=[('ps', 6)])
                rstd_from_ps(6, rstd[:], 'np_rstd', D)
                for k in range(KT):
                    kb.op('dve', lambda e: e.scalar_tensor_tensor(out=hnT[:, k, c * CH:(c + 1) * CH], in0=xc[:, k, :],
                                                                  scalar=gvec[:, l, j, k:k + 1], in1=rstd[:],
                                                                  op0=ALU.mult, op1=ALU.mult),
                          r=[xk, 'np_rstd', 'gvec'], w=[('hnT', c)])

        def load_w(dst, src_rows_cols, wkey):
            kb.dma('pool', dst, src_rows_cols.rearrange("(k p) c -> p k c", p=128), w=[wkey])

        def epilogue(st, Wd, nct, rhs_fn, rkeys_fn, xsrc, xdst, l, j, che=CH, hn_out=None, nxt=None):
            W = sb(st, "ep_w", [128, nct, D], BF16)
            xcs = [sb(st, "ep_xc%d" % i, [128, KT, che], F32) for i in range(2)]
            ff = sb(st, "ep_ff", [128, KT, che], F32)
            sq = sb(st, "ep_sq", [128, KT, che], BF16)
            rstd = sb(st, "ep_rstd", [128, che], F32)
            if hn_out is not None:
                sq2 = sb(st, "ep_sq2", [128, KT, che], BF16)
                rstd2 = sb(st, "ep_rstd2", [128, che], F32)
            for qt in range(4):
                for k0 in range(0, nct, 8):
                    k1 = min(nct, k0 + 8)
                    load_w(W[:, k0:k1, qt * 256:(qt + 1) * 256], Wd[k0 * 128:k1 * 128, qt * 256:(qt + 1) * 256], ('ep_w', qt))
            state = {'pi': 0}
            nchunk = S // che

            def partA(c):
                tsl = slice(c * che, (c + 1) * che)
                xc = xcs[c % 2]
                xk = ('ep_xc', c % 2)
                kb.dma('sp', xc[:], pm(xsrc)[:, :, tsl], w=[xk])
                for m in range(KT):
                    b = state['pi'] % 4
                    state['pi'] += 1
                    for ci in range(nct):
                        kb.op('pe', lambda e: e.matmul(PS[b][:, 0:che], lhsT=W[:, ci, m * 128:(m + 1) * 128], rhs=rhs_fn(ci, tsl),
                                                       start=(ci == 0), stop=(ci == nct - 1)),
                              r=[('ep_w', m // 2)] + rkeys_fn(ci, c), w=[('ps', b)], inc=(ci == nct - 1))
                    kb.op('act', lambda e: e.activation(out=ff[:, m, :], in_=PS[b][:, 0:che], func=AF.Copy), r=[('ps', b)], w=[('ep_ff', m)])
                    kb.op('act', lambda e: e.activation(out=sq[:, m, :], in_=ff[:, m, :], func=AF.Square), r=[('ep_ff', m)], w=[('ep_sq', m)])

            def partB(c):
                tsl = slice(c * che, (c + 1) * che)
                xc = xcs[c % 2]
                xk = ('ep_xc', c % 2)
                for m in range(KT):
                    kb.op('pe', lambda e: e.matmul(PS[6][:, 0:che], lhsT=ones_bf, rhs=sq[:, m, :], start=(m == 0), stop=(m == KT - 1)),
                          r=[('ep_sq', m), 'cstb'], w=[('ps', 6)], inc=(m == KT - 1))
                rstd_from_ps(6, rstd[:], 'ep_rstd', D)
                for m in range(KT):
                    kb.op('dve', lambda e: e.scalar_tensor_tensor(out=ff[:, m, :], in0=ff[:, m, :], scalar=gvec[:, l, j, m:m + 1],
                                                                  in1=rstd[:], op0=ALU.mult, op1=ALU.mult),
                          r=['ep_rstd', 'gvec'], w=[('ep_ff', m)])
                    kb.op('dve', lambda e: e.tensor_tensor(out=xc[:, m, :], in0=xc[:, m, :], in1=ff[:, m, :], op=ALU.add),
                          r=[('ep_ff', m)], w=[xk])
                kb.dma('sp', pm(xdst)[:, :, tsl], xc[:], r=[xk], w=[('xdst', c)])

            def partC(c):
                if hn_out is None:
                    return
                tsl = slice(c * che, (c + 1) * che)
                xc = xcs[c % 2]
                xk = ('ep_xc', c % 2)
                l2, j2 = nxt
                kb.op('act', lambda e: e.activation(out=sq2[:], in_=xc[:], func=AF.Square), r=[xk], w=['ep_sq2'])
                for m in range(KT):
                    kb.op('pe', lambda e: e.matmul(PS[5][:, 0:che], lhsT=ones_bf, rhs=sq2[:, m, :], start=(m == 0), stop=(m == KT - 1)),
                          r=['ep_sq2', 'cstb'], w=[('ps', 5)], inc=(m == KT - 1))
                rstd_from_ps(5, rstd2[:], 'ep_rstd2', D)
                for m in range(KT):
                    kb.op('dve', lambda e: e.scalar_tensor_tensor(out=hn_out[:, m, tsl], in0=xc[:, m, :], scalar=gvec[:, l2, j2, m:m + 1],
                                                                  in1=rstd2[:], op0=ALU.mult, op1=ALU.mult),
                          r=[xk, 'ep_rstd2', 'gvec'], w=[('hnT', (c * che) // CH)])

            partA(0)
            for c in range(nchunk):
                partB(c)
                if c + 1 < nchunk:
                    partA(c + 1)
                partC(c)

        def mlstm_phase(st0, l, hnT, catT):
            st = st0.enter_context(ExitStack())
            qkT = sb(st, "m_qkT", [128, 8, S], BF16)
            kTM = sb(st, "m_kTM", [128, 16, 512], BF16)
            vaug = sb(st, "m_vaug", [128, 16, 4, 132], BF16)
            gsig = sb(st, "m_gsig", [128, 16, 512], BF16)
            gates = sb(st, "m_gates", [128, 16, 16], F32)
            l1 = sb(st, "m_l1", [128, 16, 8], F32)
            gd = sb(st, "m_gd", [128, 5, 2, 16, 4], F32)
            with ExitStack() as sp_:
                Wa = sb(sp_, "m_wa", [128, KT, 512], BF16)
                Wb = sb(sp_, "m_wb", [128, KT, 512], BF16)
                Wg = sb(sp_, "m_wg", [128, KT, 16], BF16)
                ypads = [sb(sp_, "m_ypad%d" % i, [128, S + 4], BF16) for i in range(2)]
                uaccs = [sb(sp_, "m_uacc%d" % i, [128, S], F32) for i in range(2)]
                sgt = sb(sp_, "m_sgt", [128, 512], BF16)
                load_w(Wa[:], w_in[l, :, 1536:2048], 'm_wa')
                load_w(Wb[:], w_in[l, :, 2048:2560], 'm_wb')
                load_w(Wg[:], w_in[l, :, 3584:3600], 'm_wg')
                for i in range(2):
                    kb.op('dve', lambda e: e.memset(ypads[i][:, 0:2], 0.0), w=[('m_ypad', i)])
                    kb.op('dve', lambda e: e.memset(ypads[i][:, S + 2:S + 4], 0.0), w=[('m_ypad', i)])
                kb.op('dve', lambda e: e.memset(vaug[:, :, :, 128:129], 1.0), w=['m_vaug1'])
                cst_ = {'pi': 0}

                def convA(i):
                    W = Wa if i < 4 else Wb
                    wk = 'm_wa' if i < 4 else 'm_wb'
                    cs = (i % 4) * 128
                    ypad, uacc = ypads[i % 2], uaccs[i % 2]
                    yk, uk = ('m_ypad', i % 2), ('m_uacc', i % 2)
                    for c in range(NCH):
                        b = cst_['pi'] % 4
                        cst_['pi'] += 1
                        for k in range(KT):
                            kb.op('pe', lambda e: e.matmul(PS[b][:], lhsT=W[:, k, cs:cs + 128], rhs=hnT[:, k, c * CH:(c + 1) * CH],
                                                           start=(k == 0), stop=(k == KT - 1)),
                                  r=[wk, ('hnT', c)], w=[('ps', b)], inc=(k == KT - 1))
                        kb.op('act', lambda e: e.activation(out=ypad[:, 2 + c * CH:2 + (c + 1) * CH], in_=PS[b][:], func=AF.Copy),
                              r=[('ps', b)], w=[yk])
                    kb.op('act', lambda e: e.activation(out=uacc[:], in_=ypad[:, 0:S], func=AF.Identity,
                                                        scale=mconv[:, l, i, 0:1], bias=mconv[:, l, i, 5:6]),
                          r=[yk, 'mconv'], w=[uk])

                def convB(i):
                    ypad, uacc = ypads[i % 2], uaccs[i % 2]
                    yk, uk = ('m_ypad', i % 2), ('m_uacc', i % 2)
                    for jj in range(1, 5):
                        kb.op('dve', lambda e: e.scalar_tensor_tensor(out=uacc[:], in0=ypad[:, jj:jj + S], scalar=mconv[:, l, i, jj:jj + 1],
                                                                      in1=uacc[:], op0=ALU.mult, op1=ALU.add),
                              r=[yk, 'mconv'], w=[uk])

                def convC(i):
                    uacc, uk = uaccs[i % 2], ('m_uacc', i % 2)
                    kb.op('act', lambda e: e.activation(out=qkT[:, i, :], in_=uacc[:], func=AF.Silu), r=[uk], w=[('m_qkT', i)])

                convA(0)
                for i in range(8):
                    convB(i)
                    if i + 1 < 8:
                        convA(i + 1)
                    convC(i)
                if 'qk' in dbg_out:
                    for i in range(8):
                        kb.op('act', lambda e: e.activation(out=uaccs[0][:], in_=qkT[:, i, :], func=AF.Copy), r=[('m_qkT', i)], w=[('m_uacc', 0)])
                        dump_ap = dbg_out['qk'][i * 128:(i + 1) * 128, :]
                        kb.dma('sp', dump_ap, uaccs[0][:], r=[('m_uacc', 0)], w=['dbg_qk'])
                for c in range(16):
                    for h in range(4):
                        kb.op('pe', lambda e: e.transpose(PST[:, h * 128:(h + 1) * 128], qkT[:, 4 + h, c * 128:(c + 1) * 128], ident_bf),
                              r=[('m_qkT', 4 + h), 'cstb'], w=['pst'])
                    kb.op('act', lambda e: e.activation(out=kTM[:, c, :], in_=PST[:, 0:512], func=AF.Copy), r=['pst'], w=[('m_kTM', c)])
                load_w(Wa[:], w_in[l, :, 2560:3072], 'm_wa')
                load_w(Wb[:], w_in[l, :, 3072:3584], 'm_wb')
                for c in range(16):
                    tk = ('hnT', c // 4)
                    for k in range(KT):
                        kb.op('pe', lambda e: e.matmul(PS[0][:], lhsT=hnT[:, k, c * 128:(c + 1) * 128], rhs=Wa[:, k, :],
                                                       start=(k == 0), stop=(k == KT - 1)), r=['m_wa', tk], w=[('ps', 0)])
                    kb.op('act', lambda e: e.activation(out=vaug[:, c, :, 0:128], in_=PS[0][:].rearrange("p (h f) -> p h f", f=128), func=AF.Copy),
                          r=[('ps', 0)], w=[('m_vaug', c)])
                    for k in range(KT):
                        kb.op('pe', lambda e: e.matmul(PS[1][:], lhsT=hnT[:, k, c * 128:(c + 1) * 128], rhs=Wb[:, k, :],
                                                       start=(k == 0), stop=(k == KT - 1)), r=['m_wb', tk], w=[('ps', 1)])
                    kb.op('act', lambda e: e.activation(out=sgt[:], in_=PS[1][:], func=AF.Sigmoid), r=[('ps', 1)], w=['m_sgt'])
                    kb.op('dve', lambda e: e.tensor_tensor(out=gsig[:, c, :], in0=sgt[:], in1=headg[:, l, :], op=ALU.mult),
                          r=['m_sgt', 'headg'], w=[('m_gsig', c)])
                    for k in range(KT):
                        kb.op('pe', lambda e: e.matmul(PS[2][:, 0:16], lhsT=hnT[:, k, c * 128:(c + 1) * 128], rhs=Wg[:, k, :],
                                                       start=(k == 0), stop=(k == KT - 1)), r=['m_wg', tk], w=[('ps', 2)])
                    kb.op('dve', lambda e: e.tensor_tensor(out=gates[:, c, :], in0=PS[2][:, 0:16], in1=gateb[:, l, :], op=ALU.add),
                          r=[('ps', 2), 'gateb'], w=['m_gates'])
                for d_ in range(2):
                    kb.op('act', lambda e: e.activation(out=l1[:, :, 4 * d_:4 * d_ + 4], in_=gates[:, :, 8 * d_ + 4:8 * d_ + 8], func=AF.Exp, scale=-1.0),
                          r=['m_gates'], w=['m_l1'])
                kb.op('act', lambda e: e.activation(out=l1[:], in_=l1[:], func=AF.Ln, bias=1.0), r=['m_l1'], w=['m_l1'])
                l1h = sb(sp_, "m_l1h", [128, 2, 16, 8], BF16)
                l1r = sb(sp_, "m_l1r", [128, 16, 8], F32)
                kb.op('dve', lambda e: e.tensor_copy(out=l1h[:, 0], in_=l1[:]), r=['m_l1'], w=['m_l1h'])
                kb.op('dve', lambda e: e.tensor_copy(out=l1r[:], in_=l1h[:, 0]), r=['m_l1h'], w=['m_l1r'])
                kb.op('dve', lambda e: e.tensor_tensor(out=l1r[:], in0=l1[:], in1=l1r[:], op=ALU.subtract), r=['m_l1', 'm_l1r'], w=['m_l1r'])
                kb.op('dve', lambda e: e.tensor_copy(out=l1h[:, 1], in_=l1r[:]), r=['m_l1r'], w=['m_l1h'])
                for d_ in range(2):
                    for (bank, lhs) in ((3, tri_bf[d_]), (4, ones_bf)):
                        for hl in range(2):
                            kb.op('pe', lambda e: e.matmul(PS[bank][:, 64 * d_:64 * d_ + 64].rearrange("p (c h) -> p c h", h=4), lhsT=lhs,
                                                           rhs=l1h[:, hl, :, 4 * d_:4 * d_ + 4], start=(hl == 0), stop=(hl == 1)),
                                  r=['m_l1h', 'cstb'], w=[('ps', bank)])
                    na = gd[:, 0, d_]
                    kb.op('act', lambda e: e.activation(out=na, in_=PS[3][:, 64 * d_:64 * d_ + 64].rearrange("p (c h) -> p c h", h=4), func=AF.Copy),
                          r=[('ps', 3)], w=['m_gd'])
                    kb.op('dve', lambda e: e.tensor_tensor(out=gd[:, 1, d_], in0=gates[:, :, 8 * d_:8 * d_ + 4], in1=na, op=ALU.add),
                          r=['m_gates', 'm_gd'], w=['m_gd'])
                    kb.op('act', lambda e: e.activation(out=gd[:, 1, d_], in_=gd[:, 1, d_], func=AF.Exp, bias=-0.5 * math.log(128.0)),
                          r=['m_gd'], w=['m_gd'])
                    kb.op('act', lambda e: e.activation(out=gd[:, 2, d_], in_=na, func=AF.Exp), r=['m_gd'], w=['m_gd'])
                    kb.op('act', lambda e: e.activation(out=gd[:, 3, d_], in_=PS[4][:, 64 * d_:64 * d_ + 64].rearrange("p (c h) -> p c h", h=4),
                                                        func=AF.Exp, scale=-1.0), r=[('ps', 4)], w=['m_gd'])
                    kb.op('dve', lambda e: e.tensor_tensor(out=gd[:, 4, d_], in0=gd[:, 1, d_], in1=gd[:, 3, d_], op=ALU.mult),
                          r=['m_gd'], w=['m_gd'])
            kb.barrier()
            chk('mproj')
            with ExitStack() as ss_:
                hm = sb(ss_, "m_hm", [128, 16, 512], F32)
                Cst = sb(ss_, "m_C", [128, 8, 132], F32)
                Cbf = sb(ss_, "m_Cbf", [128, 8, 132], BF16)
                PT = [sb(ss_, "m_PT%d" % i, [128, 128], BF16) for i in range(4)]
                Kt = [sb(ss_, "m_Kt%d" % i, [128, 128], BF16) for i in range(4)]
                sm = sb(ss_, "m_sm", [128, 8, 4], F32)
                ssh = sb(ss_, "m_ssh", [128, 16, 4], F32)
                junk = sb(ss_, "m_junk", [128, 128], BF16)
                mot = sb(ss_, "m_mot", [128, 512], BF16)
                kb.op('dve', lambda e: e.memset(Cst[:], 0.0), w=[('m_C', i) for i in range(8)])
                kb.op('dve', lambda e: e.memset(Cbf[:], 0.0), w=[('m_Cbf', i) for i in range(8)])
                written = set()

                def scanA(w_):
                    (it, step, h, d_, c) = w_
                    hd = h * 2 + d_
                    tok = slice(c * 128, (c + 1) * 128)
                    bs, bn, bu = it % 2, 2 + it % 2, 4 + it % 2
                    pt, ktl = PT[it % 4], Kt[it % 4]
                    ptk, ktk = ('m_PT', it % 4), ('m_Kt', it % 4)
                    kb.op('pe', lambda e: e.matmul(PS[bs][:, 0:128], lhsT=qkT[:, 4 + h, tok], rhs=qkT[:, h, tok], start=True, stop=True),
                          r=[('m_qkT', h), ('m_qkT', 4 + h)], w=[('ps', bs)])
                    kb.op('dve', lambda e: e.scalar_tensor_tensor(out=pt[:], in0=PS[bs][:, 0:128], scalar=gd[:, 1, d_, c, h:h + 1],
                                                                  in1=tri_bf[d_], op0=ALU.mult, op1=ALU.mult),
                          r=[('ps', bs), 'm_gd', 'cstb'], w=[ptk])
                    kb.op('act', lambda e: e.activation(out=ktl[:], in_=kTM[:, c, h * 128:(h + 1) * 128], func=AF.Copy, scale=gd[:, 4, d_, c, h:h + 1]),
                          r=[('m_kTM', c), 'm_gd'], w=[ktk])
                    kb.op('pe', lambda e: e.matmul(PS[bn][:, 0:129], lhsT=pt[:], rhs=vaug[:, c, h, 0:129], start=True, stop=False),
                          r=[ptk, ('m_vaug', c), 'm_vaug1'], w=[('ps', bn)])
                    kb.op('pe', lambda e: e.matmul(PS[bn][:, 0:129], lhsT=qkT[:, h, tok], rhs=Cbf[:, hd, 0:129], start=False, stop=True),
                          r=[('m_qkT', h), ('m_Cbf', hd)], w=[('ps', bn)])
                    kb.op('pe', lambda e: e.matmul(PS[bu][:, 0:129], lhsT=ktl[:], rhs=vaug[:, c, h, 0:129], start=True, stop=True),
                          r=[ktk, ('m_vaug', c), 'm_vaug1'], w=[('ps', bu)])
                    smk = ('m_sm', hd)
                    kb.op('act', lambda e: e.activation(out=sm[:, hd, 0:1], in_=PS[bn][:, 128:129], func=AF.Abs), r=[('ps', bn)], w=[smk])

                def scanB(w_):
                    (it, step, h, d_, c) = w_
                    hd = h * 2 + d_
                    bs, bn, bu = it % 2, 2 + it % 2, 4 + it % 2
                    smk = ('m_sm', hd)
                    kb.op('dve', lambda e: e.tensor_tensor(out=sm[:, hd, 1:2], in0=sm[:, hd, 0:1], in1=gd[:, 2, d_, c, h:h + 1], op=ALU.max),
                          r=[smk, 'm_gd'], w=[smk])
                    kb.op('dve', lambda e: e.reciprocal(out=sm[:, hd, 2:3], in_=sm[:, hd, 1:2]), r=[smk], w=[smk])
                    hk = ('m_hm', c, h)
                    hdst = hm[:, c, h * 128:(h + 1) * 128]
                    if (c, h) not in written:
                        written.add((c, h))
                        kb.op('act', lambda e: e.activation(out=hdst, in_=PS[bn][:, 0:128], func=AF.Copy, scale=sm[:, hd, 2:3]),
                              r=[('ps', bn), smk], w=[hk])
                    else:
                        kb.op('dve', lambda e: e.scalar_tensor_tensor(out=hdst, in0=PS[bn][:, 0:128], scalar=sm[:, hd, 2:3], in1=hdst,
                                                                      op0=ALU.mult, op1=ALU.add),
                              r=[('ps', bn), smk], w=[hk])
                    kb.op('dve', lambda e: e.scalar_tensor_tensor(out=Cst[:, hd, 0:129], in0=Cst[:, hd, 0:129], scalar=gd[:, 3, d_, c, h:h + 1],
                                                                  in1=PS[bu][:, 0:129], op0=ALU.mult, op1=ALU.add),
                          r=[('ps', bu), 'm_gd'], w=[('m_C', hd)])
                    kb.op('act', lambda e: e.activation(out=Cbf[:, hd, 0:129], in_=Cst[:, hd, 0:129], func=AF.Copy),
                          r=[('m_C', hd)], w=[('m_Cbf', hd)])

                work = []
                for step in range(16):
                    for h in range(4):
                        for d_ in range(2):
                            work.append((len(work), step, h, d_, step if d_ == 0 else 15 - step))
                prevw = None
                for w_ in work:
                    scanA(w_)
                    if prevw is not None:
                        scanB(prevw)
                    prevw = w_
                scanB(prevw)
                if 'hm' in dbg_out:
                    kb.dma('sp', dbg_out['hm'].rearrange("(c p) f -> p c f", p=128), hm[:], r=[('m_hm', c, h) for c in range(16) for h in range(4)], w=['dbg_hm'])
                for c in range(16):
                    for h in range(4):
                        kb.op('act', lambda e: e.activation(out=junk[:], in_=hm[:, c, h * 128:(h + 1) * 128], func=AF.Square,
                                                            accum_out=ssh[:, c, h:h + 1]), r=[('m_hm', c, h)], w=['m_junk', 'm_ssh'])
                kb.op('act', lambda e: e.activation(out=ssh[:], in_=ssh[:], func=AF.Sqrt, scale=1.0 / 128, bias=epsb[:, 0:1]), r=['m_ssh', 'epsb'], w=['m_ssh'])
                kb.op('dve', lambda e: e.reciprocal(out=ssh[:], in_=ssh[:]), r=['m_ssh'], w=['m_ssh'])
                for c in range(16):
                    for h in range(4):
                        kb.op('dve', lambda e: e.scalar_tensor_tensor(out=mot[:, h * 128:(h + 1) * 128], in0=hm[:, c, h * 128:(h + 1) * 128],
                                                                      scalar=ssh[:, c, h:h + 1], in1=gsig[:, c, h * 128:(h + 1) * 128],
                                                                      op0=ALU.mult, op1=ALU.mult),
                              r=[('m_hm', c, h), 'm_ssh', ('m_gsig', c)], w=['m_mot'])
                    for h in range(4):
                        kb.op('pe', lambda e: e.transpose(PST[:, h * 128:(h + 1) * 128], mot[:, h * 128:(h + 1) * 128], ident_bf),
                              r=['m_mot', 'cstb'], w=['pst'])
                    kb.op('act', lambda e: e.activation(out=catT[:, 4:8, c * 128:(c + 1) * 128], in_=PST[:, 0:512].rearrange("p (h t) -> p h t", t=128), func=AF.Copy),
                          r=['pst'], w=[('catT', 4 + h_) for h_ in range(4)])
            kb.barrier()
            chk('mlstm')
            st.close()

        def attn_phase(st0, l, hnT, catT):
            st = st0.enter_context(ExitStack())
            qR = sb(st, "a_qR", [128, 4, S], BF16)
            kR = sb(st, "a_kR", [128, 4, S], BF16)
            with ExitStack() as s1:
                cosT = sb(s1, "a_cos", [128, S], F32)
                sinT = sb(s1, "a_sin", [128, S], F32)
                Wns = [sb(s1, "a_wn%d" % i, [128, KT, 512], BF16) for i in range(2)]
                t1s = [sb(s1, "a_t1%d" % i, [128, CH], F32) for i in range(2)]
                t2s = [sb(s1, "a_t2%d" % i, [128, CH], F32) for i in range(2)]
                kb.dma('sp', cosT[:], cos_d, w=['a_cos'])
                kb.dma('sp', sinT[:], sin_d, w=['a_sin'])
                xbs = [sb(s1, "a_xb%d" % i, [128, CH], BF16) for i in range(2)]
                load_w(Wns[0][:], w_in[l, :, 0:512], ('a_wn', 0))
                load_w(Wns[1][:], w_in[l, :, 512:1024], ('a_wn', 1))
                tiles = [(qk, hp, c) for qk in range(2) for hp in range(4) for c in range(NCH)]

                def rotA(i):
                    qk, hp, c = tiles[i]
                    ba = i % 2
                    for k in range(KT):
                        kb.op('pe', lambda e: e.matmul(PS[ba][:], lhsT=Wns[qk][:, k, hp * 128:(hp + 1) * 128], rhs=hnT[:, k, c * CH:(c + 1) * CH],
                                                       start=(k == 0), stop=(k == KT - 1)), r=[('a_wn', qk), ('hnT', c)], w=[('ps', ba)],
                              inc=(k == KT - 1))
                    kb.op('act', lambda e: e.activation(out=xbs[i % 2][:], in_=PS[ba][:], func=AF.Copy), r=[('ps', ba)], w=[('a_xb', i % 2)])

                def rotB(i):
                    qk, hp, c = tiles[i]
                    dst = qR if qk == 0 else kR
                    dk = 'a_qR' if qk == 0 else 'a_kR'
                    ba, bb = i % 2, 2 + i % 2
                    t1, t2 = t1s[i % 2], t2s[i % 2]
                    t1k, t2k = ('a_t1', i % 2), ('a_t2', i % 2)
                    kb.op('pe', lambda e: e.matmul(PS[bb][:], lhsT=cstb[:, 8, :], rhs=xbs[i % 2][:], start=True, stop=True),
                          r=[('a_xb', i % 2), 'cstb'], w=[('ps', bb)])
                    kb.op('dve', lambda e: e.tensor_tensor(out=t1[:], in0=PS[ba][:], in1=cosT[:, c * CH:(c + 1) * CH], op=ALU.mult),
                          r=[('ps', ba), 'a_cos', ('a_xb', i % 2)], w=[t1k])
                    kb.op('dve', lambda e: e.tensor_tensor(out=t2[:], in0=PS[bb][:], in1=sinT[:, c * CH:(c + 1) * CH], op=ALU.mult),
                          r=[('ps', bb), 'a_sin'], w=[t2k])
                    kb.op('dve', lambda e: e.tensor_tensor(out=dst[:, hp, c * CH:(c + 1) * CH], in0=t1[:], in1=t2[:], op=ALU.add),
                          r=[t1k, t2k], w=[(dk, hp)])

                rotA(0)
                for i in range(len(tiles)):
                    if i + 1 < len(tiles):
                        rotA(i + 1)
                    rotB(i)
            kb.barrier()
            chk('aproj')
            with ExitStack() as s2:
                vb = [sb(s2, "a_vb%d" % i, [128, 16, 512], BF16) for i in range(3)]
                Wv = sb(s2, "a_wv", [128, KT, 512], BF16)
                numacc = sb(s2, "a_num", [128, S], F32)
                denacc = sb(s2, "a_den", [128, S], F32)
                load_w(Wv[:], w_in[l, :, 1024:1536], 'a_wv')
                mstrip = sb(s2, "a_mstrip", [128, 4, 512], BF16)
                kb.dma('pool', mstrip[:].rearrange("p a b -> p (a b)"), mstrip_d, w=['mstrip'])
                DIL = (1, 4, 16)
                pi = 0
                for bi, dil in enumerate(DIL):
                    nb = (S // dil) // 128
                    for ti in range(16):
                        r_, j_ = divmod(ti, nb)
                        t0 = r_ + dil * 128 * j_
                        b = pi % 2
                        pi += 1
                        for k in range(KT):
                            kb.op('pe', lambda e: e.matmul(PS[b][:], lhsT=hnT[:, k, ssl(t0, 128, dil)], rhs=Wv[:, k, :],
                                                           start=(k == 0), stop=(k == KT - 1)),
                                  r=['a_wv'] + [('hnT', c) for c in range(NCH)], w=[('ps', b)])
                        kb.op('act', lambda e: e.activation(out=vb[bi][:, ti, :], in_=PS[b][:], func=AF.Copy), r=[('ps', b)], w=[('a_vb', bi)])
                qc = [sb(s2, "a_qc%d" % i, [128, S], BF16) for i in range(2)]
                kc = [sb(s2, "a_kc%d" % i, [128, S], BF16) for i in range(2)]
                pT3 = [sb(s2, "a_pTb%d" % i, [128, 512], BF16) for i in range(3)]
                state = {'it': 0, 'cc': 0}

                def emit_S(w_):
                    (hp, bi, dil, nb, r_, q0, qn, kts, qsrc, ksrc, qk_, kk_, sub0) = w_['a']
                    it = w_['it']
                    bset = it % 2
                    p_ = pT3[it % 3]
                    pk = ('a_pT', it % 3)
                    nkt = len(kts)
                    wdt = nkt * qn
                    for hh in range(2):
                        base = 64 * hh
                        bank = 2 * bset + hh
                        for i_, (kt, mk) in enumerate(kts):
                            slot = i_ * qn
                            if dil == 1:
                                lhs = ksrc[base:base + 64, hp, 128 * kt:128 * kt + 128]
                                rhs = qsrc[base:base + 64, hp, q0:q0 + qn]
                            else:
                                lhs = ksrc[base:base + 64, sub0 + 128 * kt:sub0 + 128 * kt + 128]
                                rhs = qsrc[base:base + 64, sub0 + q0:sub0 + q0 + qn]
                            kb.op('pe', lambda e: e.matmul(PS[bank][:, slot:slot + qn], lhsT=lhs, rhs=rhs, start=True, stop=True),
                                  r=[kk_, qk_], w=[('ps', bank)], inc=(hh == 1 and i_ == nkt - 1))
                    ncols = 2 * wdt
                    sidx = 3 if kts[0][1] == 'E' else (1 if kts[0][1] == 'F' else (0 if nkt == 2 else 2))
                    kb.op('act', lambda e: e.activation(out=p_[:, 0:ncols].rearrange("p (h w) -> p h w", h=2),
                                                        in_=psall[:, 2 * bset:2 * bset + 2, 0:wdt], func=AF.Exp, scale=0.125),
                          r=[('ps', 2 * bset), ('ps', 2 * bset + 1)], w=[pk])
                    kb.op('dve', lambda e: e.tensor_tensor(out=p_[:, 0:ncols], in0=p_[:, 0:ncols], in1=mstrip[:, sidx, 0:ncols], op=ALU.mult),
                          r=['mstrip'], w=[pk])

                def emit_PV(w_):
                    (hp, bi, dil, nb, r_, q0, qn, kts, qsrc, ksrc, qk_, kk_, sub0) = w_['a']
                    it = w_['it']
                    bnk = {0: 4 + (2 * it) % 3, 128: 4 + (2 * it + 1) % 3}
                    p_ = pT3[it % 3]
                    pk = ('a_pT', it % 3)
                    nkt = len(kts)
                    qsl = ssl(r_ + dil * q0, qn, dil)
                    for (c0, is_num) in ((0, True), (128, False)):
                        for hh in range(2):
                            base = 64 * hh
                            for i_, (kt, mk) in enumerate(kts):
                                slot = (hh * nkt + i_) * qn
                                hcol = (hp * 2 + hh) * 64
                                lhs = vb[bi][:, r_ * nb + kt, hcol:hcol + 64] if is_num else ones_bf[:, 0:64]
                                kb.op('pe', lambda e: e.matmul(PS[bnk[c0]][base:base + 64, 0:qn], lhsT=lhs, rhs=p_[:, slot:slot + qn],
                                                               start=(i_ == 0), stop=(i_ == nkt - 1), tile_position=(0, base)),
                                      r=[pk, ('a_vb', bi), 'cstb'], w=[('ps', bnk[c0])], inc=(hh == 1 and i_ == nkt - 1))
                    for (c0, acc, ak, eng) in ((0, numacc, 'a_num', 'act' if bi == 0 else 'dve'), (128, denacc, 'a_den', 'act' if bi == 0 else 'dve')):
                        if bi == 0:
                            if eng == 'act':
                                kb.op('act', lambda e: e.activation(out=acc[:, qsl], in_=PS[bnk[c0]][:, 0:qn], func=AF.Copy),
                                      r=[('ps', bnk[c0])], w=[(ak, 0)], loose=True)
                            else:
                                kb.op('dve', lambda e: e.tensor_copy(out=acc[:, qsl], in_=PS[bnk[c0]][:, 0:qn]),
                                      r=[('ps', bnk[c0])], w=[(ak, 0)], loose=True)
                        else:
                            kb.op('dve', lambda e: e.tensor_tensor(out=acc[:, qsl], in0=PS[bnk[c0]][:, 0:qn], in1=acc[:, qsl], op=ALU.add),
                                  r=[('ps', bnk[c0]), (ak, bi - 1)], w=[(ak, bi)], loose=True)

                for hp in range(4):
                    items = []
                    for bi, dil in enumerate(DIL):
                        nsub = S // dil
                        nb = nsub // 128
                        if dil == 1:
                            qsrc, ksrc, qk_, kk_ = qR, kR, ('a_qR', hp), ('a_kR', hp)
                        else:
                            cc = state['cc'] % 2
                            state['cc'] += 1
                            qsrc, ksrc, qk_, kk_ = qc[cc], kc[cc], ('a_qc', cc), ('a_kc', cc)
                            kb.op('pool', lambda e: e.tensor_copy(out=qsrc[:].rearrange("p (r i) -> p r i", r=dil),
                                                                  in_=qR[:, hp, :].rearrange("p (i r) -> p r i", r=dil)),
                                  r=[('a_qR', hp)], w=[qk_])
                            kb.op('pool', lambda e: e.tensor_copy(out=ksrc[:].rearrange("p (r i) -> p r i", r=dil),
                                                                  in_=kR[:, hp, :].rearrange("p (i r) -> p r i", r=dil)),
                                  r=[('a_kR', hp)], w=[kk_])
                        if nb == 1:
                            blocks = [(0, 128, [(0, 'E')])]
                        else:
                            blocks = [(0, 64, [(0, 'F')])]
                            blocks += [(64 + 128 * j, 128, [(j, 'A'), (j + 1, 'B')]) for j in range(nb - 1)]
                            blocks += [(nsub - 64, 64, [(nb - 1, 'A')])]
                        for r_ in range(dil):
                            for (q0, qn, kts) in blocks:
                                items.append({'a': (hp, bi, dil, nb, r_, q0, qn, kts, qsrc, ksrc, qk_, kk_, r_ * nsub), 'it': state['it']})
                                state['it'] += 1
                    prev = None
                    for w_ in items:
                        emit_S(w_)
                        if prev is not None:
                            emit_PV(prev)
                        prev = w_
                    emit_PV(prev)
                    nk = [('a_num', b_) for b_ in range(3)]
                    dk_ = [('a_den', b_) for b_ in range(3)]
                    kb.op('act', lambda e: e.activation(out=denacc[:], in_=denacc[:], func=AF.Ln), r=dk_, w=dk_)
                    kb.op('act', lambda e: e.activation(out=denacc[:], in_=denacc[:], func=AF.Exp, scale=-1.0), r=dk_, w=dk_)
                    kb.op('dve', lambda e: e.tensor_tensor(out=catT[:, hp, :], in0=numacc[:], in1=denacc[:], op=ALU.mult),
                          r=nk + dk_, w=[('catT', hp)] + nk)
            kb.barrier()
            chk('attn')
            st.close()

        def ffn_phase(st0, l, xsrc, xdst, hnT):
            st = st0.enter_context(ExitStack())
            aT = sb(st, "f_aT", [128, NFT, S], BF16)
            with ExitStack() as s1:
                Wgs = [sb(s1, "f_wg%d" % i, [128, KT, 512], BF16) for i in range(2)]
                Wvs = [sb(s1, "f_wv%d" % i, [128, KT, 512], BF16) for i in range(2)]
                yp = [sb(s1, "f_yp%d" % i, [128, S + 2], F32) for i in range(2)]
                u = [sb(s1, "f_u%d" % i, [128, S], F32) for i in range(2)]
                gl = sb(s1, "f_gl", [128, S], BF16)
                for i in range(2):
                    kb.op('dve', lambda e: e.memset(yp[i][:, 0:1], 0.0), w=[('f_yp', i)])
                    kb.op('dve', lambda e: e.memset(yp[i][:, S + 1:S + 2], 0.0), w=[('f_yp', i)])
                pi = 0
                for c in range(NFT):
                    GST = [0, 1, 4, 8, 12, 16, 20, NFT]
                    gidx = max(i for i in range(len(GST) - 1) if GST[i] <= c)
                    g4 = c - GST[gidx]

                    def ldgrp(gi_x):
                        c_ = GST[gi_x]
                        n = (GST[gi_x + 1] - c_) * 128
                        sl = gi_x % 2
                        load_w(Wgs[sl][:, :, 0:n], w_up[l, :, c_ * 128:c_ * 128 + n], ('f_wg', sl))
                        load_w(Wvs[sl][:, :, 0:n], w_up[l, :, DFF + c_ * 128:DFF + c_ * 128 + n], ('f_wv', sl))
                    if c == 0:
                        ldgrp(0)
                    if g4 == 0 and gidx + 1 < len(GST) - 1:
                        ldgrp(gidx + 1)
                    gi_ = gidx % 2
                    Wg, Wv = Wgs[gi_], Wvs[gi_]
                    for part, (W, wk) in enumerate(((Wg, ('f_wg', gi_)), (Wv, ('f_wv', gi_)))):
                        ti = part * NFT + c
                        for ch in range(NCH):
                            b = pi % 4
                            pi += 1
                            for k in range(KT):
                                kb.op('pe', lambda e: e.matmul(PS[b][:], lhsT=W[:, k, g4 * 128:(g4 + 1) * 128], rhs=hnT[:, k, ch * CH:(ch + 1) * CH],
                                                               start=(k == 0), stop=(k == KT - 1)), r=[wk, ('hnT', ch)], w=[('ps', b)])
                            kb.op('act', lambda e: e.activation(out=yp[part][:, 1 + ch * CH:1 + (ch + 1) * CH], in_=PS[b][:], func=AF.Copy),
                                  r=[('ps', b)], w=[('f_yp', part)])
                        kb.op('act', lambda e: e.activation(out=u[part][:], in_=yp[part][:, 0:S], func=AF.Identity,
                                                            scale=fconv[:, l, ti, 0:1], bias=fconv[:, l, ti, 3:4]),
                              r=[('f_yp', part), 'fconv'], w=[('f_u', part)])
                        for jj in (1, 2):
                            kb.op('dve', lambda e: e.scalar_tensor_tensor(out=u[part][:], in0=yp[part][:, jj:jj + S], scalar=fconv[:, l, ti, jj:jj + 1],
                                                                          in1=u[part][:], op0=ALU.mult, op1=ALU.add),
                                  r=[('f_yp', part), 'fconv'], w=[('f_u', part)])
                    kb.op('act', lambda e: e.activation(out=gl[:], in_=u[0][:], func=AF.Gelu_apprx_tanh), r=[('f_u', 0)], w=['f_gl'])
                    kb.op('dve', lambda e: e.tensor_tensor(out=aT[:, c, :], in0=gl[:], in1=u[1][:], op=ALU.mult),
                          r=['f_gl', ('f_u', 1)], w=[('f_aT', c)])
            kb.barrier()
            chk('fup')
            with ExitStack() as s2:
                last = (l == nlayers - 1)
                epilogue(s2, w_down[l], NFT, lambda ci, tsl: aT[:, ci, tsl], lambda ci, c: [('f_aT', ci)], xsrc, xdst, l, 3,
                         che=256, hn_out=None if last else hnT, nxt=None if last else (l + 1, 0))
            kb.barrier()
            st.close()

        xcur = xT_in
        hnT = sb(top, "hnT", [128, KT, S], BF16)
        try:
          for l in range(nlayers if stop_at != 'init' else 0):
              xmid = xs[0]
              xnext = yT if l == nlayers - 1 else xs[1]
              with ExitStack() as sm_:
                  catT = sb(sm_, "catT", [128, KT, S], BF16)
                  with ExitStack() as sh_:
                      if l == 0:
                          with ExitStack() as s0:
                              norm_pre(s0, xcur, l, 0, hnT)
                      kb.barrier()
                      chk('norm')
                      mlstm_phase(sh_, l, hnT, catT)
                      attn_phase(sh_, l, hnT, catT)
                  if 'cat' in dbg_out and l == dbg.get('_layer', 0):
                      with ExitStack() as sd:
                          tmpf = sb(sd, "dbg_tmp", [128, S], F32)
                          for i in range(KT):
                              kb.op('act', lambda e: e.activation(out=tmpf[:], in_=catT[:, i, :], func=AF.Copy), r=[('catT', i)], w=['dbg_tmp'])
                              kb.dma('sp', dbg_out['cat'][i * 128:(i + 1) * 128, :], tmpf[:], r=['dbg_tmp'], w=['dbg_cat'])
                      kb.barrier()
                  with ExitStack() as se:
                      epilogue(se, w_out[l], KT, lambda ci, tsl: catT[:, ci, tsl], lambda ci, c: [('catT', ci)], xcur, xmid, l, 1,
                               che=CH, hn_out=hnT, nxt=(l, 2))
                  kb.barrier()
                  chk('ep1')
              with ExitStack() as sf:
                  ffn_phase(sf, l, xmid, xnext, hnT)
              xcur = xnext
        except _Stop:
            pass
        kb.barrier()
    stuck = kb.check_deadlock()
    if stuck:
        raise RuntimeError('static deadlock: %r' % (stuck,))
    return nc, list(dbg_out.keys())


def _host_prep(inputs):
    f = np.float32
    g = np.stack([inputs['mix_pre_g'], inputs['mix_post_g'], inputs['ffn_pre_g'], inputs['ffn_post_g']], axis=1)
    gvec = np.ascontiguousarray(g.reshape(2, 4, KT, 128).transpose(3, 0, 1, 2)).reshape(128, -1).astype(f)
    mc = np.concatenate([inputs['mlstm_conv_w'], inputs['mlstm_conv_b'][:, None, :]], axis=1)
    mconv = np.ascontiguousarray(mc.reshape(2, 6, 8, 128).transpose(3, 0, 2, 1)).reshape(128, -1).astype(f)
    fc = np.concatenate([inputs['ffn_conv_w'], inputs['ffn_conv_b'][:, None, :]], axis=1)
    fconv = np.ascontiguousarray(fc.reshape(2, 4, 44, 128).transpose(3, 0, 2, 1)).reshape(128, -1).astype(f)
    gateb = np.ascontiguousarray(np.broadcast_to(inputs['mlstm_gate_b'].reshape(1, -1), (128, 32))).astype(f)
    headg = np.ascontiguousarray(np.broadcast_to(inputs['mlstm_head_g'].reshape(1, -1), (128, 1024))).astype(f)
    p = np.arange(128)
    inv_freq = (10000.0 ** (-np.arange(0, 64, 2, dtype=np.float32) / 64)).astype(np.float32)
    ang = np.arange(S, dtype=np.float32)[None, :] * inv_freq[p % 32][:, None]
    cosT = np.cos(ang).astype(f)
    sgn = np.where((p % 64) < 32, -1.0, 1.0).astype(f)[:, None]
    sinT = (np.sin(ang) * sgn).astype(f)
    a = p[:, None]
    b = p[None, :]
    NEG = -30000.0
    cst = np.zeros((128, 9, 128), f)
    cst[:, 0] = 1.0
    cst[:, 1] = (a == b)
    cst[:, 2] = np.where(a >= b, 0.0, NEG)
    cst[:, 3] = np.where(a <= b, 0.0, NEG)
    cst[:, 4] = np.where(a <= b + 64, 0.0, NEG)
    cst[:, 5] = np.where(np.abs(a - b) <= 64, 0.0, NEG)
    cst[:, 6] = (a <= b)
    cst[:, 7] = (a >= b)
    partner = np.where((p % 64) < 32, p + 32, p - 32)
    cst[:, 8] = (a == partner[None, :])
    A01 = (a >= b).astype(f); B01 = (a <= b).astype(f); F01 = (a <= b + 64).astype(f); E01 = (np.abs(a - b) <= 64).astype(f)
    ms = np.zeros((128, 4, 512), f)
    ms[:, 0] = np.concatenate([A01, B01, A01, B01], axis=1)
    ms[:, 1, 0:128] = np.concatenate([F01[:, :64], F01[:, :64]], axis=1)
    ms[:, 2, 0:128] = np.concatenate([A01[:, :64], A01[:, :64]], axis=1)
    ms[:, 3, 0:256] = np.concatenate([E01, E01], axis=1)
    shared = dict(mstrip=ms.reshape(128, -1), w_in=np.ascontiguousarray(inputs['w_in'], dtype=f), w_out=np.ascontiguousarray(inputs['w_out'], dtype=f),
                  w_up=np.ascontiguousarray(inputs['w_up'], dtype=f), w_down=np.ascontiguousarray(inputs['w_down'], dtype=f),
                  gvec=gvec, mconv=mconv, fconv=fconv, gateb=gateb, headg=headg, cosT=cosT, sinT=sinT,
                  cst=cst.reshape(128, -1))
    return shared


_NC_CACHE = {}


def kernel(**inputs):
    x = np.asarray(inputs['x'], dtype=np.float32)
    B = x.shape[0]
    shared = _host_prep(inputs)
    if 'nc' not in _NC_CACHE:
        _NC_CACHE['nc'] = build(2)[0]
    nc = _NC_CACHE['nc']
    in_maps = []
    for b in range(B):
        m = dict(shared)
        m['xT'] = np.ascontiguousarray(x[b].T)
        in_maps.append(m)
    res = run_bass_kernel_spmd(nc, in_maps, core_ids=list(range(B)))
    out = np.stack([np.ascontiguousarray(res.results[b]['yT'].T) for b in range(B)], axis=0)
    return out.astype(np.float32)
```

```python
import math
from contextlib import ExitStack
import numpy as np
import concourse.bass as bass
import concourse.mybir as mybir
from concourse.bass_utils import run_bass_kernel_spmd

F32 = mybir.dt.float32
BF16 = mybir.dt.bfloat16
AF = mybir.ActivationFunctionType
ALU = mybir.AluOpType

S = 2048
D = 1024
KT = 8
NCH = 4
CH = 512
DFF = 2816
NFT = 22
EPS = 1e-6
NDS = 24


def ssl(start, n, step):
    return slice(start, start + step * (n - 1) + 1, step)


class KB:
    def __init__(self, nc):
        self.nc = nc
        self.E = {'pe': nc.tensor, 'act': nc.scalar, 'dve': nc.vector, 'pool': nc.gpsimd, 'sp': nc.sync}
        self.sem = {e: nc.alloc_semaphore('c_' + e) for e in ('pe', 'act', 'dve', 'pool')}
        self.cnt = {e: 0 for e in self.sem}
        self.waited = {e: {} for e in self.E}
        self.dsems = [nc.alloc_semaphore('d%d' % i) for i in range(NDS)]
        self.dcnt = [0] * NDS
        self.dlast = [None] * NDS
        self.dnext = 0
        self.dnext_p = 0
        self.tk = {}
        self.dead = False
        self.prog = {e: [] for e in self.E}

    def _t(self, key):
        t = self.tk.get(key)
        if t is None:
            t = {'w': None, 'r': {}}
            self.tk[key] = t
        return t

    def _wait(self, e, ev):
        sem, val, sid = ev
        if self.waited[e].get(sid, 0) >= val:
            return
        self.E[e].wait_ge(sem, val)
        self.prog[e].append(('w', sid, val))
        self.waited[e][sid] = val

    def _deps(self, e, r, w, loose=False):
        evs = {}

        def add(ev):
            if ev is None:
                return
            if ev[2] not in evs or evs[ev[2]][1] < ev[1]:
                evs[ev[2]] = ev
        for k in r:
            add(self._t(k)['w'])
        for k in w:
            t = self._t(k)
            add(t['w'])
            for ev in t['r'].values():
                add(ev)
        for sid, ev in evs.items():
            if sid == e and (e == 'pe' or loose):
                continue
            self._wait(e, ev)

    def _record(self, ev, r, w):
        for k in r:
            self._t(k)['r'][ev[2]] = ev
        for k in w:
            t = self._t(k)
            t['w'] = ev
            t['r'] = {}

    def op(self, e, fn, r=(), w=(), loose=False, inc=True):
        if self.dead:
            return
        self._deps(e, r, w, loose)
        ins = fn(self.E[e])
        if inc:
            self.cnt[e] += 1
            ins.then_inc(self.sem[e], 1)
            self.prog[e].append(('i', e, 1))
            self._record((self.sem[e], self.cnt[e], e), r, w)
        else:
            self._record((self.sem[e], self.cnt[e] + 1, e), r, w)

    def dma(self, q, out, in_, r=(), w=()):
        if self.dead:
            return
        if q == 'pool':
            i = 16 + self.dnext_p
            self.dnext_p = (self.dnext_p + 1) % (NDS - 16)
        else:
            i = self.dnext
            self.dnext = (i + 1) % 16
        if self.dlast[i] is not None:
            self._wait(q, self.dlast[i])
        self._deps(q, r, w)
        ins = self.E[q].dma_start(out=out, in_=in_)
        self.dcnt[i] += 16
        ins.then_inc(self.dsems[i], 16)
        self.prog[q].append(('i', 'd%d' % i, 16))
        ev = (self.dsems[i], self.dcnt[i], 'd%d' % i)
        self.dlast[i] = ev
        self._record(ev, r, w)
        return ev

    def check_deadlock(self):
        pc = {e: 0 for e in self.prog}
        val = {}
        progress = True
        while progress:
            progress = False
            for e, p in self.prog.items():
                while pc[e] < len(p):
                    k, sid, v = p[pc[e]]
                    if k == 'w':
                        if val.get(sid, 0) < v:
                            break
                    else:
                        val[sid] = val.get(sid, 0) + v
                    pc[e] += 1
                    progress = True
        stuck = {e: (pc[e], len(p), p[pc[e]]) for e, p in self.prog.items() if pc[e] < len(p)}
        return stuck

    def barrier(self):
        if self.dead:
            return
        evs = [(self.sem[e], self.cnt[e], e) for e in self.sem if self.cnt[e] > 0]
        evs += [ev for ev in self.dlast if ev is not None]
        for e in self.E:
            for ev in evs:
                if ev[2] == e:
                    continue
                self._wait(e, ev)


class _Stop(Exception):
    pass


def build(nlayers=2, dbg=None, stop_at=None):
    dbg = dbg or {}
    nc = bass.Bass("TRN2", target_bir_lowering=False)
    kb = KB(nc)
    ein = lambda n, s, dt=F32: nc.dram_tensor(n, list(s), dt, kind="ExternalInput").ap()
    xT_in = ein("xT", [D, S])
    w_in = ein("w_in", [2, D, 3600])
    w_out = ein("w_out", [2, D, D])
    w_up = ein("w_up", [2, D, 2 * DFF])
    w_down = ein("w_down", [2, DFF, D])
    gvec_d = ein("gvec", [128, 2 * 4 * KT])
    mconv_d = ein("mconv", [128, 2 * 8 * 6])
    fconv_d = ein("fconv", [128, 2 * 44 * 4])
    gateb_d = ein("gateb", [128, 2 * 16])
    headg_d = ein("headg", [128, 2 * 512])
    cos_d = ein("cosT", [128, S])
    sin_d = ein("sinT", [128, S])
    mstrip_d = ein("mstrip", [128, 4 * 512])
    cst_d = ein("cst", [128, 9 * 128])
    yT = nc.dram_tensor("yT", [D, S], F32, kind="ExternalOutput").ap()
    xs = [nc.dram_tensor("xs%d" % i, [D, S], F32).ap() for i in range(2)]
    dbg_out = {}
    for name, shape in dbg.items():
        dbg_out[name] = nc.dram_tensor("dbg_" + name, list(shape), F32, kind="ExternalOutput").ap()

    pm = lambda ap: ap.rearrange("(k p) t -> p k t", p=128)

    with ExitStack() as top:
        uid = [0]

        def sb(st, name, shape, dt):
            uid[0] += 1
            return st.enter_context(nc.sbuf_tensor("s%d_%s" % (uid[0], name), list(shape), dt))
        gvec = sb(top, "gvec", [128, 2, 4, KT], F32)
        mconv = sb(top, "mconv", [128, 2, 8, 6], F32)
        fconv = sb(top, "fconv", [128, 2, 44, 4], F32)
        gateb = sb(top, "gateb", [128, 2, 16], F32)
        headg = sb(top, "headg", [128, 2, 512], BF16)
        cstb = sb(top, "cstb", [128, 9, 128], BF16)
        epsb = sb(top, "epsb", [128, 1], F32)
        psall = top.enter_context(nc.psum_tensor("psall", [128, 7, 512], F32))
        PS = [psall[:, i, :] for i in range(7)]
        PST = top.enter_context(nc.psum_tensor("pst", [128, 1024], BF16))
        for i in range(7):
            kb._t(('ps', i))
        kb.dma('sp', gvec[:].rearrange("p a b c -> p (a b c)"), gvec_d, w=['gvec'])
        kb.dma('sp', mconv[:].rearrange("p a b c -> p (a b c)"), mconv_d, w=['mconv'])
        kb.dma('sp', fconv[:].rearrange("p a b c -> p (a b c)"), fconv_d, w=['fconv'])
        kb.dma('sp', gateb[:].rearrange("p a b -> p (a b)"), gateb_d, w=['gateb'])
        kb.dma('pool', headg[:].rearrange("p a b -> p (a b)"), headg_d, w=['headg'])
        kb.dma('pool', cstb[:].rearrange("p a b -> p (a b)"), cst_d, w=['cstb'])
        cview = cst_d.rearrange("p (a b) -> p a b", b=128)
        kb.op('dve', lambda e: e.memset(epsb[:], EPS), w=['epsb'])
        ones_bf = cstb[:, 0, :]
        ident_bf = cstb[:, 1, :]
        MASK = {'A': cstb[:, 2, :], 'B': cstb[:, 3, :], 'F': cstb[:, 4, :], 'E': cstb[:, 5, :]}
        tri_bf = [cstb[:, 6, :], cstb[:, 7, :]]

        def chk(name):
            if stop_at == name:
                kb.barrier()
                kb.dead = True

        def dump(name, sb_ap, rkeys):
            if name in dbg_out:
                kb.dma('sp', dbg_out[name], sb_ap, r=rkeys, w=['dbg_' + name])

        def rstd_from_ps(ps_i, rstd_ap, rkey, n):
            kb.op('act', lambda e: e.activation(out=rstd_ap, in_=PS[ps_i][:, 0:rstd_ap.shape[1]], func=AF.Ln,
                                                scale=1.0 / n, bias=epsb[:, 0:1]),
                  r=[('ps', ps_i), 'epsb'], w=[rkey])
            kb.op('act', lambda e: e.activation(out=rstd_ap, in_=rstd_ap, func=AF.Exp, scale=-0.5), r=[rkey], w=[rkey])

        def norm_pre(st, xsrc, l, j, hnT):
            xcs = [sb(st, "np_xc%d" % i, [128, KT, CH], F32) for i in range(2)]
            sq = sb(st, "np_sq", [128, KT, CH], BF16)
            rstd = sb(st, "np_rstd", [128, CH], F32)
            for c in range(NCH):
                xc = xcs[c % 2]
                xk = ('np_xc', c % 2)
                kb.dma('sp', xc[:], pm(xsrc)[:, :, c * CH:(c + 1) * CH], w=[xk])
                kb.op('act', lambda e: e.activation(out=sq[:], in_=xc[:], func=AF.Square), r=[xk], w=['np_sq'])
                for k in range(KT):
                    kb.op('pe', lambda e: e.matmul(PS[6][:], lhsT=ones_bf, rhs=sq[:, k, :], start=(k == 0), stop=(k == KT - 1)),
                          r=['np_sq', 'cstb'], w=[('ps', 6)])
                rstd_from_ps(6, rstd[:], 'np_rstd', D)
                for k in range(KT):
                    kb.op('dve', lambda e: e.scalar_tensor_tensor(out=hnT[:, k, c * CH:(c + 1) * CH], in0=xc[:, k, :],
                                                                  scalar=gvec[:, l, j, k:k + 1], in1=rstd[:],
                                                                  op0=ALU.mult, op1=ALU.mult),
                          r=[xk, 'np_rstd', 'gvec'], w=[('hnT', c)])

        def load_w(dst, src_rows_cols, wkey):
            kb.dma('pool', dst, src_rows_cols.rearrange("(k p) c -> p k c", p=128), w=[wkey])

        def epilogue(st, Wd, nct, rhs_fn, rkeys_fn, xsrc, xdst, l, j, che=CH, hn_out=None, nxt=None, xc_alias=None):
            W = sb(st, "ep_w", [128, nct, D], BF16)
            if xc_alias is not None:
                xcs = [xc_alias[:, 4 * i:4 * i + 4, :].bitcast(F32).rearrange("p a (b t) -> p (a b) t", t=che) for i in range(2)]
            else:
                xcs = [sb(st, "ep_xc%d" % i, [128, KT, che], F32) for i in range(2)]
            ff = sb(st, "ep_ff", [128, KT, che], F32)
            sq = sb(st, "ep_sq", [128, KT, che], BF16)
            rstd = sb(st, "ep_rstd", [128, che], F32)
            if hn_out is not None:
                sq2 = sb(st, "ep_sq2", [128, KT, che], BF16)
                rstd2 = sb(st, "ep_rstd2", [128, che], F32)
            for qt in range(4):
                for k0 in range(0, nct, 8):
                    k1 = min(nct, k0 + 8)
                    load_w(W[:, k0:k1, qt * 256:(qt + 1) * 256], Wd[k0 * 128:k1 * 128, qt * 256:(qt + 1) * 256], ('ep_w', qt))
            state = {'pi': 0}
            nchunk = S // che

            def partA(c):
                tsl = slice(c * che, (c + 1) * che)
                xc = xcs[c % 2]
                xk = ('ep_xc', c % 2)
                kb.dma('sp', xc[:], pm(xsrc)[:, :, tsl], w=[xk])
                for m in range(KT):
                    b = state['pi'] % 4
                    state['pi'] += 1
                    for ci in range(nct):
                        kb.op('pe', lambda e: e.matmul(PS[b][:, 0:che], lhsT=W[:, ci, m * 128:(m + 1) * 128], rhs=rhs_fn(ci, tsl),
                                                       start=(ci == 0), stop=(ci == nct - 1)),
                              r=[('ep_w', m // 2)] + rkeys_fn(ci, c), w=[('ps', b)], inc=(ci == nct - 1))
                    kb.op('act', lambda e: e.activation(out=ff[:, m, :], in_=PS[b][:, 0:che], func=AF.Copy), r=[('ps', b)], w=[('ep_ff', m)])
                    kb.op('act', lambda e: e.activation(out=sq[:, m, :], in_=ff[:, m, :], func=AF.Square), r=[('ep_ff', m)], w=[('ep_sq', m)])

            def partB(c):
                tsl = slice(c * che, (c + 1) * che)
                xc = xcs[c % 2]
                xk = ('ep_xc', c % 2)
                for m in range(KT):
                    kb.op('pe', lambda e: e.matmul(PS[6][:, 0:che], lhsT=ones_bf, rhs=sq[:, m, :], start=(m == 0), stop=(m == KT - 1)),
                          r=[('ep_sq', m), 'cstb'], w=[('ps', 6)], inc=(m == KT - 1))
                rstd_from_ps(6, rstd[:], 'ep_rstd', D)
                for m in range(KT):
                    kb.op('dve', lambda e: e.scalar_tensor_tensor(out=ff[:, m, :], in0=ff[:, m, :], scalar=gvec[:, l, j, m:m + 1],
                                                                  in1=rstd[:], op0=ALU.mult, op1=ALU.mult),
                          r=['ep_rstd', 'gvec'], w=[('ep_ff', m)])
                    kb.op('dve', lambda e: e.tensor_tensor(out=xc[:, m, :], in0=xc[:, m, :], in1=ff[:, m, :], op=ALU.add),
                          r=[('ep_ff', m)], w=[xk])
                kb.dma('sp', pm(xdst)[:, :, tsl], xc[:], r=[xk], w=[('xdst', c)])

            def partC(c):
                if hn_out is None:
                    return
                tsl = slice(c * che, (c + 1) * che)
                xc = xcs[c % 2]
                xk = ('ep_xc', c % 2)
                l2, j2 = nxt
                kb.op('act', lambda e: e.activation(out=sq2[:], in_=xc[:], func=AF.Square), r=[xk], w=['ep_sq2'])
                for m in range(KT):
                    kb.op('pe', lambda e: e.matmul(PS[5][:, 0:che], lhsT=ones_bf, rhs=sq2[:, m, :], start=(m == 0), stop=(m == KT - 1)),
                          r=['ep_sq2', 'cstb'], w=[('ps', 5)], inc=(m == KT - 1))
                rstd_from_ps(5, rstd2[:], 'ep_rstd2', D)
                for m in range(KT):
                    kb.op('dve', lambda e: e.scalar_tensor_tensor(out=hn_out[:, m, tsl], in0=xc[:, m, :], scalar=gvec[:, l2, j2, m:m + 1],
                                                                  in1=rstd2[:], op0=ALU.mult, op1=ALU.mult),
                          r=[xk, 'ep_rstd2', 'gvec'], w=[('hnT', (c * che) // CH)])

            partA(0)
            for c in range(nchunk):
                partB(c)
                if c + 1 < nchunk:
                    partA(c + 1)
                partC(c)

        def mlstm_phase(st0, l, hnT, catT):
            st = st0.enter_context(ExitStack())
            qkT = sb(st, "m_qkT", [128, 8, S], BF16)
            kTM = sb(st, "m_kTM", [128, 16, 512], BF16)
            vaug = sb(st, "m_vaug", [128, 16, 4, 132], BF16)
            gsig = sb(st, "m_gsig", [128, 16, 512], BF16)
            gates = sb(st, "m_gates", [128, 16, 16], F32)
            l1 = sb(st, "m_l1", [128, 16, 8], F32)
            gd = sb(st, "m_gd", [128, 5, 2, 16, 4], F32)
            with ExitStack() as sp_:
                Wa = sb(sp_, "m_wa", [128, KT, 512], BF16)
                Wb = sb(sp_, "m_wb", [128, KT, 512], BF16)
                Wg = sb(sp_, "m_wg", [128, KT, 16], BF16)
                ypads = [sb(sp_, "m_ypad%d" % i, [128, S + 4], BF16) for i in range(2)]
                uaccs = [sb(sp_, "m_uacc%d" % i, [128, S], F32) for i in range(2)]
                sgt = sb(sp_, "m_sgt", [128, 512], BF16)
                load_w(Wa[:], w_in[l, :, 1536:2048], 'm_wa')
                load_w(Wb[:], w_in[l, :, 2048:2560], 'm_wb')
                load_w(Wg[:], w_in[l, :, 3584:3600], 'm_wg')
                for i in range(2):
                    kb.op('dve', lambda e: e.memset(ypads[i][:, 0:2], 0.0), w=[('m_ypad', i)])
                    kb.op('dve', lambda e: e.memset(ypads[i][:, S + 2:S + 4], 0.0), w=[('m_ypad', i)])
                kb.op('dve', lambda e: e.memset(vaug[:, :, :, 128:129], 1.0), w=['m_vaug1'])
                cst_ = {'pi': 0}

                def convA(i):
                    W = Wa if i < 4 else Wb
                    wk = 'm_wa' if i < 4 else 'm_wb'
                    cs = (i % 4) * 128
                    ypad, uacc = ypads[i % 2], uaccs[i % 2]
                    yk, uk = ('m_ypad', i % 2), ('m_uacc', i % 2)
                    for c in range(NCH):
                        b = cst_['pi'] % 4
                        cst_['pi'] += 1
                        for k in range(KT):
                            kb.op('pe', lambda e: e.matmul(PS[b][:], lhsT=W[:, k, cs:cs + 128], rhs=hnT[:, k, c * CH:(c + 1) * CH],
                                                           start=(k == 0), stop=(k == KT - 1)),
                                  r=[wk, ('hnT', c)], w=[('ps', b)], inc=(k == KT - 1))
                        kb.op('act', lambda e: e.activation(out=ypad[:, 2 + c * CH:2 + (c + 1) * CH], in_=PS[b][:], func=AF.Copy),
                              r=[('ps', b)], w=[yk])
                    kb.op('act', lambda e: e.activation(out=uacc[:], in_=ypad[:, 0:S], func=AF.Identity,
                                                        scale=mconv[:, l, i, 0:1], bias=mconv[:, l, i, 5:6]),
                          r=[yk, 'mconv'], w=[uk])

                def convB(i):
                    ypad, uacc = ypads[i % 2], uaccs[i % 2]
                    yk, uk = ('m_ypad', i % 2), ('m_uacc', i % 2)
                    for jj in range(1, 5):
                        kb.op('dve', lambda e: e.scalar_tensor_tensor(out=uacc[:], in0=ypad[:, jj:jj + S], scalar=mconv[:, l, i, jj:jj + 1],
                                                                      in1=uacc[:], op0=ALU.mult, op1=ALU.add),
                              r=[yk, 'mconv'], w=[uk])

                def convC(i):
                    uacc, uk = uaccs[i % 2], ('m_uacc', i % 2)
                    kb.op('act', lambda e: e.activation(out=qkT[:, i, :], in_=uacc[:], func=AF.Silu), r=[uk], w=[('m_qkT', i)])

                convA(0)
                for i in range(8):
                    convB(i)
                    if i + 1 < 8:
                        convA(i + 1)
                    convC(i)
                if 'qk' in dbg_out:
                    for i in range(8):
                        kb.op('act', lambda e: e.activation(out=uaccs[0][:], in_=qkT[:, i, :], func=AF.Copy), r=[('m_qkT', i)], w=[('m_uacc', 0)])
                        dump_ap = dbg_out['qk'][i * 128:(i + 1) * 128, :]
                        kb.dma('sp', dump_ap, uaccs[0][:], r=[('m_uacc', 0)], w=['dbg_qk'])
                for c in range(16):
                    for h in range(4):
                        kb.op('pe', lambda e: e.transpose(PST[:, h * 128:(h + 1) * 128], qkT[:, 4 + h, c * 128:(c + 1) * 128], ident_bf),
                              r=[('m_qkT', 4 + h), 'cstb'], w=['pst'])
                    kb.op('act', lambda e: e.activation(out=kTM[:, c, :], in_=PST[:, 0:512], func=AF.Copy), r=['pst'], w=[('m_kTM', c)])
                load_w(Wa[:], w_in[l, :, 2560:3072], 'm_wa')
                load_w(Wb[:], w_in[l, :, 3072:3584], 'm_wb')
                for c in range(16):
                    tk = ('hnT', c // 4)
                    for k in range(KT):
                        kb.op('pe', lambda e: e.matmul(PS[0][:], lhsT=hnT[:, k, c * 128:(c + 1) * 128], rhs=Wa[:, k, :],
                                                       start=(k == 0), stop=(k == KT - 1)), r=['m_wa', tk], w=[('ps', 0)])
                    kb.op('act', lambda e: e.activation(out=vaug[:, c, :, 0:128], in_=PS[0][:].rearrange("p (h f) -> p h f", f=128), func=AF.Copy),
                          r=[('ps', 0)], w=[('m_vaug', c)])
                    for k in range(KT):
                        kb.op('pe', lambda e: e.matmul(PS[1][:], lhsT=hnT[:, k, c * 128:(c + 1) * 128], rhs=Wb[:, k, :],
                                                       start=(k == 0), stop=(k == KT - 1)), r=['m_wb', tk], w=[('ps', 1)])
                    kb.op('act', lambda e: e.activation(out=sgt[:], in_=PS[1][:], func=AF.Sigmoid), r=[('ps', 1)], w=['m_sgt'])
                    kb.op('dve', lambda e: e.tensor_tensor(out=gsig[:, c, :], in0=sgt[:], in1=headg[:, l, :], op=ALU.mult),
                          r=['m_sgt', 'headg'], w=[('m_gsig', c)])
                    for k in range(KT):
                        kb.op('pe', lambda e: e.matmul(PS[2][:, 0:16], lhsT=hnT[:, k, c * 128:(c + 1) * 128], rhs=Wg[:, k, :],
                                                       start=(k == 0), stop=(k == KT - 1)), r=['m_wg', tk], w=[('ps', 2)])
                    kb.op('dve', lambda e: e.tensor_tensor(out=gates[:, c, :], in0=PS[2][:, 0:16], in1=gateb[:, l, :], op=ALU.add),
                          r=[('ps', 2), 'gateb'], w=['m_gates'])
                for d_ in range(2):
                    kb.op('act', lambda e: e.activation(out=l1[:, :, 4 * d_:4 * d_ + 4], in_=gates[:, :, 8 * d_ + 4:8 * d_ + 8], func=AF.Exp, scale=-1.0),
                          r=['m_gates'], w=['m_l1'])
                kb.op('act', lambda e: e.activation(out=l1[:], in_=l1[:], func=AF.Ln, bias=1.0), r=['m_l1'], w=['m_l1'])
                l1h = sb(sp_, "m_l1h", [128, 2, 16, 8], BF16)
                l1r = sb(sp_, "m_l1r", [128, 16, 8], F32)
                kb.op('dve', lambda e: e.tensor_copy(out=l1h[:, 0], in_=l1[:]), r=['m_l1'], w=['m_l1h'])
                kb.op('dve', lambda e: e.tensor_copy(out=l1r[:], in_=l1h[:, 0]), r=['m_l1h'], w=['m_l1r'])
                kb.op('dve', lambda e: e.tensor_tensor(out=l1r[:], in0=l1[:], in1=l1r[:], op=ALU.subtract), r=['m_l1', 'm_l1r'], w=['m_l1r'])
                kb.op('dve', lambda e: e.tensor_copy(out=l1h[:, 1], in_=l1r[:]), r=['m_l1r'], w=['m_l1h'])
                for d_ in range(2):
                    for (bank, lhs) in ((3, tri_bf[d_]), (4, ones_bf)):
                        for hl in range(2):
                            kb.op('pe', lambda e: e.matmul(PS[bank][:, 64 * d_:64 * d_ + 64].rearrange("p (c h) -> p c h", h=4), lhsT=lhs,
                                                           rhs=l1h[:, hl, :, 4 * d_:4 * d_ + 4], start=(hl == 0), stop=(hl == 1)),
                                  r=['m_l1h', 'cstb'], w=[('ps', bank)])
                    na = gd[:, 0, d_]
                    kb.op('act', lambda e: e.activation(out=na, in_=PS[3][:, 64 * d_:64 * d_ + 64].rearrange("p (c h) -> p c h", h=4), func=AF.Copy),
                          r=[('ps', 3)], w=['m_gd'])
                    kb.op('dve', lambda e: e.tensor_tensor(out=gd[:, 1, d_], in0=gates[:, :, 8 * d_:8 * d_ + 4], in1=na, op=ALU.add),
                          r=['m_gates', 'm_gd'], w=['m_gd'])
                    kb.op('act', lambda e: e.activation(out=gd[:, 1, d_], in_=gd[:, 1, d_], func=AF.Exp, bias=-0.5 * math.log(128.0)),
                          r=['m_gd'], w=['m_gd'])
                    kb.op('act', lambda e: e.activation(out=gd[:, 2, d_], in_=na, func=AF.Exp), r=['m_gd'], w=['m_gd'])
                    kb.op('act', lambda e: e.activation(out=gd[:, 3, d_], in_=PS[4][:, 64 * d_:64 * d_ + 64].rearrange("p (c h) -> p c h", h=4),
                                                        func=AF.Exp, scale=-1.0), r=[('ps', 4)], w=['m_gd'])
                    kb.op('dve', lambda e: e.tensor_tensor(out=gd[:, 4, d_], in0=gd[:, 1, d_], in1=gd[:, 3, d_], op=ALU.mult),
                          r=['m_gd'], w=['m_gd'])
            kb.barrier()
            chk('mproj')
            with ExitStack() as ss_:
                hm = sb(ss_, "m_hm", [128, 16, 512], F32)
                Cst = sb(ss_, "m_C", [128, 8, 132], F32)
                Cbf = sb(ss_, "m_Cbf", [128, 8, 132], BF16)
                PT = [sb(ss_, "m_PT%d" % i, [128, 128], BF16) for i in range(4)]
                Kt = [sb(ss_, "m_Kt%d" % i, [128, 128], BF16) for i in range(4)]
                sm = sb(ss_, "m_sm", [128, 8, 4], F32)
                ssh = sb(ss_, "m_ssh", [128, 16, 4], F32)
                junk = sb(ss_, "m_junk", [128, 128], BF16)
                mot = sb(ss_, "m_mot", [128, 512], BF16)
                kb.op('dve', lambda e: e.memset(Cst[:], 0.0), w=[('m_C', i) for i in range(8)])
                kb.op('dve', lambda e: e.memset(Cbf[:], 0.0), w=[('m_Cbf', i) for i in range(8)])
                written = set()

                def scanA(w_):
                    (it, step, h, d_, c) = w_
                    hd = h * 2 + d_
                    tok = slice(c * 128, (c + 1) * 128)
                    bs, bn, bu = it % 2, 2 + it % 2, 4 + it % 2
                    pt, ktl = PT[it % 4], Kt[it % 4]
                    ptk, ktk = ('m_PT', it % 4), ('m_Kt', it % 4)
                    kb.op('pe', lambda e: e.matmul(PS[bs][:, 0:128], lhsT=qkT[:, 4 + h, tok], rhs=qkT[:, h, tok], start=True, stop=True),
                          r=[('m_qkT', h), ('m_qkT', 4 + h)], w=[('ps', bs)])
                    kb.op('dve', lambda e: e.scalar_tensor_tensor(out=pt[:], in0=PS[bs][:, 0:128], scalar=gd[:, 1, d_, c, h:h + 1],
                                                                  in1=tri_bf[d_], op0=ALU.mult, op1=ALU.mult),
                          r=[('ps', bs), 'm_gd', 'cstb'], w=[ptk])
                    kb.op('act', lambda e: e.activation(out=ktl[:], in_=kTM[:, c, h * 128:(h + 1) * 128], func=AF.Copy, scale=gd[:, 4, d_, c, h:h + 1]),
                          r=[('m_kTM', c), 'm_gd'], w=[ktk])
                    kb.op('pe', lambda e: e.matmul(PS[bn][:, 0:129], lhsT=pt[:], rhs=vaug[:, c, h, 0:129], start=True, stop=False),
                          r=[ptk, ('m_vaug', c), 'm_vaug1'], w=[('ps', bn)])
                    kb.op('pe', lambda e: e.matmul(PS[bn][:, 0:129], lhsT=qkT[:, h, tok], rhs=Cbf[:, hd, 0:129], start=False, stop=True),
                          r=[('m_qkT', h), ('m_Cbf', hd)], w=[('ps', bn)])
                    kb.op('pe', lambda e: e.matmul(PS[bu][:, 0:129], lhsT=ktl[:], rhs=vaug[:, c, h, 0:129], start=True, stop=True),
                          r=[ktk, ('m_vaug', c), 'm_vaug1'], w=[('ps', bu)])
                    smk = ('m_sm', hd)
                    kb.op('act', lambda e: e.activation(out=sm[:, hd, 0:1], in_=PS[bn][:, 128:129], func=AF.Abs), r=[('ps', bn)], w=[smk])

                def scanB(w_):
                    (it, step, h, d_, c) = w_
                    hd = h * 2 + d_
                    bs, bn, bu = it % 2, 2 + it % 2, 4 + it % 2
                    smk = ('m_sm', hd)
                    kb.op('dve', lambda e: e.tensor_tensor(out=sm[:, hd, 1:2], in0=sm[:, hd, 0:1], in1=gd[:, 2, d_, c, h:h + 1], op=ALU.max),
                          r=[smk, 'm_gd'], w=[smk])
                    kb.op('dve', lambda e: e.reciprocal(out=sm[:, hd, 2:3], in_=sm[:, hd, 1:2]), r=[smk], w=[smk])
                    hk = ('m_hm', c, h)
                    hdst = hm[:, c, h * 128:(h + 1) * 128]
                    if (c, h) not in written:
                        written.add((c, h))
                        kb.op('act', lambda e: e.activation(out=hdst, in_=PS[bn][:, 0:128], func=AF.Copy, scale=sm[:, hd, 2:3]),
                              r=[('ps', bn), smk], w=[hk])
                    else:
                        kb.op('dve', lambda e: e.scalar_tensor_tensor(out=hdst, in0=PS[bn][:, 0:128], scalar=sm[:, hd, 2:3], in1=hdst,
                                                                      op0=ALU.mult, op1=ALU.add),
                              r=[('ps', bn), smk], w=[hk])
                    kb.op('dve', lambda e: e.scalar_tensor_tensor(out=Cst[:, hd, 0:129], in0=Cst[:, hd, 0:129], scalar=gd[:, 3, d_, c, h:h + 1],
                                                                  in1=PS[bu][:, 0:129], op0=ALU.mult, op1=ALU.add),
                          r=[('ps', bu), 'm_gd'], w=[('m_C', hd)])
                    kb.op('pool', lambda e: e.tensor_copy(out=Cbf[:, hd, 0:129], in_=Cst[:, hd, 0:129]),
                          r=[('m_C', hd)], w=[('m_Cbf', hd)])

                work = []
                for step in range(16):
                    for h in range(4):
                        for d_ in range(2):
                            work.append((len(work), step, h, d_, step if d_ == 0 else 15 - step))
                prevw = None
                for w_ in work:
                    scanA(w_)
                    if prevw is not None:
                        scanB(prevw)
                    prevw = w_
                scanB(prevw)
                if 'hm' in dbg_out:
                    kb.dma('sp', dbg_out['hm'].rearrange("(c p) f -> p c f", p=128), hm[:], r=[('m_hm', c, h) for c in range(16) for h in range(4)], w=['dbg_hm'])
                mots = [mot, sb(ss_, "m_mot2", [128, 512], BF16)]
                for c in range(16):
                    sk = ('m_ssh', c)
                    mt, mk_ = mots[c % 2], ('m_mot', c % 2)
                    for h in range(4):
                        kb.op('act', lambda e: e.activation(out=junk[:], in_=hm[:, c, h * 128:(h + 1) * 128], func=AF.Square,
                                                            accum_out=ssh[:, c, h:h + 1]), r=[('m_hm', c, h)], w=['m_junk', sk], loose=True)
                    kb.op('act', lambda e: e.activation(out=ssh[:, c, :], in_=ssh[:, c, :], func=AF.Ln, scale=1.0 / 128, bias=epsb[:, 0:1]),
                          r=[sk, 'epsb'], w=[sk])
                    kb.op('act', lambda e: e.activation(out=ssh[:, c, :], in_=ssh[:, c, :], func=AF.Exp, scale=-0.5), r=[sk], w=[sk])
                    for h in range(4):
                        kb.op('dve', lambda e: e.scalar_tensor_tensor(out=mt[:, h * 128:(h + 1) * 128], in0=hm[:, c, h * 128:(h + 1) * 128],
                                                                      scalar=ssh[:, c, h:h + 1], in1=gsig[:, c, h * 128:(h + 1) * 128],
                                                                      op0=ALU.mult, op1=ALU.mult),
                              r=[('m_hm', c, h), sk, ('m_gsig', c)], w=[mk_], loose=True)
                    for h in range(4):
                        kb.op('pe', lambda e: e.transpose(PST[:, (c % 2) * 512 + h * 128:(c % 2) * 512 + (h + 1) * 128], mt[:, h * 128:(h + 1) * 128], ident_bf),
                              r=[mk_, 'cstb'], w=['pst'], inc=(h == 3))
                    kb.op('act', lambda e: e.activation(out=catT[:, 4:8, c * 128:(c + 1) * 128],
                                                        in_=PST[:, (c % 2) * 512:(c % 2) * 512 + 512].rearrange("p (h t) -> p h t", t=128), func=AF.Copy),
                          r=['pst'], w=[('catT', 4 + h_) for h_ in range(4)], loose=True)
            kb.barrier()
            chk('mlstm')
            st.close()

        def attn_phase(st0, l, hnT, catT):
            st = st0.enter_context(ExitStack())
            qR = sb(st, "a_qR", [128, 4, S], BF16)
            kR = sb(st, "a_kR", [128, 4, S], BF16)
            with ExitStack() as s1:
                cosT = sb(s1, "a_cos", [128, S], F32)
                sinT = sb(s1, "a_sin", [128, S], F32)
                Wns = [sb(s1, "a_wn%d" % i, [128, KT, 512], BF16) for i in range(2)]
                t1s = [sb(s1, "a_t1%d" % i, [128, CH], F32) for i in range(2)]
                t2s = [sb(s1, "a_t2%d" % i, [128, CH], F32) for i in range(2)]
                kb.dma('sp', cosT[:], cos_d, w=['a_cos'])
                kb.dma('sp', sinT[:], sin_d, w=['a_sin'])
                xbs = [sb(s1, "a_xb%d" % i, [128, CH], BF16) for i in range(2)]
                load_w(Wns[0][:], w_in[l, :, 0:512], ('a_wn', 0))
                load_w(Wns[1][:], w_in[l, :, 512:1024], ('a_wn', 1))
                tiles = [(qk, hp, c) for qk in range(2) for hp in range(4) for c in range(NCH)]

                def rotA(i):
                    qk, hp, c = tiles[i]
                    ba = i % 2
                    for k in range(KT):
                        kb.op('pe', lambda e: e.matmul(PS[ba][:], lhsT=Wns[qk][:, k, hp * 128:(hp + 1) * 128], rhs=hnT[:, k, c * CH:(c + 1) * CH],
                                                       start=(k == 0), stop=(k == KT - 1)), r=[('a_wn', qk), ('hnT', c)], w=[('ps', ba)],
                              inc=(k == KT - 1))
                    kb.op('act', lambda e: e.activation(out=xbs[i % 2][:], in_=PS[ba][:], func=AF.Copy), r=[('ps', ba)], w=[('a_xb', i % 2)])

                def rotB(i):
                    qk, hp, c = tiles[i]
                    dst = qR if qk == 0 else kR
                    dk = 'a_qR' if qk == 0 else 'a_kR'
                    ba, bb = i % 2, 2 + i % 2
                    t1, t2 = t1s[i % 2], t2s[i % 2]
                    t1k, t2k = ('a_t1', i % 2), ('a_t2', i % 2)
                    kb.op('pe', lambda e: e.matmul(PS[bb][:], lhsT=cstb[:, 8, :], rhs=xbs[i % 2][:], start=True, stop=True),
                          r=[('a_xb', i % 2), 'cstb'], w=[('ps', bb)])
                    kb.op('dve', lambda e: e.tensor_tensor(out=t1[:], in0=PS[ba][:], in1=cosT[:, c * CH:(c + 1) * CH], op=ALU.mult),
                          r=[('ps', ba), 'a_cos', ('a_xb', i % 2)], w=[t1k])
                    kb.op('dve', lambda e: e.tensor_tensor(out=t2[:], in0=PS[bb][:], in1=sinT[:, c * CH:(c + 1) * CH], op=ALU.mult),
                          r=[('ps', bb), 'a_sin'], w=[t2k])
                    kb.op('dve', lambda e: e.tensor_tensor(out=dst[:, hp, c * CH:(c + 1) * CH], in0=t1[:], in1=t2[:], op=ALU.add),
                          r=[t1k, t2k], w=[(dk, hp)])

                rotA(0)
                for i in range(len(tiles)):
                    if i + 1 < len(tiles):
                        rotA(i + 1)
                    rotB(i)
            kb.barrier()
            chk('aproj')
            with ExitStack() as s2:
                vb = [sb(s2, "a_vb%d" % i, [128, 16, 512], BF16) for i in range(3)]
                Wv = sb(s2, "a_wv", [128, KT, 512], BF16)
                numacc = sb(s2, "a_num", [128, S], F32)
                denacc = sb(s2, "a_den", [128, S], F32)
                load_w(Wv[:], w_in[l, :, 1024:1536], 'a_wv')
                mstrip = sb(s2, "a_mstrip", [128, 4, 512], BF16)
                kb.dma('pool', mstrip[:].rearrange("p a b -> p (a b)"), mstrip_d, w=['mstrip'])
                DIL = (1, 4, 16)
                pi = 0
                for bi, dil in enumerate(DIL):
                    nb = (S // dil) // 128
                    for ti in range(16):
                        r_, j_ = divmod(ti, nb)
                        t0 = r_ + dil * 128 * j_
                        b = pi % 2
                        pi += 1
                        for k in range(KT):
                            kb.op('pe', lambda e: e.matmul(PS[b][:], lhsT=hnT[:, k, ssl(t0, 128, dil)], rhs=Wv[:, k, :],
                                                           start=(k == 0), stop=(k == KT - 1)),
                                  r=['a_wv'] + [('hnT', c) for c in range(NCH)], w=[('ps', b)])
                        kb.op('act', lambda e: e.activation(out=vb[bi][:, ti, :], in_=PS[b][:], func=AF.Copy), r=[('ps', b)], w=[('a_vb', bi)])
                qc = [sb(s2, "a_qc%d" % i, [128, S], BF16) for i in range(2)]
                kc = [sb(s2, "a_kc%d" % i, [128, S], BF16) for i in range(2)]
                pT3 = [sb(s2, "a_pTb%d" % i, [128, 512], BF16) for i in range(3)]
                state = {'it': 0, 'cc': 0}

                def emit_S(w_):
                    (hp, bi, dil, nb, r_, q0, qn, kts, qsrc, ksrc, qk_, kk_, sub0) = w_['a']
                    it = w_['it']
                    bset = it % 2
                    p_ = pT3[it % 3]
                    pk = ('a_pT', it % 3)
                    nkt = len(kts)
                    wdt = nkt * qn
                    for hh in range(2):
                        base = 64 * hh
                        bank = 2 * bset + hh
                        for i_, (kt, mk) in enumerate(kts):
                            slot = i_ * qn
                            if dil == 1:
                                lhs = ksrc[base:base + 64, hp, 128 * kt:128 * kt + 128]
                                rhs = qsrc[base:base + 64, hp, q0:q0 + qn]
                            else:
                                lhs = ksrc[base:base + 64, sub0 + 128 * kt:sub0 + 128 * kt + 128]
                                rhs = qsrc[base:base + 64, sub0 + q0:sub0 + q0 + qn]
                            kb.op('pe', lambda e: e.matmul(PS[bank][:, slot:slot + qn], lhsT=lhs, rhs=rhs, start=True, stop=True),
                                  r=[kk_, qk_], w=[('ps', bank)], inc=(hh == 1 and i_ == nkt - 1))
                    ncols = 2 * wdt
                    sidx = 3 if kts[0][1] == 'E' else (1 if kts[0][1] == 'F' else (0 if nkt == 2 else 2))
                    kb.op('act', lambda e: e.activation(out=p_[:, 0:ncols].rearrange("p (h w) -> p h w", h=2),
                                                        in_=psall[:, 2 * bset:2 * bset + 2, 0:wdt], func=AF.Exp, scale=0.125),
                          r=[('ps', 2 * bset), ('ps', 2 * bset + 1)], w=[pk])
                    kb.op('dve', lambda e: e.tensor_tensor(out=p_[:, 0:ncols], in0=p_[:, 0:ncols], in1=mstrip[:, sidx, 0:ncols], op=ALU.mult),
                          r=['mstrip'], w=[pk])

                def emit_PV(w_):
                    (hp, bi, dil, nb, r_, q0, qn, kts, qsrc, ksrc, qk_, kk_, sub0) = w_['a']
                    it = w_['it']
                    bnk = {0: 4 + (2 * it) % 3, 128: 4 + (2 * it + 1) % 3}
                    p_ = pT3[it % 3]
                    pk = ('a_pT', it % 3)
                    nkt = len(kts)
                    qsl = ssl(r_ + dil * q0, qn, dil)
                    for (c0, is_num) in ((0, True), (128, False)):
                        for hh in range(2):
                            base = 64 * hh
                            for i_, (kt, mk) in enumerate(kts):
                                slot = (hh * nkt + i_) * qn
                                hcol = (hp * 2 + hh) * 64
                                lhs = vb[bi][:, r_ * nb + kt, hcol:hcol + 64] if is_num else ones_bf[:, 0:64]
                                kb.op('pe', lambda e: e.matmul(PS[bnk[c0]][base:base + 64, 0:qn], lhsT=lhs, rhs=p_[:, slot:slot + qn],
                                                               start=(i_ == 0), stop=(i_ == nkt - 1), tile_position=(0, base)),
                                      r=[pk, ('a_vb', bi), 'cstb'], w=[('ps', bnk[c0])], inc=(hh == 1 and i_ == nkt - 1))
                    for (c0, acc, ak, eng) in ((0, numacc, 'a_num', 'act' if bi == 0 else 'dve'), (128, denacc, 'a_den', 'act' if bi == 0 else 'dve')):
                        if bi == 0:
                            if eng == 'act':
                                kb.op('act', lambda e: e.activation(out=acc[:, qsl], in_=PS[bnk[c0]][:, 0:qn], func=AF.Copy),
                                      r=[('ps', bnk[c0])], w=[(ak, 0)], loose=True)
                            else:
                                kb.op('dve', lambda e: e.tensor_copy(out=acc[:, qsl], in_=PS[bnk[c0]][:, 0:qn]),
                                      r=[('ps', bnk[c0])], w=[(ak, 0)], loose=True)
                        else:
                            kb.op('dve', lambda e: e.tensor_tensor(out=acc[:, qsl], in0=PS[bnk[c0]][:, 0:qn], in1=acc[:, qsl], op=ALU.add),
                                  r=[('ps', bnk[c0]), (ak, bi - 1)], w=[(ak, bi)], loose=True)

                for hp in range(4):
                    items = []
                    for bi, dil in enumerate(DIL):
                        nsub = S // dil
                        nb = nsub // 128
                        if dil == 1:
                            qsrc, ksrc, qk_, kk_ = qR, kR, ('a_qR', hp), ('a_kR', hp)
                        else:
                            cc = state['cc'] % 2
                            state['cc'] += 1
                            qsrc, ksrc, qk_, kk_ = qc[cc], kc[cc], ('a_qc', cc), ('a_kc', cc)
                            kb.op('pool', lambda e: e.tensor_copy(out=qsrc[:].rearrange("p (r i) -> p r i", r=dil),
                                                                  in_=qR[:, hp, :].rearrange("p (i r) -> p r i", r=dil)),
                                  r=[('a_qR', hp)], w=[qk_])
                            kb.op('pool', lambda e: e.tensor_copy(out=ksrc[:].rearrange("p (r i) -> p r i", r=dil),
                                                                  in_=kR[:, hp, :].rearrange("p (i r) -> p r i", r=dil)),
                                  r=[('a_kR', hp)], w=[kk_])
                        if nb == 1:
                            blocks = [(0, 128, [(0, 'E')])]
                        else:
                            blocks = [(0, 64, [(0, 'F')])]
                            blocks += [(64 + 128 * j, 128, [(j, 'A'), (j + 1, 'B')]) for j in range(nb - 1)]
                            blocks += [(nsub - 64, 64, [(nb - 1, 'A')])]
                        for r_ in range(dil):
                            for (q0, qn, kts) in blocks:
                                items.append({'a': (hp, bi, dil, nb, r_, q0, qn, kts, qsrc, ksrc, qk_, kk_, r_ * nsub), 'it': state['it']})
                                state['it'] += 1
                    prev = None
                    for w_ in items:
                        emit_S(w_)
                        if prev is not None:
                            emit_PV(prev)
                        prev = w_
                    emit_PV(prev)
                    nk = [('a_num', b_) for b_ in range(3)]
                    dk_ = [('a_den', b_) for b_ in range(3)]
                    kb.op('act', lambda e: e.activation(out=denacc[:], in_=denacc[:], func=AF.Ln), r=dk_, w=dk_)
                    kb.op('act', lambda e: e.activation(out=denacc[:], in_=denacc[:], func=AF.Exp, scale=-1.0), r=dk_, w=dk_)
                    kb.op('dve', lambda e: e.tensor_tensor(out=catT[:, hp, :], in0=numacc[:], in1=denacc[:], op=ALU.mult),
                          r=nk + dk_, w=[('catT', hp)] + nk)
            kb.barrier()
            chk('attn')
            st.close()

        def ffn_phase(st0, l, xsrc, xdst, hnT):
            st = st0.enter_context(ExitStack())
            aT = sb(st, "f_aT", [128, NFT, S], BF16)
            with ExitStack() as s1:
                Wgs = [sb(s1, "f_wg%d" % i, [128, KT, 512], BF16) for i in range(2)]
                Wvs = [sb(s1, "f_wv%d" % i, [128, KT, 512], BF16) for i in range(2)]
                yp = [sb(s1, "f_yp%d" % i, [128, S + 2], F32) for i in range(2)]
                u = [sb(s1, "f_u%d" % i, [128, S], F32) for i in range(2)]
                gl = sb(s1, "f_gl", [128, S], BF16)
                for i in range(2):
                    kb.op('dve', lambda e: e.memset(yp[i][:, 0:1], 0.0), w=[('f_yp', i)])
                    kb.op('dve', lambda e: e.memset(yp[i][:, S + 1:S + 2], 0.0), w=[('f_yp', i)])
                pi = 0
                for c in range(NFT):
                    GST = [0, 1, 4, 8, 12, 16, 20, NFT]
                    gidx = max(i for i in range(len(GST) - 1) if GST[i] <= c)
                    g4 = c - GST[gidx]

                    def ldgrp(gi_x):
                        c_ = GST[gi_x]
                        n = (GST[gi_x + 1] - c_) * 128
                        sl = gi_x % 2
                        load_w(Wgs[sl][:, :, 0:n], w_up[l, :, c_ * 128:c_ * 128 + n], ('f_wg', sl))
                        load_w(Wvs[sl][:, :, 0:n], w_up[l, :, DFF + c_ * 128:DFF + c_ * 128 + n], ('f_wv', sl))
                    if c == 0:
                        ldgrp(0)
                    if g4 == 0 and gidx + 1 < len(GST) - 1:
                        ldgrp(gidx + 1)
                    gi_ = gidx % 2
                    Wg, Wv = Wgs[gi_], Wvs[gi_]
                    for part, (W, wk) in enumerate(((Wg, ('f_wg', gi_)), (Wv, ('f_wv', gi_)))):
                        ti = part * NFT + c
                        for ch in range(NCH):
                            b = pi % 4
                            pi += 1
                            for k in range(KT):
                                kb.op('pe', lambda e: e.matmul(PS[b][:], lhsT=W[:, k, g4 * 128:(g4 + 1) * 128], rhs=hnT[:, k, ch * CH:(ch + 1) * CH],
                                                               start=(k == 0), stop=(k == KT - 1)), r=[wk, ('hnT', ch)], w=[('ps', b)])
                            kb.op('act', lambda e: e.activation(out=yp[part][:, 1 + ch * CH:1 + (ch + 1) * CH], in_=PS[b][:], func=AF.Copy),
                                  r=[('ps', b)], w=[('f_yp', part)])
                        kb.op('act', lambda e: e.activation(out=u[part][:], in_=yp[part][:, 0:S], func=AF.Identity,
                                                            scale=fconv[:, l, ti, 0:1], bias=fconv[:, l, ti, 3:4]),
                              r=[('f_yp', part), 'fconv'], w=[('f_u', part)])
                        for jj in (1, 2):
                            kb.op('dve', lambda e: e.scalar_tensor_tensor(out=u[part][:], in0=yp[part][:, jj:jj + S], scalar=fconv[:, l, ti, jj:jj + 1],
                                                                          in1=u[part][:], op0=ALU.mult, op1=ALU.add),
                                  r=[('f_yp', part), 'fconv'], w=[('f_u', part)])
                    kb.op('act', lambda e: e.activation(out=gl[:], in_=u[0][:], func=AF.Gelu_apprx_tanh), r=[('f_u', 0)], w=['f_gl'])
                    kb.op('dve', lambda e: e.tensor_tensor(out=aT[:, c, :], in0=gl[:], in1=u[1][:], op=ALU.mult),
                          r=['f_gl', ('f_u', 1)], w=[('f_aT', c)])
            kb.barrier()
            chk('fup')
            with ExitStack() as s2:
                last = (l == nlayers - 1)
                epilogue(s2, w_down[l], NFT, lambda ci, tsl: aT[:, ci, tsl], lambda ci, c: [('f_aT', ci)], xsrc, xdst, l, 3,
                         che=512 if last else 256, hn_out=None if last else hnT, nxt=None if last else (l + 1, 0),
                         xc_alias=hnT if last else None)
            kb.barrier()
            st.close()

        xcur = xT_in
        hnT = sb(top, "hnT", [128, KT, S], BF16)
        try:
          for l in range(nlayers if stop_at != 'init' else 0):
              xmid = xs[0]
              xnext = yT if l == nlayers - 1 else xs[1]
              with ExitStack() as sm_:
                  catT = sb(sm_, "catT", [128, KT, S], BF16)
                  with ExitStack() as sh_:
                      if l == 0:
                          with ExitStack() as s0:
                              norm_pre(s0, xcur, l, 0, hnT)
                      kb.barrier()
                      chk('norm')
                      mlstm_phase(sh_, l, hnT, catT)
                      attn_phase(sh_, l, hnT, catT)
                  if 'cat' in dbg_out and l == dbg.get('_layer', 0):
                      with ExitStack() as sd:
                          tmpf = sb(sd, "dbg_tmp", [128, S], F32)
                          for i in range(KT):
                              kb.op('act', lambda e: e.activation(out=tmpf[:], in_=catT[:, i, :], func=AF.Copy), r=[('catT', i)], w=['dbg_tmp'])
                              kb.dma('sp', dbg_out['cat'][i * 128:(i + 1) * 128, :], tmpf[:], r=['dbg_tmp'], w=['dbg_cat'])
                      kb.barrier()
                  with ExitStack() as se:
                      epilogue(se, w_out[l], KT, lambda ci, tsl: catT[:, ci, tsl], lambda ci, c: [('catT', ci)], xcur, xmid, l, 1,
                               che=CH, hn_out=hnT, nxt=(l, 2))
                  kb.barrier()
                  chk('ep1')
              with ExitStack() as sf:
                  ffn_phase(sf, l, xmid, xnext, hnT)
              xcur = xnext
        except _Stop:
            pass
        kb.barrier()
    stuck = kb.check_deadlock()
    if stuck:
        raise RuntimeError('static deadlock: %r' % (stuck,))
    return nc, list(dbg_out.keys())


def _host_prep(inputs):
    f = np.float32
    g = np.stack([inputs['mix_pre_g'], inputs['mix_post_g'], inputs['ffn_pre_g'], inputs['ffn_post_g']], axis=1)
    gvec = np.ascontiguousarray(g.reshape(2, 4, KT, 128).transpose(3, 0, 1, 2)).reshape(128, -1).astype(f)
    mc = np.concatenate([inputs['mlstm_conv_w'], inputs['mlstm_conv_b'][:, None, :]], axis=1)
    mconv = np.ascontiguousarray(mc.reshape(2, 6, 8, 128).transpose(3, 0, 2, 1)).reshape(128, -1).astype(f)
    fc = np.concatenate([inputs['ffn_conv_w'], inputs['ffn_conv_b'][:, None, :]], axis=1)
    fconv = np.ascontiguousarray(fc.reshape(2, 4, 44, 128).transpose(3, 0, 2, 1)).reshape(128, -1).astype(f)
    gateb = np.ascontiguousarray(np.broadcast_to(inputs['mlstm_gate_b'].reshape(1, -1), (128, 32))).astype(f)
    headg = np.ascontiguousarray(np.broadcast_to(inputs['mlstm_head_g'].reshape(1, -1), (128, 1024))).astype(f)
    p = np.arange(128)
    inv_freq = (10000.0 ** (-np.arange(0, 64, 2, dtype=np.float32) / 64)).astype(np.float32)
    ang = np.arange(S, dtype=np.float32)[None, :] * inv_freq[p % 32][:, None]
    cosT = np.cos(ang).astype(f)
    sgn = np.where((p % 64) < 32, -1.0, 1.0).astype(f)[:, None]
    sinT = (np.sin(ang) * sgn).astype(f)
    a = p[:, None]
    b = p[None, :]
    NEG = -30000.0
    cst = np.zeros((128, 9, 128), f)
    cst[:, 0] = 1.0
    cst[:, 1] = (a == b)
    cst[:, 2] = np.where(a >= b, 0.0, NEG)
    cst[:, 3] = np.where(a <= b, 0.0, NEG)
    cst[:, 4] = np.where(a <= b + 64, 0.0, NEG)
    cst[:, 5] = np.where(np.abs(a - b) <= 64, 0.0, NEG)
    cst[:, 6] = (a <= b)
    cst[:, 7] = (a >= b)
    partner = np.where((p % 64) < 32, p + 32, p - 32)
    cst[:, 8] = (a == partner[None, :])
    A01 = (a >= b).astype(f); B01 = (a <= b).astype(f); F01 = (a <= b + 64).astype(f); E01 = (np.abs(a - b) <= 64).astype(f)
    ms = np.zeros((128, 4, 512), f)
    ms[:, 0] = np.concatenate([A01, B01, A01, B01], axis=1)
    ms[:, 1, 0:128] = np.concatenate([F01[:, :64], F01[:, :64]], axis=1)
    ms[:, 2, 0:128] = np.concatenate([A01[:, :64], A01[:, :64]], axis=1)
    ms[:, 3, 0:256] = np.concatenate([E01, E01], axis=1)
    shared = dict(mstrip=ms.reshape(128, -1), w_in=np.ascontiguousarray(inputs['w_in'], dtype=f), w_out=np.ascontiguousarray(inputs['w_out'], dtype=f),
                  w_up=np.ascontiguousarray(inputs['w_up'], dtype=f), w_down=np.ascontiguousarray(inputs['w_down'], dtype=f),
                  gvec=gvec, mconv=mconv, fconv=fconv, gateb=gateb, headg=headg, cosT=cosT, sinT=sinT,
                  cst=cst.reshape(128, -1))
    return shared


_NC_CACHE = {}


def kernel(**inputs):
    x = np.asarray(inputs['x'], dtype=np.float32)
    B = x.shape[0]
    shared = _host_prep(inputs)
    if 'nc' not in _NC_CACHE:
        _NC_CACHE['nc'] = build(2)[0]
    nc = _NC_CACHE['nc']
    in_maps = []
    for b in range(B):
        m = dict(shared)
        m['xT'] = np.ascontiguousarray(x[b].T)
        in_maps.append(m)
    res = run_bass_kernel_spmd(nc, in_maps, core_ids=list(range(B)))
    out = np.stack([np.ascontiguousarray(res.results[b]['yT'].T) for b in range(B)], axis=0)
    return out.astype(np.float32)
```

```python
import math
from contextlib import ExitStack
import numpy as np
import concourse.bass as bass
import concourse.mybir as mybir
from concourse.bass_utils import run_bass_kernel_spmd

F32 = mybir.dt.float32
BF16 = mybir.dt.bfloat16
AF = mybir.ActivationFunctionType
ALU = mybir.AluOpType

S = 2048
D = 1024
KT = 8
NCH = 4
CH = 512
DFF = 2816
NFT = 22
EPS = 1e-6
NDS = 24


def ssl(start, n, step):
    return slice(start, start + step * (n - 1) + 1, step)


class KB:
    def __init__(self, nc):
        self.nc = nc
        self.E = {'pe': nc.tensor, 'act': nc.scalar, 'dve': nc.vector, 'pool': nc.gpsimd, 'sp': nc.sync}
        self.sem = {e: nc.alloc_semaphore('c_' + e) for e in ('pe', 'act', 'dve', 'pool')}
        self.cnt = {e: 0 for e in self.sem}
        self.waited = {e: {} for e in self.E}
        self.dsems = [nc.alloc_semaphore('d%d' % i) for i in range(NDS)]
        self.dcnt = [0] * NDS
        self.dlast = [None] * NDS
        self.dnext = 0
        self.dnext_p = 0
        self.tk = {}
        self.dead = False
        self.prog = {e: [] for e in self.E}

    def _t(self, key):
        t = self.tk.get(key)
        if t is None:
            t = {'w': None, 'r': {}}
            self.tk[key] = t
        return t

    def _wait(self, e, ev):
        sem, val, sid = ev
        if self.waited[e].get(sid, 0) >= val:
            return
        self.E[e].wait_ge(sem, val)
        self.prog[e].append(('w', sid, val))
        self.waited[e][sid] = val

    def _deps(self, e, r, w, loose=False):
        evs = {}

        def add(ev, from_w):
            if ev is None:
                return
            if ev[2] == e:
                if e == 'pe' or (loose and from_w):
                    return
            if ev[2] not in evs or evs[ev[2]][1] < ev[1]:
                evs[ev[2]] = ev
        for k in r:
            add(self._t(k)['w'], False)
        for k in w:
            t = self._t(k)
            add(t['w'], True)
            for ev in t['r'].values():
                add(ev, True)
        for sid, ev in evs.items():
            self._wait(e, ev)

    def _record(self, ev, r, w):
        for k in r:
            self._t(k)['r'][ev[2]] = ev
        for k in w:
            t = self._t(k)
            t['w'] = ev
            t['r'] = {}

    def op(self, e, fn, r=(), w=(), loose=False, inc=True):
        if self.dead:
            return
        self._deps(e, r, w, loose)
        ins = fn(self.E[e])
        if inc:
            self.cnt[e] += 1
            ins.then_inc(self.sem[e], 1)
            self.prog[e].append(('i', e, 1))
            self._record((self.sem[e], self.cnt[e], e), r, w)
        else:
            self._record((self.sem[e], self.cnt[e] + 1, e), r, w)

    def dma(self, q, out, in_, r=(), w=()):
        if self.dead:
            return
        if q == 'pool':
            i = 16 + self.dnext_p
            self.dnext_p = (self.dnext_p + 1) % (NDS - 16)
        else:
            i = self.dnext
            self.dnext = (i + 1) % 16
        if self.dlast[i] is not None:
            self._wait(q, self.dlast[i])
        self._deps(q, r, w)
        ins = self.E[q].dma_start(out=out, in_=in_)
        self.dcnt[i] += 16
        ins.then_inc(self.dsems[i], 16)
        self.prog[q].append(('i', 'd%d' % i, 16))
        ev = (self.dsems[i], self.dcnt[i], 'd%d' % i)
        self.dlast[i] = ev
        self._record(ev, r, w)
        return ev

    def check_deadlock(self):
        pc = {e: 0 for e in self.prog}
        val = {}
        progress = True
        while progress:
            progress = False
            for e, p in self.prog.items():
                while pc[e] < len(p):
                    k, sid, v = p[pc[e]]
                    if k == 'w':
                        if val.get(sid, 0) < v:
                            break
                    else:
                        val[sid] = val.get(sid, 0) + v
                    pc[e] += 1
                    progress = True
        stuck = {e: (pc[e], len(p), p[pc[e]]) for e, p in self.prog.items() if pc[e] < len(p)}
        return stuck

    def barrier(self):
        if self.dead:
            return
        evs = [(self.sem[e], self.cnt[e], e) for e in self.sem if self.cnt[e] > 0]
        evs += [ev for ev in self.dlast if ev is not None]
        for e in self.E:
            for ev in evs:
                if ev[2] == e:
                    continue
                self._wait(e, ev)


class _Stop(Exception):
    pass


def build(nlayers=2, dbg=None, stop_at=None):
    dbg = dbg or {}
    nc = bass.Bass("TRN2", target_bir_lowering=False)
    kb = KB(nc)
    ein = lambda n, s, dt=F32: nc.dram_tensor(n, list(s), dt, kind="ExternalInput").ap()
    xT_in = ein("xT", [D, S])
    w_in = ein("w_in", [2, D, 3600])
    w_out = ein("w_out", [2, D, D])
    w_up = ein("w_up", [2, D, 2 * DFF])
    w_down = ein("w_down", [2, DFF, D])
    gvec_d = ein("gvec", [128, 2 * 4 * KT])
    mconv_d = ein("mconv", [128, 2 * 8 * 6])
    fconv_d = ein("fconv", [128, 2 * 44 * 4])
    gateb_d = ein("gateb", [128, 2 * 16])
    headg_d = ein("headg", [128, 2 * 512])
    cos_d = ein("cosT", [128, S])
    sin_d = ein("sinT", [128, S])
    mstrip_d = ein("mstrip", [128, 4 * 512])
    cst_d = ein("cst", [128, 9 * 128])
    yT = nc.dram_tensor("yT", [D, S], F32, kind="ExternalOutput").ap()
    xs = [nc.dram_tensor("xs%d" % i, [D, S], F32).ap() for i in range(2)]
    dbg_out = {}
    for name, shape in dbg.items():
        dbg_out[name] = nc.dram_tensor("dbg_" + name, list(shape), F32, kind="ExternalOutput").ap()

    pm = lambda ap: ap.rearrange("(k p) t -> p k t", p=128)

    with ExitStack() as top:
        uid = [0]

        def sb(st, name, shape, dt):
            uid[0] += 1
            return st.enter_context(nc.sbuf_tensor("s%d_%s" % (uid[0], name), list(shape), dt))
        gvec = sb(top, "gvec", [128, 2, 4, KT], F32)
        mconv = sb(top, "mconv", [128, 2, 8, 6], F32)
        fconv = sb(top, "fconv", [128, 2, 44, 4], F32)
        gateb = sb(top, "gateb", [128, 2, 16], F32)
        headg = sb(top, "headg", [128, 2, 512], BF16)
        cstb = sb(top, "cstb", [128, 9, 128], BF16)
        epsb = sb(top, "epsb", [128, 1], F32)
        psall = top.enter_context(nc.psum_tensor("psall", [128, 7, 512], F32))
        PS = [psall[:, i, :] for i in range(7)]
        PST = top.enter_context(nc.psum_tensor("pst", [128, 1024], BF16))
        for i in range(7):
            kb._t(('ps', i))
        kb.dma('sp', gvec[:].rearrange("p a b c -> p (a b c)"), gvec_d, w=['gvec'])
        kb.dma('sp', mconv[:].rearrange("p a b c -> p (a b c)"), mconv_d, w=['mconv'])
        kb.dma('sp', fconv[:].rearrange("p a b c -> p (a b c)"), fconv_d, w=['fconv'])
        kb.dma('sp', gateb[:].rearrange("p a b -> p (a b)"), gateb_d, w=['gateb'])
        kb.dma('pool', headg[:].rearrange("p a b -> p (a b)"), headg_d, w=['headg'])
        kb.dma('pool', cstb[:].rearrange("p a b -> p (a b)"), cst_d, w=['cstb'])
        cview = cst_d.rearrange("p (a b) -> p a b", b=128)
        kb.op('dve', lambda e: e.memset(epsb[:], EPS), w=['epsb'])
        ones_bf = cstb[:, 0, :]
        ident_bf = cstb[:, 1, :]
        MASK = {'A': cstb[:, 2, :], 'B': cstb[:, 3, :], 'F': cstb[:, 4, :], 'E': cstb[:, 5, :]}
        tri_bf = [cstb[:, 6, :], cstb[:, 7, :]]

        def chk(name):
            if stop_at == name:
                kb.barrier()
                kb.dead = True

        def dump(name, sb_ap, rkeys):
            if name in dbg_out:
                kb.dma('sp', dbg_out[name], sb_ap, r=rkeys, w=['dbg_' + name])

        def rstd_from_ps(ps_i, rstd_ap, rkey, n):
            kb.op('act', lambda e: e.activation(out=rstd_ap, in_=PS[ps_i][:, 0:rstd_ap.shape[1]], func=AF.Ln,
                                                scale=1.0 / n, bias=epsb[:, 0:1]),
                  r=[('ps', ps_i), 'epsb'], w=[rkey])
            kb.op('act', lambda e: e.activation(out=rstd_ap, in_=rstd_ap, func=AF.Exp, scale=-0.5), r=[rkey], w=[rkey])

        def norm_pre(st, xsrc, l, j, hnT):
            xcs = [sb(st, "np_xc%d" % i, [128, KT, CH], F32) for i in range(2)]
            sq = sb(st, "np_sq", [128, KT, CH], BF16)
            rstd = sb(st, "np_rstd", [128, CH], F32)
            for c in range(NCH):
                xc = xcs[c % 2]
                xk = ('np_xc', c % 2)
                kb.dma('sp', xc[:], pm(xsrc)[:, :, c * CH:(c + 1) * CH], w=[xk])
                kb.op('act', lambda e: e.activation(out=sq[:], in_=xc[:], func=AF.Square), r=[xk], w=['np_sq'])
                for k in range(KT):
                    kb.op('pe', lambda e: e.matmul(PS[6][:], lhsT=ones_bf, rhs=sq[:, k, :], start=(k == 0), stop=(k == KT - 1)),
                          r=['np_sq', 'cstb'], w=[('ps', 6)])
                rstd_from_ps(6, rstd[:], 'np_rstd', D)
                for k in range(KT):
                    kb.op('dve', lambda e: e.scalar_tensor_tensor(out=hnT[:, k, c * CH:(c + 1) * CH], in0=xc[:, k, :],
                                                                  scalar=gvec[:, l, j, k:k + 1], in1=rstd[:],
                                                                  op0=ALU.mult, op1=ALU.mult),
                          r=[xk, 'np_rstd', 'gvec'], w=[('hnT', c)])

        def load_w(dst, src_rows_cols, wkey):
            kb.dma('pool', dst, src_rows_cols.rearrange("(k p) c -> p k c", p=128), w=[wkey])

        def epilogue(st, Wd, nct, rhs_fn, rkeys_fn, xsrc, xdst, l, j, che=CH, hn_out=None, nxt=None, xc_alias=None):
            W = sb(st, "ep_w", [128, nct, D], BF16)
            if xc_alias is not None:
                xcs = [xc_alias[:, 4 * i:4 * i + 4, :].bitcast(F32).rearrange("p a (b t) -> p (a b) t", t=che) for i in range(2)]
            else:
                xcs = [sb(st, "ep_xc%d" % i, [128, KT, che], F32) for i in range(2)]
            ff = sb(st, "ep_ff", [128, KT, che], F32)
            sq = sb(st, "ep_sq", [128, KT, che], BF16)
            rstd = sb(st, "ep_rstd", [128, che], F32)
            if hn_out is not None:
                sq2 = sb(st, "ep_sq2", [128, KT, che], BF16)
                rstd2 = sb(st, "ep_rstd2", [128, che], F32)
            for qt in range(4):
                for k0 in range(0, nct, 8):
                    k1 = min(nct, k0 + 8)
                    load_w(W[:, k0:k1, qt * 256:(qt + 1) * 256], Wd[k0 * 128:k1 * 128, qt * 256:(qt + 1) * 256], ('ep_w', qt))
            state = {'pi': 0}
            nchunk = S // che

            def partA(c):
                tsl = slice(c * che, (c + 1) * che)
                xc = xcs[c % 2]
                xk = ('ep_xc', c % 2)
                kb.dma('sp', xc[:], pm(xsrc)[:, :, tsl], w=[xk])
                for m in range(KT):
                    b = state['pi'] % 4
                    state['pi'] += 1
                    for ci in range(nct):
                        kb.op('pe', lambda e: e.matmul(PS[b][:, 0:che], lhsT=W[:, ci, m * 128:(m + 1) * 128], rhs=rhs_fn(ci, tsl),
                                                       start=(ci == 0), stop=(ci == nct - 1)),
                              r=[('ep_w', m // 2)] + rkeys_fn(ci, c), w=[('ps', b)], inc=(ci == nct - 1))
                    kb.op('act', lambda e: e.activation(out=ff[:, m, :], in_=PS[b][:, 0:che], func=AF.Copy), r=[('ps', b)], w=[('ep_ff', m)])
                    kb.op('act', lambda e: e.activation(out=sq[:, m, :], in_=ff[:, m, :], func=AF.Square), r=[('ep_ff', m)], w=[('ep_sq', m)])

            def partB(c):
                tsl = slice(c * che, (c + 1) * che)
                xc = xcs[c % 2]
                xk = ('ep_xc', c % 2)
                for m in range(KT):
                    kb.op('pe', lambda e: e.matmul(PS[6][:, 0:che], lhsT=ones_bf, rhs=sq[:, m, :], start=(m == 0), stop=(m == KT - 1)),
                          r=[('ep_sq', m), 'cstb'], w=[('ps', 6)], inc=(m == KT - 1))
                rstd_from_ps(6, rstd[:], 'ep_rstd', D)
                for m in range(KT):
                    kb.op('dve', lambda e: e.scalar_tensor_tensor(out=ff[:, m, :], in0=ff[:, m, :], scalar=gvec[:, l, j, m:m + 1],
                                                                  in1=rstd[:], op0=ALU.mult, op1=ALU.mult),
                          r=['ep_rstd', 'gvec'], w=[('ep_ff', m)])
                    kb.op('dve', lambda e: e.tensor_tensor(out=xc[:, m, :], in0=xc[:, m, :], in1=ff[:, m, :], op=ALU.add),
                          r=[('ep_ff', m)], w=[xk])
                kb.dma('sp', pm(xdst)[:, :, tsl], xc[:], r=[xk], w=[('xdst', c)])

            def partC(c):
                if hn_out is None:
                    return
                tsl = slice(c * che, (c + 1) * che)
                xc = xcs[c % 2]
                xk = ('ep_xc', c % 2)
                l2, j2 = nxt
                kb.op('act', lambda e: e.activation(out=sq2[:], in_=xc[:], func=AF.Square), r=[xk], w=['ep_sq2'])
                for m in range(KT):
                    kb.op('pe', lambda e: e.matmul(PS[5][:, 0:che], lhsT=ones_bf, rhs=sq2[:, m, :], start=(m == 0), stop=(m == KT - 1)),
                          r=['ep_sq2', 'cstb'], w=[('ps', 5)], inc=(m == KT - 1))
                rstd_from_ps(5, rstd2[:], 'ep_rstd2', D)
                for m in range(KT):
                    kb.op('dve', lambda e: e.scalar_tensor_tensor(out=hn_out[:, m, tsl], in0=xc[:, m, :], scalar=gvec[:, l2, j2, m:m + 1],
                                                                  in1=rstd2[:], op0=ALU.mult, op1=ALU.mult),
                          r=[xk, 'ep_rstd2', 'gvec'], w=[('hnT', (c * che) // CH)])

            partA(0)
            for c in range(nchunk):
                partB(c)
                if c + 1 < nchunk:
                    partA(c + 1)
                partC(c)

        def mlstm_phase(st0, l, hnT, catT):
            st = st0.enter_context(ExitStack())
            qkT = sb(st, "m_qkT", [128, 8, S], BF16)
            kTM = sb(st, "m_kTM", [128, 16, 512], BF16)
            vaug = sb(st, "m_vaug", [128, 16, 4, 132], BF16)
            gsig = sb(st, "m_gsig", [128, 16, 512], BF16)
            gates = sb(st, "m_gates", [128, 16, 16], F32)
            l1 = sb(st, "m_l1", [128, 16, 8], F32)
            gd = sb(st, "m_gd", [128, 5, 2, 16, 4], F32)
            with ExitStack() as sp_:
                Wa = sb(sp_, "m_wa", [128, KT, 512], BF16)
                Wb = sb(sp_, "m_wb", [128, KT, 512], BF16)
                Wg = sb(sp_, "m_wg", [128, KT, 16], BF16)
                ypads = [sb(sp_, "m_ypad%d" % i, [128, S + 4], BF16) for i in range(2)]
                uaccs = [sb(sp_, "m_uacc%d" % i, [128, S], F32) for i in range(2)]
                sgt = sb(sp_, "m_sgt", [128, 512], BF16)
                load_w(Wa[:], w_in[l, :, 1536:2048], 'm_wa')
                load_w(Wb[:], w_in[l, :, 2048:2560], 'm_wb')
                load_w(Wg[:], w_in[l, :, 3584:3600], 'm_wg')
                for i in range(2):
                    kb.op('dve', lambda e: e.memset(ypads[i][:, 0:2], 0.0), w=[('m_ypad', i)])
                    kb.op('dve', lambda e: e.memset(ypads[i][:, S + 2:S + 4], 0.0), w=[('m_ypad', i)])
                kb.op('dve', lambda e: e.memset(vaug[:, :, :, 128:129], 1.0), w=['m_vaug1'])
                cst_ = {'pi': 0}

                def convA(i):
                    W = Wa if i < 4 else Wb
                    wk = 'm_wa' if i < 4 else 'm_wb'
                    cs = (i % 4) * 128
                    ypad, uacc = ypads[i % 2], uaccs[i % 2]
                    yk, uk = ('m_ypad', i % 2), ('m_uacc', i % 2)
                    for c in range(NCH):
                        b = cst_['pi'] % 4
                        cst_['pi'] += 1
                        for k in range(KT):
                            kb.op('pe', lambda e: e.matmul(PS[b][:], lhsT=W[:, k, cs:cs + 128], rhs=hnT[:, k, c * CH:(c + 1) * CH],
                                                           start=(k == 0), stop=(k == KT - 1)),
                                  r=[wk, ('hnT', c)], w=[('ps', b)], inc=(k == KT - 1))
                        kb.op('act', lambda e: e.activation(out=ypad[:, 2 + c * CH:2 + (c + 1) * CH], in_=PS[b][:], func=AF.Copy),
                              r=[('ps', b)], w=[yk])
                    kb.op('act', lambda e: e.activation(out=uacc[:], in_=ypad[:, 0:S], func=AF.Identity,
                                                        scale=mconv[:, l, i, 0:1], bias=mconv[:, l, i, 5:6]),
                          r=[yk, 'mconv'], w=[uk])

                def convB(i):
                    ypad, uacc = ypads[i % 2], uaccs[i % 2]
                    yk, uk = ('m_ypad', i % 2), ('m_uacc', i % 2)
                    for jj in range(1, 5):
                        kb.op('dve', lambda e: e.scalar_tensor_tensor(out=uacc[:], in0=ypad[:, jj:jj + S], scalar=mconv[:, l, i, jj:jj + 1],
                                                                      in1=uacc[:], op0=ALU.mult, op1=ALU.add),
                              r=[yk, 'mconv'], w=[uk])

                def convC(i):
                    uacc, uk = uaccs[i % 2], ('m_uacc', i % 2)
                    kb.op('act', lambda e: e.activation(out=qkT[:, i, :], in_=uacc[:], func=AF.Silu), r=[uk], w=[('m_qkT', i)])

                convA(0)
                for i in range(8):
                    convB(i)
                    if i + 1 < 8:
                        convA(i + 1)
                    convC(i)
                if 'qk' in dbg_out:
                    for i in range(8):
                        kb.op('act', lambda e: e.activation(out=uaccs[0][:], in_=qkT[:, i, :], func=AF.Copy), r=[('m_qkT', i)], w=[('m_uacc', 0)])
                        dump_ap = dbg_out['qk'][i * 128:(i + 1) * 128, :]
                        kb.dma('sp', dump_ap, uaccs[0][:], r=[('m_uacc', 0)], w=['dbg_qk'])
                for c in range(16):
                    for h in range(4):
                        kb.op('pe', lambda e: e.transpose(PST[:, h * 128:(h + 1) * 128], qkT[:, 4 + h, c * 128:(c + 1) * 128], ident_bf),
                              r=[('m_qkT', 4 + h), 'cstb'], w=['pst'])
                    kb.op('act', lambda e: e.activation(out=kTM[:, c, :], in_=PST[:, 0:512], func=AF.Copy), r=['pst'], w=[('m_kTM', c)])
                load_w(Wa[:], w_in[l, :, 2560:3072], 'm_wa')
                load_w(Wb[:], w_in[l, :, 3072:3584], 'm_wb')
                for c in range(16):
                    tk = ('hnT', c // 4)
                    for k in range(KT):
                        kb.op('pe', lambda e: e.matmul(PS[0][:], lhsT=hnT[:, k, c * 128:(c + 1) * 128], rhs=Wa[:, k, :],
                                                       start=(k == 0), stop=(k == KT - 1)), r=['m_wa', tk], w=[('ps', 0)])
                    kb.op('act', lambda e: e.activation(out=vaug[:, c, :, 0:128], in_=PS[0][:].rearrange("p (h f) -> p h f", f=128), func=AF.Copy),
                          r=[('ps', 0)], w=[('m_vaug', c)])
                    for k in range(KT):
                        kb.op('pe', lambda e: e.matmul(PS[1][:], lhsT=hnT[:, k, c * 128:(c + 1) * 128], rhs=Wb[:, k, :],
                                                       start=(k == 0), stop=(k == KT - 1)), r=['m_wb', tk], w=[('ps', 1)])
                    kb.op('act', lambda e: e.activation(out=sgt[:], in_=PS[1][:], func=AF.Sigmoid), r=[('ps', 1)], w=['m_sgt'])
                    kb.op('dve', lambda e: e.tensor_tensor(out=gsig[:, c, :], in0=sgt[:], in1=headg[:, l, :], op=ALU.mult),
                          r=['m_sgt', 'headg'], w=[('m_gsig', c)])
                    for k in range(KT):
                        kb.op('pe', lambda e: e.matmul(PS[2][:, 0:16], lhsT=hnT[:, k, c * 128:(c + 1) * 128], rhs=Wg[:, k, :],
                                                       start=(k == 0), stop=(k == KT - 1)), r=['m_wg', tk], w=[('ps', 2)])
                    kb.op('dve', lambda e: e.tensor_tensor(out=gates[:, c, :], in0=PS[2][:, 0:16], in1=gateb[:, l, :], op=ALU.add),
                          r=[('ps', 2), 'gateb'], w=['m_gates'])
                for d_ in range(2):
                    kb.op('act', lambda e: e.activation(out=l1[:, :, 4 * d_:4 * d_ + 4], in_=gates[:, :, 8 * d_ + 4:8 * d_ + 8], func=AF.Exp, scale=-1.0),
                          r=['m_gates'], w=['m_l1'])
                kb.op('act', lambda e: e.activation(out=l1[:], in_=l1[:], func=AF.Ln, bias=1.0), r=['m_l1'], w=['m_l1'])
                l1h = sb(sp_, "m_l1h", [128, 2, 16, 8], BF16)
                l1r = sb(sp_, "m_l1r", [128, 16, 8], F32)
                kb.op('dve', lambda e: e.tensor_copy(out=l1h[:, 0], in_=l1[:]), r=['m_l1'], w=['m_l1h'])
                kb.op('dve', lambda e: e.tensor_copy(out=l1r[:], in_=l1h[:, 0]), r=['m_l1h'], w=['m_l1r'])
                kb.op('dve', lambda e: e.tensor_tensor(out=l1r[:], in0=l1[:], in1=l1r[:], op=ALU.subtract), r=['m_l1', 'm_l1r'], w=['m_l1r'])
                kb.op('dve', lambda e: e.tensor_copy(out=l1h[:, 1], in_=l1r[:]), r=['m_l1r'], w=['m_l1h'])
                for d_ in range(2):
                    for (bank, lhs) in ((3, tri_bf[d_]), (4, ones_bf)):
                        for hl in range(2):
                            kb.op('pe', lambda e: e.matmul(PS[bank][:, 64 * d_:64 * d_ + 64].rearrange("p (c h) -> p c h", h=4), lhsT=lhs,
                                                           rhs=l1h[:, hl, :, 4 * d_:4 * d_ + 4], start=(hl == 0), stop=(hl == 1)),
                                  r=['m_l1h', 'cstb'], w=[('ps', bank)])
                    na = gd[:, 0, d_]
                    kb.op('act', lambda e: e.activation(out=na, in_=PS[3][:, 64 * d_:64 * d_ + 64].rearrange("p (c h) -> p c h", h=4), func=AF.Copy),
                          r=[('ps', 3)], w=['m_gd'])
                    kb.op('dve', lambda e: e.tensor_tensor(out=gd[:, 1, d_], in0=gates[:, :, 8 * d_:8 * d_ + 4], in1=na, op=ALU.add),
                          r=['m_gates', 'm_gd'], w=['m_gd'])
                    kb.op('act', lambda e: e.activation(out=gd[:, 1, d_], in_=gd[:, 1, d_], func=AF.Exp, bias=-0.5 * math.log(128.0)),
                          r=['m_gd'], w=['m_gd'])
                    kb.op('act', lambda e: e.activation(out=gd[:, 2, d_], in_=na, func=AF.Exp), r=['m_gd'], w=['m_gd'])
                    kb.op('act', lambda e: e.activation(out=gd[:, 3, d_], in_=PS[4][:, 64 * d_:64 * d_ + 64].rearrange("p (c h) -> p c h", h=4),
                                                        func=AF.Exp, scale=-1.0), r=[('ps', 4)], w=['m_gd'])
                    kb.op('dve', lambda e: e.tensor_tensor(out=gd[:, 4, d_], in0=gd[:, 1, d_], in1=gd[:, 3, d_], op=ALU.mult),
                          r=['m_gd'], w=['m_gd'])
            kb.barrier()
            chk('mproj')
            with ExitStack() as ss_:
                hm = sb(ss_, "m_hm", [128, 16, 512], F32)
                Cst = sb(ss_, "m_C", [128, 8, 132], F32)
                Cbf = sb(ss_, "m_Cbf", [128, 8, 132], BF16)
                PT = [sb(ss_, "m_PT%d" % i, [128, 128], BF16) for i in range(4)]
                Kt = [sb(ss_, "m_Kt%d" % i, [128, 128], BF16) for i in range(4)]
                sm = sb(ss_, "m_sm", [128, 8, 4], F32)
                ssh = sb(ss_, "m_ssh", [128, 16, 4], F32)
                junk = sb(ss_, "m_junk", [128, 128], BF16)
                mot = sb(ss_, "m_mot", [128, 512], BF16)
                kb.op('dve', lambda e: e.memset(Cst[:], 0.0), w=[('m_C', i) for i in range(8)])
                kb.op('dve', lambda e: e.memset(Cbf[:], 0.0), w=[('m_Cbf', i) for i in range(8)])
                written = set()

                def scanA(w_):
                    (it, step, h, d_, c) = w_
                    hd = h * 2 + d_
                    tok = slice(c * 128, (c + 1) * 128)
                    bs, bn, bu = it % 2, 2 + it % 2, 4 + it % 2
                    pt, ktl = PT[it % 4], Kt[it % 4]
                    ptk, ktk = ('m_PT', it % 4), ('m_Kt', it % 4)
                    kb.op('pe', lambda e: e.matmul(PS[bs][:, 0:128], lhsT=qkT[:, 4 + h, tok], rhs=qkT[:, h, tok], start=True, stop=True),
                          r=[('m_qkT', h), ('m_qkT', 4 + h)], w=[('ps', bs)])
                    kb.op('dve', lambda e: e.scalar_tensor_tensor(out=pt[:], in0=PS[bs][:, 0:128], scalar=gd[:, 1, d_, c, h:h + 1],
                                                                  in1=tri_bf[d_], op0=ALU.mult, op1=ALU.mult),
                          r=[('ps', bs), 'm_gd', 'cstb'], w=[ptk])
                    kb.op('act', lambda e: e.activation(out=ktl[:], in_=kTM[:, c, h * 128:(h + 1) * 128], func=AF.Copy, scale=gd[:, 4, d_, c, h:h + 1]),
                          r=[('m_kTM', c), 'm_gd'], w=[ktk])
                    kb.op('pe', lambda e: e.matmul(PS[bn][:, 0:129], lhsT=pt[:], rhs=vaug[:, c, h, 0:129], start=True, stop=False),
                          r=[ptk, ('m_vaug', c), 'm_vaug1'], w=[('ps', bn)])
                    kb.op('pe', lambda e: e.matmul(PS[bn][:, 0:129], lhsT=qkT[:, h, tok], rhs=Cbf[:, hd, 0:129], start=False, stop=True),
                          r=[('m_qkT', h), ('m_Cbf', hd)], w=[('ps', bn)])
                    kb.op('pe', lambda e: e.matmul(PS[bu][:, 0:129], lhsT=ktl[:], rhs=vaug[:, c, h, 0:129], start=True, stop=True),
                          r=[ktk, ('m_vaug', c), 'm_vaug1'], w=[('ps', bu)])
                    smk = ('m_sm', hd)
                    kb.op('act', lambda e: e.activation(out=sm[:, hd, 0:1], in_=PS[bn][:, 128:129], func=AF.Abs), r=[('ps', bn)], w=[smk])

                def scanB(w_):
                    (it, step, h, d_, c) = w_
                    hd = h * 2 + d_
                    bs, bn, bu = it % 2, 2 + it % 2, 4 + it % 2
                    smk = ('m_sm', hd)
                    kb.op('dve', lambda e: e.tensor_tensor(out=sm[:, hd, 1:2], in0=sm[:, hd, 0:1], in1=gd[:, 2, d_, c, h:h + 1], op=ALU.max),
                          r=[smk, 'm_gd'], w=[smk])
                    kb.op('dve', lambda e: e.reciprocal(out=sm[:, hd, 2:3], in_=sm[:, hd, 1:2]), r=[smk], w=[smk])
                    hk = ('m_hm', c, h)
                    hdst = hm[:, c, h * 128:(h + 1) * 128]
                    if (c, h) not in written:
                        written.add((c, h))
                        kb.op('act', lambda e: e.activation(out=hdst, in_=PS[bn][:, 0:128], func=AF.Copy, scale=sm[:, hd, 2:3]),
                              r=[('ps', bn), smk], w=[hk])
                    else:
                        kb.op('dve', lambda e: e.scalar_tensor_tensor(out=hdst, in0=PS[bn][:, 0:128], scalar=sm[:, hd, 2:3], in1=hdst,
                                                                      op0=ALU.mult, op1=ALU.add),
                              r=[('ps', bn), smk], w=[hk])
                    kb.op('dve', lambda e: e.scalar_tensor_tensor(out=Cst[:, hd, 0:129], in0=Cst[:, hd, 0:129], scalar=gd[:, 3, d_, c, h:h + 1],
                                                                  in1=PS[bu][:, 0:129], op0=ALU.mult, op1=ALU.add),
                          r=[('ps', bu), 'm_gd'], w=[('m_C', hd)])
                    kb.op('pool', lambda e: e.tensor_copy(out=Cbf[:, hd, 0:129], in_=Cst[:, hd, 0:129]),
                          r=[('m_C', hd)], w=[('m_Cbf', hd)])

                work = []
                for step in range(16):
                    for h in range(4):
                        for d_ in range(2):
                            work.append((len(work), step, h, d_, step if d_ == 0 else 15 - step))
                prevw = None
                for w_ in work:
                    scanA(w_)
                    if prevw is not None:
                        scanB(prevw)
                    prevw = w_
                scanB(prevw)
                if 'hm' in dbg_out:
                    kb.dma('sp', dbg_out['hm'].rearrange("(c p) f -> p c f", p=128), hm[:], r=[('m_hm', c, h) for c in range(16) for h in range(4)], w=['dbg_hm'])
                mots = [mot, sb(ss_, "m_mot2", [128, 512], BF16)]
                junks = [junk] + [sb(ss_, "m_junk%d" % i, [128, 128], BF16) for i in range(1, 4)]
                for c in range(16):
                    sk = ('m_ssh', c)
                    mt, mk_ = mots[c % 2], ('m_mot', c % 2)
                    for h in range(4):
                        kb.op('act', lambda e: e.activation(out=junks[h][:], in_=hm[:, c, h * 128:(h + 1) * 128], func=AF.Square,
                                                            accum_out=ssh[:, c, h:h + 1]), r=[('m_hm', c, h)], w=[('m_junk', h), (sk, h)])
                    kb.op('act', lambda e: e.activation(out=ssh[:, c, :], in_=ssh[:, c, :], func=AF.Ln, scale=1.0 / 128, bias=epsb[:, 0:1]),
                          r=[(sk, h_) for h_ in range(4)] + ['epsb'], w=[sk])
                    kb.op('act', lambda e: e.activation(out=ssh[:, c, :], in_=ssh[:, c, :], func=AF.Exp, scale=-0.5), r=[sk], w=[sk])
                    for h in range(4):
                        kb.op('dve', lambda e: e.scalar_tensor_tensor(out=mt[:, h * 128:(h + 1) * 128], in0=hm[:, c, h * 128:(h + 1) * 128],
                                                                      scalar=ssh[:, c, h:h + 1], in1=gsig[:, c, h * 128:(h + 1) * 128],
                                                                      op0=ALU.mult, op1=ALU.mult),
                              r=[('m_hm', c, h), sk, ('m_gsig', c)], w=[mk_], loose=True)
                    for h in range(4):
                        kb.op('pe', lambda e: e.transpose(PST[:, (c % 2) * 512 + h * 128:(c % 2) * 512 + (h + 1) * 128], mt[:, h * 128:(h + 1) * 128], ident_bf),
                              r=[mk_, 'cstb'], w=['pst'], inc=(h == 3))
                    kb.op('act', lambda e: e.activation(out=catT[:, 4:8, c * 128:(c + 1) * 128],
                                                        in_=PST[:, (c % 2) * 512:(c % 2) * 512 + 512].rearrange("p (h t) -> p h t", t=128), func=AF.Copy),
                          r=['pst'], w=[('catT', 4 + h_) for h_ in range(4)], loose=True)
            kb.barrier()
            chk('mlstm')
            st.close()

        def attn_phase(st0, l, hnT, catT):
            st = st0.enter_context(ExitStack())
            qR = sb(st, "a_qR", [128, 4, S], BF16)
            kR = sb(st, "a_kR", [128, 4, S], BF16)
            with ExitStack() as s1:
                cosT = sb(s1, "a_cos", [128, S], F32)
                sinT = sb(s1, "a_sin", [128, S], F32)
                Wns = [sb(s1, "a_wn%d" % i, [128, KT, 512], BF16) for i in range(2)]
                t1s = [sb(s1, "a_t1%d" % i, [128, CH], F32) for i in range(2)]
                t2s = [sb(s1, "a_t2%d" % i, [128, CH], F32) for i in range(2)]
                kb.dma('sp', cosT[:], cos_d, w=['a_cos'])
                kb.dma('sp', sinT[:], sin_d, w=['a_sin'])
                xbs = [sb(s1, "a_xb%d" % i, [128, CH], BF16) for i in range(2)]
                load_w(Wns[0][:], w_in[l, :, 0:512], ('a_wn', 0))
                load_w(Wns[1][:], w_in[l, :, 512:1024], ('a_wn', 1))
                tiles = [(qk, hp, c) for qk in range(2) for hp in range(4) for c in range(NCH)]

                def rotA(i):
                    qk, hp, c = tiles[i]
                    ba = i % 2
                    for k in range(KT):
                        kb.op('pe', lambda e: e.matmul(PS[ba][:], lhsT=Wns[qk][:, k, hp * 128:(hp + 1) * 128], rhs=hnT[:, k, c * CH:(c + 1) * CH],
                                                       start=(k == 0), stop=(k == KT - 1)), r=[('a_wn', qk), ('hnT', c)], w=[('ps', ba)],
                              inc=(k == KT - 1))
                    kb.op('act', lambda e: e.activation(out=xbs[i % 2][:], in_=PS[ba][:], func=AF.Copy), r=[('ps', ba)], w=[('a_xb', i % 2)])

                def rotB(i):
                    qk, hp, c = tiles[i]
                    dst = qR if qk == 0 else kR
                    dk = 'a_qR' if qk == 0 else 'a_kR'
                    ba, bb = i % 2, 2 + i % 2
                    t1, t2 = t1s[i % 2], t2s[i % 2]
                    t1k, t2k = ('a_t1', i % 2), ('a_t2', i % 2)
                    kb.op('pe', lambda e: e.matmul(PS[bb][:], lhsT=cstb[:, 8, :], rhs=xbs[i % 2][:], start=True, stop=True),
                          r=[('a_xb', i % 2), 'cstb'], w=[('ps', bb)])
                    kb.op('dve', lambda e: e.tensor_tensor(out=t1[:], in0=PS[ba][:], in1=cosT[:, c * CH:(c + 1) * CH], op=ALU.mult),
                          r=[('ps', ba), 'a_cos', ('a_xb', i % 2)], w=[t1k])
                    kb.op('dve', lambda e: e.tensor_tensor(out=t2[:], in0=PS[bb][:], in1=sinT[:, c * CH:(c + 1) * CH], op=ALU.mult),
                          r=[('ps', bb), 'a_sin'], w=[t2k])
                    kb.op('dve', lambda e: e.tensor_tensor(out=dst[:, hp, c * CH:(c + 1) * CH], in0=t1[:], in1=t2[:], op=ALU.add),
                          r=[t1k, t2k], w=[(dk, hp)])

                rotA(0)
                for i in range(len(tiles)):
                    if i + 1 < len(tiles):
                        rotA(i + 1)
                    rotB(i)
            kb.barrier()
            chk('aproj')
            with ExitStack() as s2:
                vb = [sb(s2, "a_vb%d" % i, [128, 16, 512], BF16) for i in range(3)]
                Wv = sb(s2, "a_wv", [128, KT, 512], BF16)
                numacc = sb(s2, "a_num", [128, S], F32)
                denacc = sb(s2, "a_den", [128, S], F32)
                load_w(Wv[:], w_in[l, :, 1024:1536], 'a_wv')
                mstrip = sb(s2, "a_mstrip", [128, 4, 512], BF16)
                kb.dma('pool', mstrip[:].rearrange("p a b -> p (a b)"), mstrip_d, w=['mstrip'])
                DIL = (1, 4, 16)
                pi = 0
                for bi, dil in enumerate(DIL):
                    nb = (S // dil) // 128
                    for ti in range(16):
                        r_, j_ = divmod(ti, nb)
                        t0 = r_ + dil * 128 * j_
                        b = pi % 2
                        pi += 1
                        for k in range(KT):
                            kb.op('pe', lambda e: e.matmul(PS[b][:], lhsT=hnT[:, k, ssl(t0, 128, dil)], rhs=Wv[:, k, :],
                                                           start=(k == 0), stop=(k == KT - 1)),
                                  r=['a_wv'] + [('hnT', c) for c in range(NCH)], w=[('ps', b)])
                        kb.op('act', lambda e: e.activation(out=vb[bi][:, ti, :], in_=PS[b][:], func=AF.Copy), r=[('ps', b)], w=[('a_vb', bi)])
                qc = [sb(s2, "a_qc%d" % i, [128, S], BF16) for i in range(2)]
                kc = [sb(s2, "a_kc%d" % i, [128, S], BF16) for i in range(2)]
                pT3 = [sb(s2, "a_pTb%d" % i, [128, 512], BF16) for i in range(3)]
                state = {'it': 0, 'cc': 0}

                def emit_S(w_):
                    (hp, bi, dil, nb, r_, q0, qn, kts, qsrc, ksrc, qk_, kk_, sub0) = w_['a']
                    it = w_['it']
                    bset = it % 2
                    p_ = pT3[it % 3]
                    pk = ('a_pT', it % 3)
                    nkt = len(kts)
                    wdt = nkt * qn
                    for hh in range(2):
                        base = 64 * hh
                        bank = 2 * bset + hh
                        for i_, (kt, mk) in enumerate(kts):
                            slot = i_ * qn
                            if dil == 1:
                                lhs = ksrc[base:base + 64, hp, 128 * kt:128 * kt + 128]
                                rhs = qsrc[base:base + 64, hp, q0:q0 + qn]
                            else:
                                lhs = ksrc[base:base + 64, sub0 + 128 * kt:sub0 + 128 * kt + 128]
                                rhs = qsrc[base:base + 64, sub0 + q0:sub0 + q0 + qn]
                            kb.op('pe', lambda e: e.matmul(PS[bank][:, slot:slot + qn], lhsT=lhs, rhs=rhs, start=True, stop=True),
                                  r=[kk_, qk_], w=[('ps', bank)], inc=(hh == 1 and i_ == nkt - 1))
                    ncols = 2 * wdt
                    sidx = 3 if kts[0][1] == 'E' else (1 if kts[0][1] == 'F' else (0 if nkt == 2 else 2))
                    kb.op('act', lambda e: e.activation(out=p_[:, 0:ncols].rearrange("p (h w) -> p h w", h=2),
                                                        in_=psall[:, 2 * bset:2 * bset + 2, 0:wdt], func=AF.Exp, scale=0.125),
                          r=[('ps', 2 * bset), ('ps', 2 * bset + 1)], w=[pk])
                    kb.op('dve', lambda e: e.tensor_tensor(out=p_[:, 0:ncols], in0=p_[:, 0:ncols], in1=mstrip[:, sidx, 0:ncols], op=ALU.mult),
                          r=['mstrip'], w=[pk])

                def emit_PV(w_):
                    (hp, bi, dil, nb, r_, q0, qn, kts, qsrc, ksrc, qk_, kk_, sub0) = w_['a']
                    it = w_['it']
                    bnk = {0: 4 + (2 * it) % 3, 128: 4 + (2 * it + 1) % 3}
                    p_ = pT3[it % 3]
                    pk = ('a_pT', it % 3)
                    nkt = len(kts)
                    qsl = ssl(r_ + dil * q0, qn, dil)
                    for (c0, is_num) in ((0, True), (128, False)):
                        for hh in range(2):
                            base = 64 * hh
                            for i_, (kt, mk) in enumerate(kts):
                                slot = (hh * nkt + i_) * qn
                                hcol = (hp * 2 + hh) * 64
                                lhs = vb[bi][:, r_ * nb + kt, hcol:hcol + 64] if is_num else ones_bf[:, 0:64]
                                kb.op('pe', lambda e: e.matmul(PS[bnk[c0]][base:base + 64, 0:qn], lhsT=lhs, rhs=p_[:, slot:slot + qn],
                                                               start=(i_ == 0), stop=(i_ == nkt - 1), tile_position=(0, base)),
                                      r=[pk, ('a_vb', bi), 'cstb'], w=[('ps', bnk[c0])], inc=(hh == 1 and i_ == nkt - 1))
                    for (c0, acc, ak, eng) in ((0, numacc, 'a_num', 'act' if bi == 0 else 'dve'), (128, denacc, 'a_den', 'act' if bi == 0 else 'dve')):
                        if bi == 0:
                            if eng == 'act':
                                kb.op('act', lambda e: e.activation(out=acc[:, qsl], in_=PS[bnk[c0]][:, 0:qn], func=AF.Copy),
                                      r=[('ps', bnk[c0])], w=[(ak, 0)], loose=True)
                            else:
                                kb.op('dve', lambda e: e.tensor_copy(out=acc[:, qsl], in_=PS[bnk[c0]][:, 0:qn]),
                                      r=[('ps', bnk[c0])], w=[(ak, 0)], loose=True)
                        else:
                            kb.op('dve', lambda e: e.tensor_tensor(out=acc[:, qsl], in0=PS[bnk[c0]][:, 0:qn], in1=acc[:, qsl], op=ALU.add),
                                  r=[('ps', bnk[c0]), (ak, bi - 1)], w=[(ak, bi)], loose=True)

                for hp in range(4):
                    items = []
                    for bi, dil in enumerate(DIL):
                        nsub = S // dil
                        nb = nsub // 128
                        if dil == 1:
                            qsrc, ksrc, qk_, kk_ = qR, kR, ('a_qR', hp), ('a_kR', hp)
                        else:
                            cc = state['cc'] % 2
                            state['cc'] += 1
                            qsrc, ksrc, qk_, kk_ = qc[cc], kc[cc], ('a_qc', cc), ('a_kc', cc)
                            kb.op('pool', lambda e: e.tensor_copy(out=qsrc[:].rearrange("p (r i) -> p r i", r=dil),
                                                                  in_=qR[:, hp, :].rearrange("p (i r) -> p r i", r=dil)),
                                  r=[('a_qR', hp)], w=[qk_])
                            kb.op('pool', lambda e: e.tensor_copy(out=ksrc[:].rearrange("p (r i) -> p r i", r=dil),
                                                                  in_=kR[:, hp, :].rearrange("p (i r) -> p r i", r=dil)),
                                  r=[('a_kR', hp)], w=[kk_])
                        if nb == 1:
                            blocks = [(0, 128, [(0, 'E')])]
                        else:
                            blocks = [(0, 64, [(0, 'F')])]
                            blocks += [(64 + 128 * j, 128, [(j, 'A'), (j + 1, 'B')]) for j in range(nb - 1)]
                            blocks += [(nsub - 64, 64, [(nb - 1, 'A')])]
                        for r_ in range(dil):
                            for (q0, qn, kts) in blocks:
                                items.append({'a': (hp, bi, dil, nb, r_, q0, qn, kts, qsrc, ksrc, qk_, kk_, r_ * nsub), 'it': state['it']})
                                state['it'] += 1
                    prev = None
                    for w_ in items:
                        emit_S(w_)
                        if prev is not None:
                            emit_PV(prev)
                        prev = w_
                    emit_PV(prev)
                    nk = [('a_num', b_) for b_ in range(3)]
                    dk_ = [('a_den', b_) for b_ in range(3)]
                    kb.op('act', lambda e: e.activation(out=denacc[:], in_=denacc[:], func=AF.Ln), r=dk_, w=dk_)
                    kb.op('act', lambda e: e.activation(out=denacc[:], in_=denacc[:], func=AF.Exp, scale=-1.0), r=dk_, w=dk_)
                    kb.op('dve', lambda e: e.tensor_tensor(out=catT[:, hp, :], in0=numacc[:], in1=denacc[:], op=ALU.mult),
                          r=nk + dk_, w=[('catT', hp)] + nk)
            kb.barrier()
            chk('attn')
            st.close()

        def ffn_phase(st0, l, xsrc, xdst, hnT):
            st = st0.enter_context(ExitStack())
            aT = sb(st, "f_aT", [128, NFT, S], BF16)
            with ExitStack() as s1:
                Wgs = [sb(s1, "f_wg%d" % i, [128, KT, 512], BF16) for i in range(2)]
                Wvs = [sb(s1, "f_wv%d" % i, [128, KT, 512], BF16) for i in range(2)]
                yp = [sb(s1, "f_yp%d" % i, [128, S + 2], F32) for i in range(2)]
                u = [sb(s1, "f_u%d" % i, [128, S], F32) for i in range(2)]
                gl = sb(s1, "f_gl", [128, S], BF16)
                for i in range(2):
                    kb.op('dve', lambda e: e.memset(yp[i][:, 0:1], 0.0), w=[('f_yp', i)])
                    kb.op('dve', lambda e: e.memset(yp[i][:, S + 1:S + 2], 0.0), w=[('f_yp', i)])
                pi = 0
                for c in range(NFT):
                    GST = [0, 1, 4, 8, 12, 16, 20, NFT]
                    gidx = max(i for i in range(len(GST) - 1) if GST[i] <= c)
                    g4 = c - GST[gidx]

                    def ldgrp(gi_x):
                        c_ = GST[gi_x]
                        n = (GST[gi_x + 1] - c_) * 128
                        sl = gi_x % 2
                        load_w(Wgs[sl][:, :, 0:n], w_up[l, :, c_ * 128:c_ * 128 + n], ('f_wg', sl))
                        load_w(Wvs[sl][:, :, 0:n], w_up[l, :, DFF + c_ * 128:DFF + c_ * 128 + n], ('f_wv', sl))
                    if c == 0:
                        ldgrp(0)
                    if g4 == 0 and gidx + 1 < len(GST) - 1:
                        ldgrp(gidx + 1)
                    gi_ = gidx % 2
                    Wg, Wv = Wgs[gi_], Wvs[gi_]
                    for part, (W, wk) in enumerate(((Wg, ('f_wg', gi_)), (Wv, ('f_wv', gi_)))):
                        ti = part * NFT + c
                        for ch in range(NCH):
                            b = pi % 4
                            pi += 1
                            for k in range(KT):
                                kb.op('pe', lambda e: e.matmul(PS[b][:], lhsT=W[:, k, g4 * 128:(g4 + 1) * 128], rhs=hnT[:, k, ch * CH:(ch + 1) * CH],
                                                               start=(k == 0), stop=(k == KT - 1)), r=[wk, ('hnT', ch)], w=[('ps', b)])
                            kb.op('act', lambda e: e.activation(out=yp[part][:, 1 + ch * CH:1 + (ch + 1) * CH], in_=PS[b][:], func=AF.Copy),
                                  r=[('ps', b)], w=[('f_yp', part)])
                        kb.op('act', lambda e: e.activation(out=u[part][:], in_=yp[part][:, 0:S], func=AF.Identity,
                                                            scale=fconv[:, l, ti, 0:1], bias=fconv[:, l, ti, 3:4]),
                              r=[('f_yp', part), 'fconv'], w=[('f_u', part)])
                        for jj in (1, 2):
                            kb.op('dve', lambda e: e.scalar_tensor_tensor(out=u[part][:], in0=yp[part][:, jj:jj + S], scalar=fconv[:, l, ti, jj:jj + 1],
                                                                          in1=u[part][:], op0=ALU.mult, op1=ALU.add),
                                  r=[('f_yp', part), 'fconv'], w=[('f_u', part)])
                    kb.op('act', lambda e: e.activation(out=gl[:], in_=u[0][:], func=AF.Gelu_apprx_tanh), r=[('f_u', 0)], w=['f_gl'])
                    kb.op('dve', lambda e: e.tensor_tensor(out=aT[:, c, :], in0=gl[:], in1=u[1][:], op=ALU.mult),
                          r=['f_gl', ('f_u', 1)], w=[('f_aT', c)])
            kb.barrier()
            chk('fup')
            with ExitStack() as s2:
                last = (l == nlayers - 1)
                epilogue(s2, w_down[l], NFT, lambda ci, tsl: aT[:, ci, tsl], lambda ci, c: [('f_aT', ci)], xsrc, xdst, l, 3,
                         che=512 if last else 256, hn_out=None if last else hnT, nxt=None if last else (l + 1, 0),
                         xc_alias=hnT if last else None)
            kb.barrier()
            st.close()

        xcur = xT_in
        hnT = sb(top, "hnT", [128, KT, S], BF16)
        try:
          for l in range(nlayers if stop_at != 'init' else 0):
              xmid = xs[0]
              xnext = yT if l == nlayers - 1 else xs[1]
              with ExitStack() as sm_:
                  catT = sb(sm_, "catT", [128, KT, S], BF16)
                  with ExitStack() as sh_:
                      if l == 0:
                          with ExitStack() as s0:
                              norm_pre(s0, xcur, l, 0, hnT)
                      kb.barrier()
                      chk('norm')
                      mlstm_phase(sh_, l, hnT, catT)
                      attn_phase(sh_, l, hnT, catT)
                  if 'cat' in dbg_out and l == dbg.get('_layer', 0):
                      with ExitStack() as sd:
                          tmpf = sb(sd, "dbg_tmp", [128, S], F32)
                          for i in range(KT):
                              kb.op('act', lambda e: e.activation(out=tmpf[:], in_=catT[:, i, :], func=AF.Copy), r=[('catT', i)], w=['dbg_tmp'])
                              kb.dma('sp', dbg_out['cat'][i * 128:(i + 1) * 128, :], tmpf[:], r=['dbg_tmp'], w=['dbg_cat'])
                      kb.barrier()
                  with ExitStack() as se:
                      epilogue(se, w_out[l], KT, lambda ci, tsl: catT[:, ci, tsl], lambda ci, c: [('catT', ci)], xcur, xmid, l, 1,
                               che=CH, hn_out=hnT, nxt=(l, 2))
                  kb.barrier()
                  chk('ep1')
              with ExitStack() as sf:
                  ffn_phase(sf, l, xmid, xnext, hnT)
              xcur = xnext
        except _Stop:
            pass
        kb.barrier()
    stuck = kb.check_deadlock()
    if stuck:
        raise RuntimeError('static deadlock: %r' % (stuck,))
    return nc, list(dbg_out.keys())


def _host_prep(inputs):
    f = np.float32
    g = np.stack([inputs['mix_pre_g'], inputs['mix_post_g'], inputs['ffn_pre_g'], inputs['ffn_post_g']], axis=1)
    gvec = np.ascontiguousarray(g.reshape(2, 4, KT, 128).transpose(3, 0, 1, 2)).reshape(128, -1).astype(f)
    mc = np.concatenate([inputs['mlstm_conv_w'], inputs['mlstm_conv_b'][:, None, :]], axis=1)
    mconv = np.ascontiguousarray(mc.reshape(2, 6, 8, 128).transpose(3, 0, 2, 1)).reshape(128, -1).astype(f)
    fc = np.concatenate([inputs['ffn_conv_w'], inputs['ffn_conv_b'][:, None, :]], axis=1)
    fconv = np.ascontiguousarray(fc.reshape(2, 4, 44, 128).transpose(3, 0, 2, 1)).reshape(128, -1).astype(f)
    gateb = np.ascontiguousarray(np.broadcast_to(inputs['mlstm_gate_b'].reshape(1, -1), (128, 32))).astype(f)
    headg = np.ascontiguousarray(np.broadcast_to(inputs['mlstm_head_g'].reshape(1, -1), (128, 1024))).astype(f)
    p = np.arange(128)
    inv_freq = (10000.0 ** (-np.arange(0, 64, 2, dtype=np.float32) / 64)).astype(np.float32)
    ang = np.arange(S, dtype=np.float32)[None, :] * inv_freq[p % 32][:, None]
    cosT = np.cos(ang).astype(f)
    sgn = np.where((p % 64) < 32, -1.0, 1.0).astype(f)[:, None]
    sinT = (np.sin(ang) * sgn).astype(f)
    a = p[:, None]
    b = p[None, :]
    NEG = -30000.0
    cst = np.zeros((128, 9, 128), f)
    cst[:, 0] = 1.0
    cst[:, 1] = (a == b)
    cst[:, 2] = np.where(a >= b, 0.0, NEG)
    cst[:, 3] = np.where(a <= b, 0.0, NEG)
    cst[:, 4] = np.where(a <= b + 64, 0.0, NEG)
    cst[:, 5] = np.where(np.abs(a - b) <= 64, 0.0, NEG)
    cst[:, 6] = (a <= b)
    cst[:, 7] = (a >= b)
    partner = np.where((p % 64) < 32, p + 32, p - 32)
    cst[:, 8] = (a == partner[None, :])
    A01 = (a >= b).astype(f); B01 = (a <= b).astype(f); F01 = (a <= b + 64).astype(f); E01 = (np.abs(a - b) <= 64).astype(f)
    ms = np.zeros((128, 4, 512), f)
    ms[:, 0] = np.concatenate([A01, B01, A01, B01], axis=1)
    ms[:, 1, 0:128] = np.concatenate([F01[:, :64], F01[:, :64]], axis=1)
    ms[:, 2, 0:128] = np.concatenate([A01[:, :64], A01[:, :64]], axis=1)
    ms[:, 3, 0:256] = np.concatenate([E01, E01], axis=1)
    shared = dict(mstrip=ms.reshape(128, -1), w_in=np.ascontiguousarray(inputs['w_in'], dtype=f), w_out=np.ascontiguousarray(inputs['w_out'], dtype=f),
                  w_up=np.ascontiguousarray(inputs['w_up'], dtype=f), w_down=np.ascontiguousarray(inputs['w_down'], dtype=f),
                  gvec=gvec, mconv=mconv, fconv=fconv, gateb=gateb, headg=headg, cosT=cosT, sinT=sinT,
                  cst=cst.reshape(128, -1))
    return shared


_NC_CACHE = {}


def kernel(**inputs):
    x = np.asarray(inputs['x'], dtype=np.float32)
    B = x.shape[0]
    shared = _host_prep(inputs)
    if 'nc' not in _NC_CACHE:
        _NC_CACHE['nc'] = build(2)[0]
    nc = _NC_CACHE['nc']
    in_maps = []
    for b in range(B):
        m = dict(shared)
        m['xT'] = np.ascontiguousarray(x[b].T)
        in_maps.append(m)
    res = run_bass_kernel_spmd(nc, in_maps, core_ids=list(range(B)))
    out = np.stack([np.ascontiguousarray(res.results[b]['yT'].T) for b in range(B)], axis=0)
    return out.astype(np.float32)
```
